# Optimizing a Trainium2 kernel written in Bass

```python
import math
import jax, jax.numpy as jnp
from jax import lax
import numpy as np

D_MODEL = 1024
BATCH = 16
SEQ = 256
DEPTH = 1
DEC_BATCH = 4
DEC_SEQ = 1024
PAST_LEN = 512

GRID_W = 64
ATTN_HEADS = 8
ATTN_DH = 64
ATTN_VD = 2 * ATTN_DH
ROPE_BASE = 10000.0
ROPE_AXIS_DIM = ATTN_DH // 2
Q_BLOCK = 128
D_INNER = 2 * D_MODEL
SSM_HEADDIM = 64
SSM_HEADS = D_INNER // SSM_HEADDIM
SSM_GROUPS = 8
SSM_STATE = 128
D_CONV = 5
CHUNK = 128
CONV_DIM = D_INNER + 2 * SSM_GROUPS * SSM_STATE
D_FF = ((8 * D_MODEL // 3 + 255) // 256) * 256
W_Q = ATTN_HEADS * 2 * ATTN_DH
W_K = ATTN_HEADS * 2 * ATTN_DH
W_V = ATTN_HEADS * ATTN_VD
W_Z = D_INNER
W_XBC = CONV_DIM
W_DT = 2 * SSM_HEADS
W_GATES = 2 * D_MODEL
D_IN_TOTAL = W_Q + W_K + W_V + W_Z + W_XBC + W_DT + W_GATES
EPS = 1e-6

kernel_name = "hybrid_diffattn_ssd_prefix_context_step"


def rms_norm(x, g):
    xf = x.astype(jnp.float32)
    y = xf * lax.rsqrt(jnp.mean(xf * xf, axis=-1, keepdims=True) + EPS)
    return (y * g.astype(jnp.float32)).astype(x.dtype)


def axial_rope_tables(n_tok):
    rows = n_tok // GRID_W
    row = jnp.repeat(jnp.arange(rows, dtype=jnp.float32), GRID_W)
    col = jnp.tile(jnp.arange(GRID_W, dtype=jnp.float32), rows)
    inv = ROPE_BASE ** (-jnp.arange(0, ROPE_AXIS_DIM, 2, dtype=jnp.float32) / ROPE_AXIS_DIM)
    ang_r = row[:, None] * inv[None, :]
    ang_c = col[:, None] * inv[None, :]
    return jnp.cos(ang_r), jnp.sin(ang_r), jnp.cos(ang_c), jnp.sin(ang_c)


def _rotate(x, cos, sin):
    x1, x2 = jnp.split(x, 2, axis=-1)
    cos = cos[None, :, None, None, :]
    sin = sin[None, :, None, None, :]
    return jnp.concatenate([x1 * cos - x2 * sin, x2 * cos + x1 * sin], axis=-1)


def apply_axial_rope(x, rope):
    cos_r, sin_r, cos_c, sin_c = rope
    xf = x.astype(jnp.float32)
    xr = _rotate(xf[..., :ROPE_AXIS_DIM], cos_r, sin_r)
    xc = _rotate(xf[..., ROPE_AXIS_DIM:], cos_c, sin_c)
    return jnp.concatenate([xr, xc], axis=-1).astype(x.dtype)


def diff_attention(q, k, v, lam):
    b, lq, h, _, dh = q.shape
    nb = lq // Q_BLOCK
    qb = jnp.moveaxis(q.reshape(b, nb, Q_BLOCK, h, 2, dh), 1, 0)
    scale = dh ** -0.5

    def one_block(qblk):
        s = jnp.einsum("bqhcd,bkhcd->bhcqk", qblk, k).astype(jnp.float32) * scale
        p = jax.nn.softmax(s, axis=-1)
        w = (p[:, :, 0] - lam * p[:, :, 1]).astype(v.dtype)
        return jnp.einsum("bhqk,bkhe->bqhe", w, v)

    o = lax.map(one_block, qb)
    return jnp.moveaxis(o, 0, 1).reshape(b, lq, h, v.shape[-1])


def centred_dwconv(x, w, bias):
    y = lax.conv_general_dilated(
        x, w[:, None, :].astype(x.dtype), window_strides=(1,),
        padding=[(D_CONV // 2, D_CONV // 2)],
        dimension_numbers=("NWC", "WIO", "NWC"),
        feature_group_count=x.shape[-1])
    return y + bias


def _segsum(a):
    t = a.shape[-1]
    cs = jnp.cumsum(a, axis=-1)
    diff = cs[..., :, None] - cs[..., None, :]
    mask = jnp.tril(jnp.ones((t, t), dtype=bool))
    return jnp.where(mask, diff, -jnp.inf)


def ssd_scan(x, dt, A, Bm, Cm, h0):
    b, L, H, P = x.shape
    G, N = Bm.shape[2], Bm.shape[3]
    R = H // G
    nc = L // CHUNK
    x = x.astype(jnp.float32)
    xd = (x * dt[..., None]).reshape(b, nc, CHUNK, G, R, P)
    a = (dt * A).reshape(b, nc, CHUNK, G, R).transpose(0, 3, 4, 1, 2)
    Bc = Bm.astype(jnp.float32).reshape(b, nc, CHUNK, G, N)
    Cc = Cm.astype(jnp.float32).reshape(b, nc, CHUNK, G, N)
    a_cum = jnp.cumsum(a, axis=-1)
    Lmat = jnp.exp(_segsum(a))
    CB = jnp.einsum("bclgn,bcsgn->bgcls", Cc, Bc)
    y_diag = jnp.einsum("bgcls,bgrcls,bcsgrp->bclgrp", CB, Lmat, xd)
    decay_states = jnp.exp(a_cum[..., -1:] - a_cum)
    states = jnp.einsum("bclgn,bgrcl,bclgrp->bcgrpn", Bc, decay_states, xd)
    states = jnp.concatenate([h0.astype(jnp.float32).reshape(b, 1, G, R, P, N), states], axis=1)
    chunk_a = jnp.pad(a_cum[..., -1], ((0, 0), (0, 0), (0, 0), (1, 0)))
    decay_chunk = jnp.exp(_segsum(chunk_a))
    new_states = jnp.einsum("bgrzc,bcgrpn->bzgrpn", decay_chunk, states)
    prev_states, final = new_states[:, :-1], new_states[:, -1]
    y_off = jnp.einsum("bclgn,bcgrpn,bgrcl->bclgrp", Cc, prev_states, jnp.exp(a_cum))
    y = (y_diag + y_off).reshape(b, L, H, P)
    return y, final.reshape(b, H, P, N)


def ada_mod(cond, w_ada, b_ada):
    return (jax.nn.silu(cond) @ w_ada + b_ada)[:, None, :]


def trunk_layer(x, mod, rope, ctx_k, ctx_v, h0_f, h0_b, lam_init,
                norm1_g, norm2_g, w_in, q_norm_g, k_norm_g,
                lambda_q1, lambda_k1, lambda_q2, lambda_k2, attn_sub_g,
                conv_w, conv_b, A_log, dt_bias, D_skip, ssm_norm_g,
                w_branch_a, w_branch_b, w_out, w_ffn_gate, w_ffn_up, w_ffn_down):
    b, L, _ = x.shape
    shift1, scale1, gate1, shift2, scale2, gate2 = jnp.split(mod, 6, axis=-1)
    h = rms_norm(x, norm1_g) * (1 + scale1) + shift1
    splits = [W_Q, W_Q + W_K, W_Q + W_K + W_V, W_Q + W_K + W_V + W_Z,
              W_Q + W_K + W_V + W_Z + W_XBC, W_Q + W_K + W_V + W_Z + W_XBC + W_DT]
    q, k, v, z, xbc, dt_raw, gates = jnp.split(h @ w_in, splits, axis=-1)

    q = rms_norm(q.reshape(b, L, ATTN_HEADS, 2, ATTN_DH), q_norm_g)
    k = rms_norm(k.reshape(b, L, ATTN_HEADS, 2, ATTN_DH), k_norm_g)
    v = v.reshape(b, L, ATTN_HEADS, ATTN_VD)
    if rope is None:
        keys, vals = k, v
    else:
        q = apply_axial_rope(q, rope)
        keys = jnp.concatenate([apply_axial_rope(k, rope), ctx_k.astype(k.dtype)], axis=1)
        vals = jnp.concatenate([v, ctx_v.astype(v.dtype)], axis=1)
    lam = (jnp.exp(jnp.sum(lambda_q1.astype(jnp.float32) * lambda_k1.astype(jnp.float32)))
           - jnp.exp(jnp.sum(lambda_q2.astype(jnp.float32) * lambda_k2.astype(jnp.float32)))
           + lam_init)
    o = diff_attention(q, keys, vals, lam)
    o = rms_norm(o, attn_sub_g) * (1.0 - lam_init)
    out_a = o.reshape(b, L, ATTN_HEADS * ATTN_VD) @ w_branch_a

    xbc = jax.nn.silu(centred_dwconv(xbc, conv_w, conv_b))
    xs, Bs, Cs = jnp.split(xbc, [D_INNER, D_INNER + SSM_GROUPS * SSM_STATE], axis=-1)
    xs = xs.reshape(b, L, SSM_HEADS, SSM_HEADDIM)
    Bs = Bs.reshape(b, L, SSM_GROUPS, SSM_STATE)
    Cs = Cs.reshape(b, L, SSM_GROUPS, SSM_STATE)
    dt = jax.nn.softplus(dt_raw.astype(jnp.float32).reshape(b, L, 2, SSM_HEADS)
                         + dt_bias.astype(jnp.float32))
    A = -jnp.exp(A_log.astype(jnp.float32))
    if h0_f is None:
        h0_f = jnp.zeros((b, SSM_HEADS, SSM_HEADDIM, SSM_STATE), jnp.float32)
        h0_b = jnp.zeros((b, SSM_HEADS, SSM_HEADDIM, SSM_STATE), jnp.float32)
    y_f, hf = ssd_scan(xs, dt[:, :, 0], A[0], Bs, Cs, h0_f)
    y_b, hb = ssd_scan(jnp.flip(xs, 1), jnp.flip(dt[:, :, 1], 1), A[1],
                       jnp.flip(Bs, 1), jnp.flip(Cs, 1), h0_b)
    y = y_f + jnp.flip(y_b, 1) + D_skip.astype(jnp.float32)[:, None] * xs.astype(jnp.float32)
    y = rms_norm(y.reshape(b, L, D_INNER).astype(x.dtype) * jax.nn.silu(z), ssm_norm_g)
    out_b = y @ w_branch_b

    g_a, g_b = jnp.split(gates, 2, axis=-1)
    merged = jax.nn.sigmoid(g_a) * out_a + jax.nn.sigmoid(g_b) * out_b
    x = x + gate1 * (merged @ w_out)

    h2 = rms_norm(x, norm2_g) * (1 + scale2) + shift2
    x = x + gate2 * ((jax.nn.silu(h2 @ w_ffn_gate) * (h2 @ w_ffn_up)) @ w_ffn_down)
    return x, k, v, hf.astype(x.dtype), hb.astype(x.dtype)


def setup_inputs(seed: int = 0) -> dict:
    key = jax.random.key(seed)
    ks = jax.random.split(key, 40)
    f32 = jnp.float32

    def nrm(i, shape, scale):
        return jax.random.normal(ks[i], shape, f32) * scale

    dt0 = jnp.exp(jax.random.uniform(ks[30], (DEPTH, 2, SSM_HEADS), f32,
                                     math.log(1e-3), math.log(1e-1)))
    return {
        "x_prompt": nrm(0, (BATCH, SEQ, D_MODEL), 1.0),
        "x_sample": nrm(1, (DEC_BATCH, DEC_SEQ, D_MODEL), 1.0),
        "c": nrm(2, (DEC_BATCH, D_MODEL), 1.0),
        "cache_k": nrm(3, (DEC_BATCH, DEPTH, PAST_LEN, ATTN_HEADS, 2, ATTN_DH), 1.0),
        "cache_v": nrm(4, (DEC_BATCH, DEPTH, PAST_LEN, ATTN_HEADS, ATTN_VD), 1.0),
        "state_ssm_fwd": nrm(5, (DEC_BATCH, DEPTH, SSM_HEADS, SSM_HEADDIM, SSM_STATE), 0.1),
        "state_ssm_bwd": nrm(6, (DEC_BATCH, DEPTH, SSM_HEADS, SSM_HEADDIM, SSM_STATE), 0.1),
        "c_ctx": nrm(7, (D_MODEL,), 1.0),
        "norm1_g": 1.0 + nrm(8, (DEPTH, D_MODEL), 0.05),
        "norm2_g": 1.0 + nrm(9, (DEPTH, D_MODEL), 0.05),
        "w_ada": nrm(10, (DEPTH, D_MODEL, 6 * D_MODEL), 0.5 * D_MODEL ** -0.5),
        "b_ada": nrm(11, (DEPTH, 6 * D_MODEL), 0.02),
        "w_in": nrm(12, (DEPTH, D_MODEL, D_IN_TOTAL), D_MODEL ** -0.5),
        "q_norm_g": 1.0 + nrm(13, (DEPTH, ATTN_DH), 0.05),
        "k_norm_g": 1.0 + nrm(14, (DEPTH, ATTN_DH), 0.05),
        "lambda_q1": nrm(15, (DEPTH, ATTN_DH), 0.1),
        "lambda_k1": nrm(16, (DEPTH, ATTN_DH), 0.1),
        "lambda_q2": nrm(17, (DEPTH, ATTN_DH), 0.1),
        "lambda_k2": nrm(18, (DEPTH, ATTN_DH), 0.1),
        "attn_sub_g": 1.0 + nrm(19, (DEPTH, ATTN_VD), 0.05),
        "conv_w": nrm(20, (DEPTH, D_CONV, CONV_DIM), D_CONV ** -0.5),
        "conv_b": nrm(21, (DEPTH, CONV_DIM), 0.02),
        "A_log": jnp.log(jax.random.uniform(ks[22], (DEPTH, 2, SSM_HEADS), f32, 1.0, 16.0)),
        "dt_bias": dt0 + jnp.log(-jnp.expm1(-dt0)),
        "D_skip": 1.0 + nrm(23, (DEPTH, SSM_HEADS), 0.05),
        "ssm_norm_g": 1.0 + nrm(24, (DEPTH, D_INNER), 0.05),
        "w_branch_a": nrm(25, (DEPTH, ATTN_HEADS * ATTN_VD, D_MODEL), (ATTN_HEADS * ATTN_VD) ** -0.5),
        "w_branch_b": nrm(26, (DEPTH, D_INNER, D_MODEL), D_INNER ** -0.5),
        "w_out": nrm(27, (DEPTH, D_MODEL, D_MODEL), D_MODEL ** -0.5),
        "w_ffn_gate": nrm(28, (DEPTH, D_MODEL, D_FF), D_MODEL ** -0.5),
        "w_ffn_up": nrm(29, (DEPTH, D_MODEL, D_FF), D_MODEL ** -0.5),
        "w_ffn_down": nrm(31, (DEPTH, D_FF, D_MODEL), D_FF ** -0.5),
    }


def reference(x_prompt, x_sample, c, cache_k, cache_v, state_ssm_fwd, state_ssm_bwd, c_ctx,
              norm1_g, norm2_g, w_ada, b_ada, w_in, q_norm_g, k_norm_g,
              lambda_q1, lambda_k1, lambda_q2, lambda_k2, attn_sub_g,
              conv_w, conv_b, A_log, dt_bias, D_skip, ssm_norm_g,
              w_branch_a, w_branch_b, w_out, w_ffn_gate, w_ffn_up, w_ffn_down):
    rope = axial_rope_tables(x_sample.shape[1])
    y_prompt = x_prompt
    y_sample = x_sample
    ks_, vs_, hfs, hbs = [], [], [], []
    for l in range(DEPTH):
        lam_init = 0.8 - 0.6 * math.exp(-0.3 * l)
        lw = dict(norm1_g=norm1_g[l], norm2_g=norm2_g[l], w_in=w_in[l],
                  q_norm_g=q_norm_g[l], k_norm_g=k_norm_g[l],
                  lambda_q1=lambda_q1[l], lambda_k1=lambda_k1[l],
                  lambda_q2=lambda_q2[l], lambda_k2=lambda_k2[l], attn_sub_g=attn_sub_g[l],
                  conv_w=conv_w[l], conv_b=conv_b[l], A_log=A_log[l], dt_bias=dt_bias[l],
                  D_skip=D_skip[l], ssm_norm_g=ssm_norm_g[l],
                  w_branch_a=w_branch_a[l], w_branch_b=w_branch_b[l], w_out=w_out[l],
                  w_ffn_gate=w_ffn_gate[l], w_ffn_up=w_ffn_up[l], w_ffn_down=w_ffn_down[l])
        mod_ctx = ada_mod(c_ctx[None, :], w_ada[l], b_ada[l])
        y_prompt, k_ctx, v_ctx, hf, hb = trunk_layer(
            y_prompt, mod_ctx, None, None, None, None, None, lam_init, **lw)
        ks_.append(k_ctx)
        vs_.append(v_ctx)
        hfs.append(hf)
        hbs.append(hb)
        mod_lat = ada_mod(c, w_ada[l], b_ada[l])
        y_sample, _, _, _, _ = trunk_layer(
            y_sample, mod_lat, rope, cache_k[:, l], cache_v[:, l],
            state_ssm_fwd[:, l], state_ssm_bwd[:, l], lam_init, **lw)
    new_cache_k = jnp.stack(ks_, axis=1)
    new_cache_v = jnp.stack(vs_, axis=1)
    new_state_ssm_fwd = jnp.stack(hfs, axis=1)
    new_state_ssm_bwd = jnp.stack(hbs, axis=1)
    return (y_prompt, y_sample, new_cache_k, new_cache_v, new_state_ssm_fwd, new_state_ssm_bwd)
```

```python
import math
import contextlib
import numpy as np
import concourse.bass as bass
import concourse.mybir as mybir
from concourse.bass_utils import run_bass_kernel_spmd

F32 = mybir.dt.float32
BF16 = mybir.dt.bfloat16
AF = mybir.ActivationFunctionType
ALU = mybir.AluOpType
AX = mybir.AxisListType

D = 1024
NH = 8
DFF = 2816
DIN = 11328
EPS = 1e-6
LAM_INIT = 0.8 - 0.6 * math.exp(-0.3 * 0)
C_Q, C_K, C_V, C_Z, C_X, C_B, C_C, C_DT, C_GA, C_GB = 0, 1024, 2048, 3072, 5120, 7168, 8192, 9216, 9280, 10304

CO_ID, CO_U, CO_LW, CO_SL, CO_SU, CO_ONE, CO_COS, CO_SIN, CO_SEL = 0, 128, 256, 384, 512, 640, 768, 1024, 1280
NCON = 1280
BO_QG, BO_KG, BO_SUB, BO_DSK, BO_DTBP, BO_DTBS, BO_ALP, BO_ALS, BO_L = 0, 64, 128, 256, 288, 352, 416, 480, 544
NBV = 544 + 256
PO_G1, PO_G2, PO_CWP, PO_CWS, PO_CB, PO_SSG, PO_SUBG = 0, 8, 16, 176, 336, 368, 384
NPV = 385


class Prog:
    def __init__(self, nc):
        self.nc = nc
        self.ins = []
        self.last_w = {}
        self.readers = {}

    enabled = True
    tag = ""
    barrier_id = None
    barrier_from = 0

    def barrier(self, fn):
        if not self.enabled:
            return
        deps = {}
        last = {}
        for i in range(self.barrier_from, len(self.ins)):
            I = self.ins[i]
            if I["dma"]:
                deps[i] = 2
            else:
                last[I["eng"]] = i
        for i in last.values():
            deps[i] = 2
        iid = self.op("dve", fn)
        self.ins[iid]["deps"].update(deps)
        self.barrier_id = iid
        self.barrier_from = iid

    def op(self, eng, fn, reads=(), writes=(), dma=False):
        if not self.enabled:
            return None
        iid = len(self.ins)
        deps = {}
        if self.barrier_id is not None:
            deps[self.barrier_id] = 2
        for b in reads:
            w = self.last_w.get(b)
            if w is not None:
                deps[w] = 2
            if b.startswith("ps") and eng != "pe":
                for r in self.readers.get(b, ()):
                    if self.ins[r]["eng"] != eng:
                        deps[r] = max(deps.get(r, 0), 1)
        for b in writes:
            w = self.last_w.get(b)
            if w is not None:
                deps[w] = max(deps.get(w, 0), 1)
            for r in self.readers.get(b, ()):
                deps.setdefault(r, 0)
        self.ins.append(dict(eng=eng, fn=fn, deps=deps, dma=dma, tag=self.tag))
        for b in writes:
            self.last_w[b] = iid
            self.readers[b] = []
        for b in reads:
            if b not in writes:
                lst = self.readers.setdefault(b, [])
                if not dma:
                    lst[:] = [r for r in lst if self.ins[r]["dma"] or self.ins[r]["eng"] != eng]
                lst.append(iid)
        return iid

    def _need(self, I, Dd, true_dep):
        if I["dma"] or Dd["dma"]:
            return True
        if I["eng"] != Dd["eng"]:
            return True
        if I["eng"] == "pe":
            return False
        return True

    def emit(self, final_wait_ids=()):
        nc = self.nc
        ins = self.ins
        NDMA = 8
        dma_rr = {}
        prev_on_sem = {}
        dma_key = {}
        for i, I in enumerate(ins):
            if I["dma"]:
                k = dma_rr.get(I["eng"], 0)
                dma_rr[I["eng"]] = k + 1
                key = ("dma", I["eng"], k % NDMA)
                dma_key[i] = key
                if key in prev_on_sem:
                    I["deps"][prev_on_sem[key]] = 2
                prev_on_sem[key] = i
        needed = set(final_wait_ids)
        for i, I in enumerate(ins):
            for d, td in I["deps"].items():
                if self._need(I, ins[d], td):
                    needed.add(d)
        cnt = {}
        sig = {}
        for i, I in enumerate(ins):
            if I["dma"]:
                key = dma_key[i]
                cnt[key] = cnt.get(key, 0) + 16
                sig[i] = (key, cnt[key])
            elif i in needed:
                key = ("c", I["eng"])
                cnt[key] = cnt.get(key, 0) + 1
                sig[i] = (key, cnt[key])
        keys = sorted(set(k for k, _ in sig.values()), key=str)
        self.stats = dict(n_ins=len(ins), n_sig=len(sig), cnt={str(k): v for k, v in cnt.items()})
        with contextlib.ExitStack() as es:
            sems = {k: es.enter_context(nc.semaphore("s_" + "_".join(map(str, k)))) for k in keys}
            block = es.enter_context(nc.Block())
            per_eng = {}
            for i, I in enumerate(ins):
                per_eng.setdefault(I["eng"], []).append(i)
            nwaits = [0]

            def run_engine(ename, e):
                known = {}
                for i in per_eng.get(ename, []):
                    I = ins[i]
                    for d in sorted(I["deps"]):
                        if not self._need(I, ins[d], I["deps"][d]):
                            continue
                        key, val = sig[d]
                        if known.get(key, 0) >= val:
                            continue
                        e.wait_ge(sems[key], val)
                        nwaits[0] += 1
                        known[key] = val
                    r = I["fn"](e)
                    if i in sig:
                        key, val = sig[i]
                        r.then_inc(sems[key], 16 if I["dma"] else 1)
                if ename == "sp":
                    for d in final_wait_ids:
                        key, val = sig[d]
                        if known.get(key, 0) >= val:
                            continue
                        e.wait_ge(sems[key], val)
                        known[key] = val

            @block.sync
            def _(e):
                run_engine("sp", e)

            @block.gpsimd
            def _(e):
                run_engine("pool", e)

            @block.tensor
            def _(e):
                run_engine("pe", e)

            @block.vector
            def _(e):
                run_engine("dve", e)

            @block.scalar
            def _(e):
                run_engine("act", e)
            self.stats["n_waits"] = nwaits[0]


def build_program(debug=None, stop=None):
    nc = bass.Bass("TRN2", target_bir_lowering=False)
    debug = debug or {}

    def din(name, shape):
        return nc.dram_tensor(name, list(shape), F32, kind="ExternalInput").ap()

    def dout(name, shape):
        return nc.dram_tensor(name, list(shape), F32, kind="ExternalOutput").ap()

    x_all = din("x_all", [1536, D])
    condT = din("condT", [128, 16])
    ckT = din("ckT", [128, NH, 512])
    cv = din("cv", [512, NH, 128])
    h0T = din("h0T", [128, 2, 2048])
    w_ada = din("w_ada", [D, 6 * D])
    b_ada2 = din("b_ada2", [2, 6 * D])
    w_in = din("w_in", [D, DIN])
    w_dt = din("w_dt", [D, 128])
    w_a = din("w_a", [D, D])
    w_b = din("w_b", [2 * D, D])
    w_o = din("w_o", [D, D])
    w_g = din("w_g", [D, DFF])
    w_u = din("w_u", [D, DFF])
    w_d = din("w_d", [DFF, D])
    pvec_d = din("pvec", [128, NPV])
    bvec_d = din("bvec", [128, NBV])
    ssmg_d = din("ssmg", [128, 2048])
    consts_d = din("consts", [128, NCON])

    y_p = dout("y_p", [512, D])
    y_s = dout("y_s", [512, D])
    nk = dout("nk", [512, D])
    nv = dout("nv", [512, D])
    sf = dout("sf", [2, 2048, 128])
    sb = dout("sb", [2, 2048, 128])
    BF_DBG = ("hT", "KT", "QT", "Vaug", "OT", "xtok", "BTt", "CTt", "Btok", "YT", "mT", "actT")
    dbg_out = {k: nc.dram_tensor("dbg_" + k, list(shp), BF16 if k in BF_DBG else F32, kind="ExternalOutput").ap()
               for k, shp in debug.items()}

    P = Prog(nc)
    finals = []
    uid = [0]

    def U_():
        uid[0] += 1
        return uid[0]

    def sb_t(es, name, shape, dt=F32):
        return es.enter_context(nc.sbuf_tensor("sb_" + name, list(shape), dt))

    def mm(out, lhsT, rhs, start, stop, r, w):
        return P.op("pe", lambda e: e.matmul(out, lhsT=lhsT, rhs=rhs, start=start, stop=stop), reads=r, writes=w)

    def tr(out, in_, ident, r, w):
        return P.op("pe", lambda e: e.transpose(out=out, in_=in_, identity=ident), reads=r, writes=w)

    def act(out, in_, func, r, w, scale=1.0, bias=None, accum=None):
        kw = dict(scale=scale)
        if bias is not None:
            kw["bias"] = bias
        if accum is not None:
            kw["accum_out"] = accum
        return P.op("act", lambda e: e.activation(out=out, in_=in_, func=func, **kw), reads=r, writes=w)

    def tt(out, in0, in1, op, r, w, eng="dve"):
        return P.op(eng, lambda e: e.tensor_tensor(out=out, in0=in0, in1=in1, op=op), reads=r, writes=w)

    def ts(out, in0, s1, s2, op0, op1, r, w, eng="dve"):
        if op1 is None:
            return P.op(eng, lambda e: e.tensor_scalar(out=out, in0=in0, scalar1=s1, scalar2=None, op0=op0),
                        reads=r, writes=w)
        return P.op(eng, lambda e: e.tensor_scalar(out=out, in0=in0, scalar1=s1, scalar2=s2, op0=op0, op1=op1),
                    reads=r, writes=w)

    def stt(out, in0, scalar, in1, op0, op1, r, w):
        return P.op("dve", lambda e: e.scalar_tensor_tensor(out=out, in0=in0, scalar=scalar, in1=in1, op0=op0, op1=op1),
                    reads=r, writes=w)

    def cp(out, in_, r, w, eng="dve"):
        if eng == "act":
            return P.op("act", lambda e: e.copy(out=out, in_=in_), reads=r, writes=w)
        return P.op(eng, lambda e: e.tensor_copy(out=out, in_=in_), reads=r, writes=w)

    def dma(out, in_, r, w, q="sp"):
        return P.op(q, lambda e: e.dma_start(out=out, in_=in_), reads=r, writes=w, dma=True)

    def memset(ap, val, w, eng="dve"):
        return P.op(eng, lambda e: e.memset(ap, val), writes=w)

    def dbg(name, ap_sb, r):
        if name in dbg_out and P.enabled:
            finals.append(dma(dbg_out[name], ap_sb, r, []))

    def phase_end(name):
        if stop == name:
            P.enabled = False

    def rstd_from_ss(ss_ap, n, out_ap, tmp_ap, key_in, key_tmp, key_out, inv_n):
        ts(tmp_ap, ss_ap, inv_n, EPS, ALU.mult, ALU.add, [key_in], [key_tmp])
        act(tmp_ap, tmp_ap, AF.Ln, [key_tmp], [key_tmp])
        act(out_ap, tmp_ap, AF.Exp, [key_tmp], [key_out], scale=-0.5)

    with contextlib.ExitStack() as L0:
        psA = L0.enter_context(nc.psum_tensor("psA", [128, 2048], F32))
        psB = L0.enter_context(nc.psum_tensor("psB", [128, 2048], F32))

        def bank(i):
            t = psA if i < 4 else psB
            j = i % 4
            return t[:, j * 512:(j + 1) * 512]

        PK = ["ps%d" % i for i in range(8)]

        consts = sb_t(L0, "consts", [128, NCON])
        pvec = sb_t(L0, "pvec", [128, NPV])
        bvec = sb_t(L0, "bvec", [128, NBV])
        identb = sb_t(L0, "identb", [128, 128], BF16)
        AB = sb_t(L0, "AB", [128, 6, 8, 2])
        lamt = sb_t(L0, "lamt", [128, 4])
        hT = sb_t(L0, "hT", [128, 8, 1536], BF16)
        bar_t = sb_t(L0, "bar_t", [128, 2])

        def barrier():
            P.barrier(lambda e: e.memset(bar_t[:], 0.0))

        dma(consts[:], consts_d, [], ["consts"])
        dma(pvec[:], pvec_d, [], ["pvec"])
        dma(bvec[:], bvec_d, [], ["bvec"])
        ident = consts[:, CO_ID:CO_ID + 128]
        mU = consts[:, CO_U:CO_U + 128]
        mLW = consts[:, CO_LW:CO_LW + 128]
        mSL = consts[:, CO_SL:CO_SL + 128]
        mSU = consts[:, CO_SU:CO_SU + 128]
        ones = consts[:, CO_ONE:CO_ONE + 128]
        cp(identb[:], ident, ["consts"], ["identb"])

        hkeys = ["hT.%d" % t for t in range(12)]

        def norm_mod_to_hT(src_tile_fn, ntiles, rsel, Aidx, dst, dst_keys, es, tagp, hook=None):
            xn = [sb_t(es, "%s_xn%d" % (tagp, i), [128, D]) for i in range(2)]
            junk = sb_t(es, tagp + "_junk", [128, D])
            st = [sb_t(es, "%s_st%d" % (tagp, i), [128, 4]) for i in range(2)]
            for t in range(ntiles):
                if hook is not None:
                    hook(t)
                xt, xkey = src_tile_fn(t)
                b = t % 2
                sk = "%s_st%d" % (tagp, b)
                act(junk[:], xt, AF.Square, [xkey], [tagp + "_junk", sk + "a"], accum=st[b][:, 0:1])
                rstd_from_ss(st[b][:, 0:1], 1, st[b][:, 2:3], st[b][:, 1:2], sk + "a", sk + "b", sk + "c", 1.0 / D)
                xk = "%s_xn%d" % (tagp, b)
                ts(xn[b][:], xt, st[b][:, 2:3], None, ALU.mult, None, [xkey, sk + "c"], [xk])
                r = rsel(t)
                for half in range(2):
                    pb = 6 + half
                    for q in range(4):
                        kt = half * 4 + q
                        tr(bank(pb)[:, q * 128:(q + 1) * 128], xn[b][:, kt * 128:(kt + 1) * 128], ident,
                           [xk, "consts"], [PK[pb]])
                    for q in range(4):
                        kt = half * 4 + q
                        act(dst[:, kt, t * 128:(t + 1) * 128], bank(pb)[:, q * 128:(q + 1) * 128], AF.Identity,
                            [PK[pb], "AB"], [dst_keys[t]],
                            scale=AB[:, Aidx, kt, r:r + 1], bias=AB[:, Aidx + 1, kt, r:r + 1])

        with contextlib.ExitStack() as sA:
            mod = sb_t(sA, "mod", [2, 6 * D])
            bada = sb_t(sA, "bada", [2, 6 * D])
            cT = sb_t(sA, "cT", [128, 16])
            scT = sb_t(sA, "scT", [128, 16], BF16)
            wada = [sb_t(sA, "wada%d" % i, [128, 8, 512], BF16) for i in range(3)]
            ltmp = sb_t(sA, "ltmp", [128, 64])
            dma(cT[:], condT, [], ["cT"])
            dma(bada[:], b_ada2, [], ["bada"])
            act(scT[:], cT[:], AF.Silu, ["cT"], ["scT"])
            w_ada_v = w_ada.rearrange("(kt p) c -> p kt c", p=128)
            def ada_cb(cb):
                wb = cb % 3
                dma(wada[wb][:], w_ada_v[:, :, cb * 512:(cb + 1) * 512], [], ["wada%d" % wb], q="pool")
                pb = cb % 2
                for kt in range(8):
                    mm(bank(pb)[0:2, :], scT[:, kt * 2:kt * 2 + 2], wada[wb][:, kt, :], kt == 0, kt == 7,
                       ["scT", "wada%d" % wb], [PK[pb]])
                tt(mod[:, cb * 512:(cb + 1) * 512], bank(pb)[0:2, :], bada[:, cb * 512:(cb + 1) * 512], ALU.add,
                   [PK[pb], "bada"], ["mod.%d" % (cb // 2)])

            mT4 = bank(2)[:, 0:96].rearrange("p (s k r) -> p s k r", s=6, k=8)

            def ada_sections(sis):
                secs = (0, 1, 3, 4, 2, 5)
                for si in sis:
                    sec = secs[si]
                    for kt in range(8):
                        c0 = (si * 8 + kt) * 2
                        tr(bank(2)[:, c0:c0 + 2], mod[0:2, sec * D + kt * 128: sec * D + (kt + 1) * 128], ident[0:2, 0:2],
                           ["mod.%d" % sec, "consts"], [PK[2]])

            def ada_AB(j):
                po_g, s_shift, s_scale = ((PO_G1, 0, 1), (PO_G2, 2, 3))[j]
                gv = pvec[:, po_g:po_g + 8].unsqueeze(2).to_broadcast([128, 8, 2])
                ts(AB[:, 2 * j], mT4[:, s_scale], 1.0, None, ALU.add, None, [PK[2]], ["AB"])
                tt(AB[:, 2 * j], AB[:, 2 * j], gv, ALU.mult, ["AB", "pvec"], ["AB"])
                cp(AB[:, 2 * j + 1], mT4[:, s_shift], [PK[2]], ["AB"])

            for cb in range(4):
                ada_cb(cb)
            ada_sections([0, 1])
            ada_AB(0)
            phase_end("A2")
            for j in range(2):
                tt(ltmp[:], bvec[:, BO_L + 128 * j:BO_L + 128 * j + 64], bvec[:, BO_L + 128 * j + 64:BO_L + 128 * j + 128],
                   ALU.mult, ["bvec"], ["ltmp"])
                P.op("dve", lambda e, j=j: e.tensor_reduce(out=lamt[:, 2 + j:3 + j], in_=ltmp[:], axis=AX.X, op=ALU.add),
                     reads=["ltmp"], writes=["lamt"])
            act(lamt[:, 2:4], lamt[:, 2:4], AF.Exp, ["lamt"], ["lamt"])
            tt(lamt[:, 0:1], lamt[:, 2:3], lamt[:, 3:4], ALU.subtract, ["lamt"], ["lamt"])
            ts(lamt[:, 0:1], lamt[:, 0:1], LAM_INIT, None, ALU.add, None, ["lamt"], ["lamt"])
            ts(lamt[:, 1:2], lamt[:, 0:1], -1.0, None, ALU.mult, None, ["lamt"], ["lamt"])
            phase_end("A")

            xts = [sb_t(sA, "xt%d" % i, [128, D]) for i in range(3)]

            def src_tile(t):
                b = t % 3
                dma(xts[b][:], x_all[t * 128:(t + 1) * 128, :], [], ["xt%d" % b])
                return xts[b][:], "xt%d" % b

            def ada_hook(t):
                if t < 8:
                    ada_cb(4 + t)

            norm_mod_to_hT(src_tile, 12, lambda t: 0 if t < 4 else 1, 0, hT, hkeys, sA, "nB", hook=ada_hook)
            ada_sections([2, 3, 4, 5])
            ada_AB(1)
            cp(AB[:, 4], mT4[:, 4], [PK[2]], ["AB"])
            cp(AB[:, 5], mT4[:, 5], [PK[2]], ["AB"])
        dbg("hT", hT[:], hkeys)
        phase_end("B")

        barrier()
        with contextlib.ExitStack() as L1:
            OT = sb_t(L1, "OT", [128, NH, 1024], BF16)
            with contextlib.ExitStack() as sC:
                Vaug = sb_t(sC, "Vaug", [128, 16, NH, 130], BF16)
                KT = sb_t(sC, "KT", [128, NH, 2048], BF16)
                QT = sb_t(sC, "QT", [128, NH, 1024], BF16)
                vkeys = ["V.%d" % t for t in range(16)]
                kkeys = ["KT.%d" % t for t in range(16)]
                qkeys = ["QT.%d" % t for t in range(8)]
                import os
                KD = os.environ.get("KDBG", "")
                if "nomemset" not in KD:
                    memset(Vaug[:, :, :, 128:129], 1.0, vkeys)
                if "nock" not in KD:
                    dma(KT[:, :, 1536:2048], ckT, [], kkeys[12:16], q="pool")
                if "nocv" not in KD:
                    for t in range(4):
                        dma(Vaug[:, 12 + t, :, 0:128], cv[t * 128:(t + 1) * 128, :, :], [], [vkeys[12 + t]], q="pool")
                w_in_v = w_in.rearrange("(kt p) c -> p kt c", p=128)
                with contextlib.ExitStack() as sW:
                    wq = [sb_t(sW, "wqkv%d" % i, [128, 8, 512], BF16) for i in range(3)]
                    wcnt = [0]

                    def wnext(c0):
                        b = wcnt[0] % 3
                        wcnt[0] += 1
                        dma(wq[b][:], w_in_v[:, :, c0:c0 + 512], [], ["wqkv%d" % b], q="pool")
                        return wq[b], "wqkv%d" % b

                    vst = [sb_t(sW, "vst%d" % i, [128, 512]) for i in range(2)]
                    for cb in range(2):
                        wt, wk_ = wnext(C_V + cb * 512)
                        for t in range(12):
                            pb = t % 2
                            for kt in range(8):
                                mm(bank(pb), hT[:, kt, t * 128:(t + 1) * 128], wt[:, kt, :], kt == 0, kt == 7,
                                   [hkeys[t], wk_], [PK[pb]])
                            if "noact" not in KD:
                                act(Vaug[:, t, 4 * cb:4 * cb + 4, 0:128],
                                    bank(pb).rearrange("p (h e) -> p h e", h=4), AF.Identity, [PK[pb]], [vkeys[t]])
                            if t < 4 and "nonv" not in KD:
                                vb = (cb * 4 + t) % 2
                                cp(vst[vb][:], bank(pb), [PK[pb], vkeys[t]], ["vst%d" % vb])
                                finals.append(dma(nv[t * 128:(t + 1) * 128, cb * 512:(cb + 1) * 512], vst[vb][:],
                                                  ["vst%d" % vb], []))
                    phase_end("C1")
                    sq = sb_t(sW, "sq", [128, D])
                    kn = [sb_t(sW, "kn%d" % i, [128, D]) for i in range(2)]
                    kr = [sb_t(sW, "kr%d" % i, [128, D]) for i in range(2)]
                    rt = sb_t(sW, "rt", [128, 2, 512])
                    st16 = sb_t(sW, "st16", [128, 2, 16])

                    sq2 = [sq, sb_t(sW, "sq_b", [128, D])]
                    st16b = sb_t(sW, "st16_b", [128, 2, 16])
                    st2 = [st16, st16b]

                    def qk_section(col0, tiles, gofs, dstT, dkeys, colfn, is_k):
                        wts = [wnext(col0), wnext(col0 + 512)]
                        tiles = list(tiles)

                        def emit_proj(i):
                            t = tiles[i]
                            for cb in range(2):
                                pb = 2 + 2 * (i % 2) + cb
                                for kt in range(8):
                                    mm(bank(pb), hT[:, kt, t * 128:(t + 1) * 128], wts[cb][0][:, kt, :], kt == 0, kt == 7,
                                       [hkeys[t], wts[cb][1]], [PK[pb]])

                        def emit_rest(i):
                            t = tiles[i]
                            par = i % 2
                            psq = psA[:, 1024:2048] if par == 0 else psB[:, 0:1024]
                            pk2 = [PK[2 + 2 * par], PK[3 + 2 * par]]
                            b = i % 2
                            sqb, sqk = sq2[b], "sq%d" % b
                            stb, sk = st2[b], "st16_%d" % b
                            act(sqb[:], psq, AF.Square, pk2, [sqk])
                            P.op("dve", lambda e: e.tensor_reduce(out=stb[:, 0, :], in_=sqb[:].rearrange("p (g d) -> p g d", d=64),
                                                                   axis=AX.X, op=ALU.add), reads=[sqk], writes=[sk + "a"])
                            rstd_from_ss(stb[:, 0, :], 16, stb[:, 1, :], stb[:, 0, :], sk + "a", sk + "a", sk + "b", 1.0 / 64)
                            knk = "kn%d" % b
                            tt(kn[b][:].rearrange("p (g d) -> p g d", d=64), psq.rearrange("p (g d) -> p g d", d=64),
                               stb[:, 1, :].unsqueeze(2).to_broadcast([128, 16, 64]), ALU.mult, pk2 + [sk + "b"], [knk])
                            tt(kn[b][:].rearrange("p (g d) -> p g d", d=64), kn[b][:].rearrange("p (g d) -> p g d", d=64),
                               bvec[:, gofs:gofs + 64].unsqueeze(1).to_broadcast([128, 16, 64]), ALU.mult, [knk, "bvec"], [knk])
                            src, srck = kn[b], knk
                            if t >= 4:
                                rti = t - 4
                                cosv = consts[:, CO_COS + rti * 32:CO_COS + rti * 32 + 32].rearrange("p (a f) -> p a f", a=2) \
                                    .unsqueeze(1).to_broadcast([128, 16, 2, 16])
                                sinv = consts[:, CO_SIN + rti * 32:CO_SIN + rti * 32 + 32].rearrange("p (a f) -> p a f", a=2) \
                                    .unsqueeze(1).to_broadcast([128, 16, 2, 16])
                                x5 = kn[b][:].rearrange("p (g a h f) -> p g a h f", g=16, a=2, h=2)
                                o5 = kr[b][:].rearrange("p (g a h f) -> p g a h f", g=16, a=2, h=2)
                                t5 = rt[:].rearrange("p j (g a f) -> p j g a f", g=16, a=2)
                                krk = "kr%d" % b
                                tt(t5[:, 0], x5[:, :, :, 0, :], cosv, ALU.mult, [knk, "consts"], ["rt0"])
                                tt(t5[:, 1], x5[:, :, :, 1, :], sinv, ALU.mult, [knk, "consts"], ["rt1"])
                                tt(o5[:, :, :, 0, :], t5[:, 0], t5[:, 1], ALU.subtract, ["rt0", "rt1"], [krk])
                                tt(t5[:, 0], x5[:, :, :, 1, :], cosv, ALU.mult, [knk, "consts"], ["rt0"])
                                tt(t5[:, 1], x5[:, :, :, 0, :], sinv, ALU.mult, [knk, "consts"], ["rt1"])
                                tt(o5[:, :, :, 1, :], t5[:, 0], t5[:, 1], ALU.add, ["rt0", "rt1"], [krk])
                                src, srck = kr[b], krk
                            if is_k and t < 4:
                                finals.append(dma(nk[t * 128:(t + 1) * 128, :], src[:], [srck], []))
                            c0 = colfn(t)
                            for half in range(2):
                                pb = 6 + half
                                for q in range(4):
                                    h = half * 4 + q
                                    tr(bank(pb)[:, q * 128:(q + 1) * 128], src[:, h * 128:(h + 1) * 128], ident,
                                       [srck, "consts"], [PK[pb]])
                                act(dstT[:, half * 4:half * 4 + 4, c0:c0 + 128],
                                    bank(pb).rearrange("p (h t) -> p h t", h=4), AF.Identity, [PK[pb]], [dkeys(t)])

                        for step in range(len(tiles) + 1):
                            if step < len(tiles):
                                emit_proj(step)
                            if step >= 1:
                                emit_rest(step - 1)

                    qk_section(C_K, range(12), BO_KG, KT, lambda t: kkeys[t], lambda t: t * 128, True)
                    phase_end("C2")
                    qk_section(C_Q, range(8), BO_QG, QT, lambda t: qkeys[t], lambda t: t * 128, False)
                dbg("KT", KT[:], kkeys)
                dbg("QT", QT[:], qkeys)
                dbg("Vaug", Vaug[:], vkeys)
                phase_end("C")

                barrier()
                with contextlib.ExitStack() as sD:
                    NPT = 8
                    PT = sb_t(sD, "PT", [128, NPT, 512], BF16)
                    onesb = sb_t(sD, "onesb", [128, 128], BF16)
                    lnc = sb_t(sD, "lnc", [128, 1])
                    rc = [sb_t(sD, "rc%d" % i, [128, 512]) for i in range(2)]
                    tq = [sb_t(sD, "tq%d" % i, [128, 512]) for i in range(2)]
                    ocmb = [sb_t(sD, "ocmb%d" % i, [128, 512]) for i in range(2)]
                    sqo = sb_t(sD, "sqo", [128, 512])
                    rso = sb_t(sD, "rso", [128, 512])
                    memset(onesb[:], 1.0, ["onesb"])
                    memset(lnc[:], math.log(1.0 - LAM_INIT), ["lnc"])
                    subgT = pvec[:, PO_SUBG:PO_SUBG + 1]
                    segs = [([0, 1], [0, 1]), ([2, 3], [2, 3]), ([4, 5, 6, 7], list(range(4, 16)))]
                    it = 0
                    ptc = 0
                    for qts, kts in segs:
                        nq = len(qts) * 128
                        q0 = qts[0] * 128
                        qk_ = [qkeys[t] for t in qts]
                        nk_t = len(kts)
                        for h in range(NH):
                            slots = [[], []]

                            def pv(c, ki):
                                bO, bL = 2 + 2 * c, 3 + 2 * c
                                kt = kts[ki]
                                sl = slots[c][ki]
                                mm(bank(bO)[:, 0:nq], Vaug[:, kt, h, 0:128], PT[:, sl, 0:nq], ki == 0, ki == nk_t - 1,
                                   ["PT.%d" % sl, vkeys[kt]], [PK[bO]])
                                mm(bank(bL)[:, 0:nq], onesb[:], PT[:, sl, 0:nq], ki == 0, ki == nk_t - 1,
                                   ["PT.%d" % sl, "onesb"], [PK[bL]])

                            for ki, kt in enumerate(kts):
                                kc0 = kt * 128
                                for c in range(2):
                                    pr = slice(64 * c, 64 * c + 64)
                                    pb = (0 if c == 0 else 6) + ki % 2
                                    sl = ptc % NPT
                                    ptc += 1
                                    slots[c].append(sl)
                                    mm(bank(pb)[:, 0:nq], KT[pr, h, kc0:kc0 + 128], QT[pr, h, q0:q0 + nq], True, True,
                                       [kkeys[kt]] + qk_, [PK[pb]])
                                    act(PT[:, sl, 0:nq], bank(pb)[:, 0:nq], AF.Exp, [PK[pb]], ["PT.%d" % sl], scale=0.125)
                                if ki >= 1:
                                    pv(0, ki - 1)
                                    pv(1, ki - 1)
                            pv(0, nk_t - 1)
                            pv(1, nk_t - 1)
                            fb = it % 2
                            it += 1
                            for c in range(2):
                                bO, bL = 2 + 2 * c, 3 + 2 * c
                                act(rc[c][:, 0:nq], bank(bL)[:, 0:nq], AF.Ln, [PK[bL]], ["rc%d" % c])
                                act(rc[c][:, 0:nq], rc[c][:, 0:nq], AF.Exp, ["rc%d" % c], ["rc%d" % c], scale=-1.0)
                                tt(tq[c][:, 0:nq], bank(bO)[:, 0:nq], rc[c][:, 0:nq], ALU.mult, [PK[bO], "rc%d" % c], ["tq%d" % c])
                            ok_ = "ocmb%d" % fb
                            stt(ocmb[fb][:, 0:nq], tq[1][:, 0:nq], lamt[:, 1:2], tq[0][:, 0:nq], ALU.mult, ALU.add,
                                ["tq0", "tq1", "lamt"], [ok_])
                            act(sqo[:, 0:nq], ocmb[fb][:, 0:nq], AF.Square, [ok_], ["sqo"])
                            pbt = it % 2
                            mm(bank(pbt)[:, 0:nq], ones, sqo[:, 0:nq], True, True, ["consts", "sqo"], [PK[pbt]])
                            ts(rso[:, 0:nq], bank(pbt)[:, 0:nq], 1.0 / 128, EPS, ALU.mult, ALU.add, [PK[pbt]], ["rso"])
                            act(rso[:, 0:nq], rso[:, 0:nq], AF.Ln, ["rso"], ["rso"])
                            act(rso[:, 0:nq], rso[:, 0:nq], AF.Exp, ["rso", "lnc"], ["rso"], scale=-0.5, bias=lnc[:, 0:1])
                            stt(OT[:, h, q0:q0 + nq], ocmb[fb][:, 0:nq], subgT, rso[:, 0:nq], ALU.mult, ALU.mult,
                                [ok_, "pvec", "rso"], ["OT"])
            dbg("OT", OT[:], ["OT"])
            phase_end("D")

            barrier()
            with contextlib.ExitStack() as L1b:
                YT = sb_t(L1b, "YT", [128, 16, 1024], BF16)
                arenaW = sb_t(L1b, "arenaW", [128, 12288], BF16)

                def wview(off, k, c):
                    return arenaW[:, off:off + k * c].rearrange("p (k c) -> p k c", k=k)
                ssacc = sb_t(L1b, "ssacc", [128, 8])
                memset(ssacc[:], 0.0, ["ssacc"])
                w_in_v = w_in.rearrange("(kt p) c -> p kt c", p=128)
                with contextlib.ExitStack() as sF:
                    wdt = sb_t(sF, "wdt", [128, 8, 128], BF16)
                    DT = sb_t(sF, "DT", [128, 12, 64])
                    Aa = sb_t(sF, "Aa", [128, 12, 64])
                    Aneg = sb_t(sF, "Aneg", [128, 128])
                    dma(wdt[:], w_dt.rearrange("(kt p) c -> p kt c", p=128), [], ["wdt"], q="pool")
                    for t in range(12):
                        c0 = 0 if t < 4 else 64
                        pb, po = (0, t * 64) if t < 8 else (1, (t - 8) * 64)
                        for kt in range(8):
                            mm(bank(pb)[:, po:po + 64], hT[:, kt, t * 128:(t + 1) * 128], wdt[:, kt, c0:c0 + 64], kt == 0, kt == 7,
                               [hkeys[t], "wdt"], [PK[pb]])
                    tt(DT[:, 0:4, :], bank(0)[:, 0:256].rearrange("p (t c) -> p t c", t=4),
                       bvec[:, BO_DTBP:BO_DTBP + 64].unsqueeze(1).to_broadcast([128, 4, 64]), ALU.add, [PK[0], "bvec"], ["DT"])
                    tt(DT[:, 4:8, :], bank(0)[:, 256:512].rearrange("p (t c) -> p t c", t=4),
                       bvec[:, BO_DTBS:BO_DTBS + 64].unsqueeze(1).to_broadcast([128, 4, 64]), ALU.add, [PK[0], "bvec"], ["DT"])
                    tt(DT[:, 8:12, :], bank(1)[:, 0:256].rearrange("p (t c) -> p t c", t=4),
                       bvec[:, BO_DTBS:BO_DTBS + 64].unsqueeze(1).to_broadcast([128, 4, 64]), ALU.add, [PK[1], "bvec"], ["DT"])
                    act(DT[:], DT[:], AF.Exp, ["DT"], ["DT"])
                    act(DT[:], DT[:], AF.Ln, ["DT"], ["DT"], bias=ones[:, 0:1])
                    act(Aneg[:], bvec[:, BO_ALP:BO_ALP + 128], AF.Exp, ["bvec"], ["Aneg"])
                    ts(Aneg[:], Aneg[:], -1.0, None, ALU.mult, None, ["Aneg"], ["Aneg"])
                    tt(Aa[:, 0:4, :], DT[:, 0:4, :], Aneg[:, 0:64].unsqueeze(1).to_broadcast([128, 4, 64]), ALU.mult,
                       ["DT", "Aneg"], ["Aa"])
                    tt(Aa[:, 4:12, :], DT[:, 4:12, :], Aneg[:, 64:128].unsqueeze(1).to_broadcast([128, 8, 64]), ALU.mult,
                       ["DT", "Aneg"], ["Aa"])
                    dbg("DT", DT[:], ["DT"])
                    phase_end("E")

                    wx = [wview(0, 8, 256), wview(2048, 8, 256)]
                    wB = [wview(4096, 8, 128), wview(5120, 8, 128)]
                    wC = [wview(6144, 8, 128), wview(7168, 8, 128)]
                    wz = [wview(8192, 8, 256), wview(10240, 8, 256)]
                    AKEYS = ["wx0", "wx1", "wB0", "wB1", "wC0", "wC1", "wz0", "wz1"]
                    rawpad1 = sb_t(sF, "rawpad0", [128, 1548], BF16)
                    dg = sb_t(sF, "dg", [128, 4, 5, 128], BF16)
                    xs = sb_t(sF, "xs", [128, 1536])
                    xtok = sb_t(sF, "xtok", [128, 12, 256], BF16)
                    BTt = sb_t(sF, "BTt", [128, 1536], BF16)
                    CTt = sb_t(sF, "CTt", [128, 1536], BF16)
                    Btok = sb_t(sF, "Btok", [128, 12, 128], BF16)
                    zs = sb_t(sF, "zs", [128, 8, 256])
                    h0g1 = sb_t(sF, "h0g0", [128, 2, 256])
                    h0g = [h0g1, h0g1]
                    a8 = sb_t(sF, "a8", [128, 2, 12, 4])
                    dt8 = sb_t(sF, "dt8", [128, 2, 12, 4])
                    EDT = sb_t(sF, "EDT", [128, 3, 2, 12, 4])
                    DDg = sb_t(sF, "DDg", [128, 2, 12, 4])
                    diagD = sb_t(sF, "diagD", [128, 4, 128], BF16)
                    CBm = [sb_t(sF, "CBm%d" % i, [128, 4, 128], BF16) for i in range(2)]
                    aU1 = sb_t(sF, "aU0", [128, 2, 4, 128])
                    aU = [aU1, aU1]
                    Et = [sb_t(sF, "Et%d" % i, [128, 2, 4, 128], BF16) for i in range(2)]
                    Mt = [sb_t(sF, "Mt%d" % i, [128, 4, 4, 128], BF16) for i in range(2)]
                    xd = [sb_t(sF, "xd%d" % i, [128, 4, 256], BF16) for i in range(2)]
                    xdd = [sb_t(sF, "xdd0", [128, 4, 256], BF16), sb_t(sF, "xdd1", [128, 8, 256], BF16)]
                    hprev = [sb_t(sF, "hprev%d" % i, [128, 4, 256], BF16) for i in range(2)]
                    hs = [[sb_t(sF, "hs%d_%d" % (d, i), [128, 256]) for i in range(2)] for d in range(2)]
                    hzero = sb_t(sF, "hzero", [128, 256])
                    t1 = sb_t(sF, "t1", [128, 4, 256])
                    t2 = sb_t(sF, "t2", [128, 4, 256])
                    sstmp = sb_t(sF, "sstmp", [128, 4])
                    fst1 = sb_t(sF, "fst0", [128, 2, 128])
                    fst = [fst1, fst1]
                    memset(rawpad1[:], 0.0, ["rawpad0"], eng="pool")
                    memset(hzero[:], 0.0, ["hzero"])
                    seg_pad = [(0, 0, 256), (260, 256, 256), (520, 512, 1024)]
                    fcount = [0]
                    rpc = [0]
                    PARTS = ("p", "s")

                    def stageA(g, part):
                        wi = g % 2
                        pn = PARTS[part]
                        if part == 0:
                            dma(wx[wi][:], w_in_v[:, :, C_X + g * 256:C_X + (g + 1) * 256], [], ["wx%d" % wi], q="pool")
                            dma(wB[wi][:], w_in_v[:, :, C_B + g * 128:C_B + (g + 1) * 128], [], ["wB%d" % wi], q="pool")
                            dma(wC[wi][:], w_in_v[:, :, C_C + g * 128:C_C + (g + 1) * 128], [], ["wC%d" % wi], q="pool")
                            dma(wz[wi][:], w_in_v[:, :, C_Z + g * 256:C_Z + (g + 1) * 256], [], ["wz%d" % wi], q="pool")
                        tbs = [0] if part == 0 else [1, 2]
                        segs = seg_pad[0:2] if part == 0 else seg_pad[2:3]
                        chunks = list(range(0, 4)) if part == 0 else list(range(4, 12))
                        tok0 = 0 if part == 0 else 512
                        ntok = 512 if part == 0 else 1024
                        blocks = [(wx[wi], "wx%d" % wi, 0, 2 * g), (wx[wi], "wx%d" % wi, 128, 2 * g + 1),
                                  (wB[wi], "wB%d" % wi, 0, 16 + g), (wC[wi], "wC%d" % wi, 0, 24 + g)]
                        rk = "rawpad0"
                        ak = "acc"
                        xk = "xs"
                        pbank = {}

                        def proj(bi):
                            wt, wk_, wc0, cblk = blocks[bi]
                            for tb in tbs:
                                pb = tb % 2
                                for kt in range(8):
                                    mm(bank(pb), wt[:, kt, wc0:wc0 + 128], hT[:, kt, tb * 512:(tb + 1) * 512], kt == 0, kt == 7,
                                       [wk_] + hkeys[tb * 4:tb * 4 + 4], [PK[pb]])

                        def evac(bi):
                            for tb in tbs:
                                pb = tb % 2
                                if tb == 0:
                                    dst = rawpad1[:, 0:520].rearrange("p (s c) -> p s c", s=2)[:, :, 2:258]
                                    act(dst, bank(pb).rearrange("p (s c) -> p s c", s=2), AF.Identity, [PK[pb]], [rk])
                                else:
                                    o = 522 + (tb - 1) * 512
                                    act(rawpad1[:, o:o + 512], bank(pb), AF.Identity, [PK[pb]], [rk])

                        po_w = PO_CWP if part == 0 else PO_CWS
                        for bi_ in range(4):
                            cblk_ = blocks[bi_][3]
                            for j in range(5):
                                act(dg[:, bi_, j, :], ident, AF.Identity, ["consts", "pvec"], ["dg.%d.%d" % (bi_, j)],
                                    scale=pvec[:, po_w + cblk_ * 5 + j: po_w + cblk_ * 5 + j + 1])
                        subs = [(0, 0, 0, 256), (260, 0, 256, 256)] if part == 0 else [(520, 0, 0, 512), (520 + 512, 1, 0, 512)]

                        def conv(bi):
                            for (pbase, pb, co, n) in subs:
                                for j in range(5):
                                    mm(bank(pb)[:, co:co + n], dg[:, bi, j, :], rawpad1[:, pbase + j:pbase + j + n], j == 0, j == 4,
                                       ["dg.%d.%d" % (bi, j), rk], [PK[pb]])

                        def silu(bi):
                            cblk = blocks[bi][3]
                            bia = pvec[:, PO_CB + cblk:PO_CB + cblk + 1]
                            outs = []
                            if part == 0:
                                outs.append((0, 0, 512, 0))
                            else:
                                outs.append((0, 0, 512, 512))
                                outs.append((1, 0, 512, 1024))
                            for (pb, co, n, tcol) in outs:
                                if bi == 3:
                                    act(CTt[:, tcol:tcol + n], bank(pb)[:, co:co + n], AF.Silu, [PK[pb], "pvec"], ["CTt." + pn], bias=bia)
                                else:
                                    act(xs[:, tcol:tcol + n], bank(pb)[:, co:co + n], AF.Silu, [PK[pb], "pvec"], [xk], bias=bia)
                            if bi == 2:
                                for (pb, co, n, tcol) in outs:
                                    act(BTt[:, tcol:tcol + n], bank(pb)[:, co:co + n], AF.Silu, [PK[pb], "pvec"], ["BTt." + pn], bias=bia)

                        def transposes(bi):
                            if bi == 3:
                                return
                            for q4 in range(0, len(chunks), 4):
                                pb = (q4 // 4) % 2
                                cs = chunks[q4:q4 + 4]
                                for q, t in enumerate(cs):
                                    tr(bank(pb)[:, q * 128:(q + 1) * 128], xs[:, t * 128:(t + 1) * 128], ident,
                                       [xk, "consts"], [PK[pb]])
                                src = bank(pb).rearrange("p (t c) -> p t c", t=4)
                                if bi < 2:
                                    act(xtok[:, cs[0]:cs[0] + 4, bi * 128:(bi + 1) * 128], src, AF.Identity, [PK[pb]], ["xtok." + pn])
                                else:
                                    act(Btok[:, cs[0]:cs[0] + 4, :], src, AF.Identity, [PK[pb]], ["Btok." + pn])

                        for bi in range(4):
                            proj(bi)
                            evac(bi)
                            yield
                            conv(bi)
                            silu(bi)
                            yield
                            transposes(bi)
                            yield
                        for t in ([0, 1, 2, 3] if part == 0 else [4, 5, 6, 7]):
                            pb = t % 2
                            for kt in range(8):
                                mm(bank(pb)[:, 0:256], hT[:, kt, t * 128:(t + 1) * 128], wz[wi][:, kt, :], kt == 0, kt == 7,
                                   [hkeys[t], "wz%d" % wi], [PK[pb]])
                            act(zs[:, t, :], bank(pb)[:, 0:256], AF.Silu, [PK[pb]], ["zs.%d" % t])
                        yield

                    def prepG(g):
                        for d in range(2):
                            hc0 = d * 32 + 4 * g
                            cp(a8[:, d], Aa[:, :, hc0:hc0 + 4], ["Aa"], ["a8"])
                            cp(dt8[:, d], DT[:, :, hc0:hc0 + 4], ["DT"], ["dt8"])
                        for d in range(2):
                            rhs = a8[:, d].rearrange("p c h -> p (c h)")
                            for ki, msk in enumerate(((mU, mLW)[d], (mSL, mSU)[d], ones)):
                                o = (ki * 2 + d) * 48
                                mm(bank(7)[:, o:o + 48], msk, rhs, True, True, ["a8", "consts"], [PK[7]])
                        act(EDT[:].rearrange("p k d c h -> p (k d c h)"), bank(7)[:, 0:288], AF.Exp, [PK[7]], ["EDT"])
                        tt(DDg[:], EDT[:, 1], dt8[:], ALU.mult, ["EDT", "dt8"], ["DDg"])
                        dsk = bvec[:, BO_DSK + 4 * g:BO_DSK + 4 * g + 4]
                        for h in range(4):
                            ts(diagD[:, h, :], ident, dsk[:, h:h + 1], None, ALU.mult, None, ["consts", "bvec"], ["diagD"])

                    def stageB(g, part):
                        wi = g % 2
                        pn = PARTS[part]
                        XK, BTK, CTK, BKK, ZK = "xtok." + pn, "BTt." + pn, "CTt." + pn, "Btok." + pn, "zs." + pn
                        c0 = 0 if part == 0 else 4
                        if part == 0:
                            prepG(g)
                        if part == 0:
                            chains = [(0, None, [0, 1], True, 0), (1, None, [1, 0], True, 0),
                                      (0, None, [2, 3], True, 1), (1, None, [3, 2], True, 1)]
                        else:
                            chains = [(0, h0g1[:, 0, :], [0, 1, 2, 3], False, 0), (1, h0g1[:, 1, :], [7, 6, 5, 4, 3, 2, 1, 0], False, 0)]
                        work = []
                        for chn, (d, init, order, need_final, sqi) in enumerate(chains):
                            for kstep, ci in enumerate(order):
                                work.append((chn, kstep, ci))
                        state = {}
                        for chn, (d, init, order, need_final, sqi) in enumerate(chains):
                            state[chn] = (hzero[:], "hzero") if init is None else (init, "h0g0")
                        pos = [0]

                        def s_round():
                            rnd = []
                            nslot = 0
                            while pos[0] < len(work) and nslot < 8:
                                chn, kstep, ci = work[pos[0]]
                                d, init, order, need_final, sqi = chains[chn]
                                upd = need_final or kstep < len(order) - 1
                                rnd.append((chn, kstep, ci, upd, nslot if upd else None))
                                if upd:
                                    pb = 4 + nslot // 2
                                    so = (nslot % 2) * 256
                                    mm(bank(pb)[:, so:so + 256], Btok[:, c0 + ci, :], xdd[d][:, ci, :], True, True,
                                       [BKK, "xdd%d" % d], [PK[pb]])
                                    nslot += 1
                                pos[0] += 1
                            return rnd

                        def rec_round(rnd):
                            for chn, kstep, ci, upd, sl in rnd:
                                d, init, order, need_final, sqi = chains[chn]
                                cur, curk = state[chn]
                                if ci < 4:
                                    cp(hprev[d][:, ci, :], cur, [curk], ["hprev%d.%d" % (d, ci)])
                                if upd:
                                    pb = 4 + sl // 2
                                    so = (sl % 2) * 256
                                    nxt, nk_ = hs[d][kstep % 2], "hs%d_%d" % (d, kstep % 2)
                                    if init is None and kstep == 0:
                                        cp(nxt[:], bank(pb)[:, so:so + 256], [PK[pb]], [nk_])
                                    else:
                                        tt(nxt[:].rearrange("p (h q) -> p h q", h=4), cur.rearrange("p (h q) -> p h q", h=4),
                                           EDT[:, 2, d, c0 + ci, :].unsqueeze(2).to_broadcast([128, 4, 64]), ALU.mult,
                                           [curk, "EDT"], [nk_])
                                        tt(nxt[:], nxt[:], bank(pb)[:, so:so + 256], ALU.add, [nk_, PK[pb]], [nk_])
                                    cur, curk = nxt[:], nk_
                                    state[chn] = (cur, curk)
                                if need_final and kstep == len(order) - 1:
                                    fcount[0] += 1
                                    pbf = fcount[0] % 2
                                    for j in range(2):
                                        tr(bank(pbf)[:, j * 128:(j + 1) * 128], cur[:, j * 128:(j + 1) * 128], ident,
                                           [curk, "consts"], [PK[pbf]])
                                    cp(fst1[:], bank(pbf)[:, 0:256].rearrange("p (j n) -> p j n", j=2), [PK[pbf]], ["fst0"], eng="act")
                                    dst_o = (sf if d == 0 else sb)[sqi, g * 256:(g + 1) * 256, :].rearrange("(j p) n -> p j n", p=128)
                                    finals.append(dma(dst_o, fst1[:], ["fst0"], []))

                        def dexp(k, d, half):
                            ab = k % 2
                            msk_u = mU if d == 0 else mLW
                            msk_s = mSL if d == 0 else mSU
                            cc = c0 + 2 * half
                            tt(aU1[:], a8[:, d, cc:cc + 2, :].unsqueeze(3).to_broadcast([128, 2, 4, 128]),
                               msk_u.unsqueeze(1).unsqueeze(1).to_broadcast([128, 2, 4, 128]), ALU.mult,
                               ["a8", "consts"], ["aU0"])
                            for k2 in range(2):
                                mm(bank(2 + k2), msk_s, aU1[:, k2].rearrange("p h l -> p (h l)"), True, True,
                                   ["aU0", "consts"], [PK[2 + k2]])
                            act(Et[ab][:].rearrange("p c h l -> p (c h l)"), psA[:, 1024:2048],
                                AF.Exp, [PK[2], PK[3]], ["Et%d" % ab])

                        def mtmul(k, d, half):
                            ab = k % 2
                            tt(Mt[d][:, 2 * half:2 * half + 2], Et[ab][:],
                               CBm[d][:, 2 * half:2 * half + 2, :].unsqueeze(2).to_broadcast([128, 2, 4, 128]),
                               ALU.mult, ["Et%d" % ab, "CBm%d" % d], ["Mt%d" % d])

                        for d in range(2):
                            nch = 8 if (part == 1 and d == 1) else 4
                            tt(xdd[d][:, 0:nch, :].rearrange("p c (h q) -> p c h q", h=4),
                               xtok[:, c0:c0 + nch, :].rearrange("p c (h q) -> p c h q", h=4),
                               DDg[:, d, c0:c0 + nch, :].unsqueeze(3).to_broadcast([128, nch, 4, 64]), ALU.mult,
                               [XK, "DDg"], ["xdd%d" % d])
                        for ci in range(4):
                            c = c0 + ci
                            mm(bank(1)[:, ci * 128:(ci + 1) * 128], BTt[:, c * 128:(c + 1) * 128], CTt[:, c * 128:(c + 1) * 128],
                               True, True, [BTK, CTK], [PK[1]])
                        cb3 = bank(1).rearrange("p (c l) -> p c l", c=4)
                        tt(CBm[0][:], cb3, mU.unsqueeze(1).to_broadcast([128, 4, 128]), ALU.mult, [PK[1], "consts"], ["CBm0"])
                        tt(CBm[1][:], cb3, mLW.unsqueeze(1).to_broadcast([128, 4, 128]), ALU.mult, [PK[1], "consts"], ["CBm1"])
                        rnd1 = s_round()
                        yield
                        seq = [(0, 0), (0, 1), (1, 0), (1, 1)]
                        dexp(0, 0, 0)
                        yield
                        for d in range(2):
                            tt(xd[d][:].rearrange("p c (h q) -> p c h q", h=4),
                               xtok[:, c0:c0 + 4, :].rearrange("p c (h q) -> p c h q", h=4),
                               dt8[:, d, c0:c0 + 4, :].unsqueeze(3).to_broadcast([128, 4, 4, 64]), ALU.mult,
                               [XK, "dt8"], ["xd%d" % d])
                        yield
                        rec_round(rnd1)
                        yield
                        for k in range(4):
                            d, half = seq[k]
                            if k + 1 < 4:
                                dexp(k + 1, *seq[k + 1])
                            mtmul(k, d, half)
                            yield
                        while pos[0] < len(work):
                            rnd = s_round()
                            yield
                            rec_round(rnd)
                            yield
                        for ci in range(4):
                            pb = 6 + ci // 2
                            yo = (ci % 2) * 256
                            for h in range(4):
                                mm(bank(pb)[:, yo + h * 64:yo + (h + 1) * 64], diagD[:, h, :], xtok[:, c0 + ci, h * 64:(h + 1) * 64],
                                   h == 0, False, ["diagD", XK], [PK[pb]])
                            for d in range(2):
                                for h in range(4):
                                    mm(bank(pb)[:, yo + h * 64:yo + (h + 1) * 64], Mt[d][:, ci, h, :], xd[d][:, ci, h * 64:(h + 1) * 64],
                                       False, d == 1 and h == 3, ["Mt%d" % d, "xd%d" % d], [PK[pb]])
                        for d in range(2):
                            for ci in range(4):
                                pb = (2 if d == 0 else 4) + ci // 2
                                yo = (ci % 2) * 256
                                mm(bank(pb)[:, yo:yo + 256], CTt[:, (c0 + ci) * 128:(c0 + ci + 1) * 128], hprev[d][:, ci, :], True, True,
                                   [CTK, "hprev%d.%d" % (d, ci)], [PK[pb]])
                        yield
                        for hb in range(2):
                            for d in range(2):
                                pb = (2 if d == 0 else 4) + hb
                                dstt = t1 if d == 0 else t2
                                tt(dstt[:, 2 * hb:2 * hb + 2, :].rearrange("p c (h q) -> p c h q", h=4),
                                   bank(pb).rearrange("p (c h q) -> p c h q", c=2, h=4),
                                   EDT[:, 0, d, c0 + 2 * hb:c0 + 2 * hb + 2, :].unsqueeze(3).to_broadcast([128, 2, 4, 64]), ALU.mult,
                                   [PK[pb], "EDT"], ["t1" if d == 0 else "t2"])
                        tt(t1[:], t1[:], t2[:], ALU.add, ["t1", "t2"], ["t1"])
                        for hb in range(2):
                            tt(t1[:, 2 * hb:2 * hb + 2, :], t1[:, 2 * hb:2 * hb + 2, :],
                               bank(6 + hb).rearrange("p (c q) -> p c q", c=2), ALU.add, ["t1", PK[6 + hb]], ["t1"])
                        if g == 0 and part == 1:
                            dbg("ysum", t1[:], ["t1"])
                        tt(t2[:], t1[:], zs[:, c0:c0 + 4, :], ALU.mult, ["t1"] + ["zs.%d" % (c0 + i_) for i_ in range(4)], ["t2"])
                        yield
                        for ci in range(4):
                            act(t1[:, ci, :], t2[:, ci, :], AF.Square, ["t2"], ["t1", "sstmp"], accum=sstmp[:, ci:ci + 1])
                        tt(ssacc[:, c0:c0 + 4], ssacc[:, c0:c0 + 4], sstmp[:], ALU.add, ["ssacc", "sstmp"], ["ssacc"])
                        for j in range(2):
                            pb = 2 + j
                            for ci in range(4):
                                tr(bank(pb)[:, ci * 128:(ci + 1) * 128], t2[:, ci, j * 128:(j + 1) * 128], ident, ["t2", "consts"], [PK[pb]])
                            act(YT[:, 2 * g + j, c0 * 128:(c0 + 4) * 128], bank(pb), AF.Identity, [PK[pb], "pvec"], ["YT"],
                                scale=pvec[:, PO_SSG + 2 * g + j:PO_SSG + 2 * g + j + 1])
                        if part == 1 and g < 7:
                            dma(h0g1[:], h0T[:, :, (g + 1) * 256:(g + 2) * 256], [], ["h0g0"])
                        yield

                    def run_interleaved(gens):
                        gens = [[x[0], x[1], 0] for x in gens if x is not None]
                        while gens:
                            for x in list(gens):
                                try:
                                    P.tag = "%s#%d" % (x[0], x[2])
                                    x[2] += 1
                                    next(x[1])
                                except StopIteration:
                                    gens.remove(x)
                        P.tag = ""

                    dma(h0g1[:], h0T[:, :, 0:256], [], ["h0g0"])
                    def SA(g, p):
                        return ("A%d%s" % (g, PARTS[p]), stageA(g, p))

                    def SB(g, p):
                        return ("B%d%s" % (g, PARTS[p]), stageB(g, p))

                    run_interleaved([SA(0, 0)])
                    run_interleaved([SB(0, 0), SA(0, 1)])
                    wgb0_v, wa0_v, wga0_v = wview(0, 8, 512), wview(4096, 8, 512), wview(8192, 8, 512)
                    for g in range(8):
                        if g == 7:
                            dma(wgb0_v, w_in_v[:, :, C_GB:C_GB + 512], [], ["wgb0"] + AKEYS, q="pool")
                            dma(wa0_v, w_a.rearrange("(kt p) c -> p kt c", p=128)[:, :, 0:512], [], ["wa0"] + AKEYS, q="pool")
                            dma(wga0_v, w_in_v[:, :, C_GA:C_GA + 512], [], ["wga0"] + AKEYS, q="pool")
                        run_interleaved([SB(g, 1), SA(g + 1, 0) if g < 7 else None])
                        if g < 7:
                            run_interleaved([SB(g + 1, 0), SA(g + 1, 1)])
                dbg("YT", YT[:], ["YT"])
                dbg("ssacc", ssacc[:], ["ssacc"])
                phase_end("F")

                barrier()
                with contextlib.ExitStack() as sG:
                    wga = [wga0_v, sb_t(sG, "wga1", [128, 8, 512], BF16)]
                    wgb = [wgb0_v, sb_t(sG, "wgb1", [128, 8, 512], BF16)]
                    wa = [wa0_v, sb_t(sG, "wa1", [128, 8, 512], BF16)]
                    wb_ = [sb_t(sG, "wb%d" % i, [128, 16, 512], BF16) for i in range(2)]
                    rsy = sb_t(sG, "rsy", [128, 16])
                    sga = sb_t(sG, "sga", [128, 512])
                    sgb = sb_t(sG, "sgb", [128, 512])
                    mrg = [sb_t(sG, "mrg%d" % i, [128, 512]) for i in range(2)]
                    w_a_v = w_a.rearrange("(kt p) c -> p kt c", p=128)
                    w_b_v = w_b.rearrange("(kt p) c -> p kt c", p=128)
                    dma(wb_[0][:], w_b_v[:, :, 0:512], [], ["wb0"], q="pool")
                    dma(wgb[1][:], w_in_v[:, :, C_GB + 512:C_GB + 1024], [], ["wgb1"], q="pool")
                    dma(wa[1][:], w_a_v[:, :, 512:1024], [], ["wa1"], q="pool")
                    dma(wb_[1][:], w_b_v[:, :, 512:1024], [], ["wb1"], q="pool")
                    dma(wga[1][:], w_in_v[:, :, C_GA + 512:C_GA + 1024], [], ["wga1"], q="pool")
                    rstd_from_ss(ssacc[:], 8, rsy[:, 8:16], rsy[:, 0:8], "ssacc", "rsya", "rsyb", 1.0 / 2048)
                    sga2 = [sga, sb_t(sG, "sga_b", [128, 512])]
                    sgb2 = [sgb, sb_t(sG, "sgb_b", [128, 512])]
                    for t in range(8):
                        for cb in range(2):
                            b0 = 4 * cb
                            sga_, sgb_ = sga2[cb], sgb2[cb]
                            sak, sbk = "sga%d" % cb, "sgb%d" % cb
                            for kt in range(8):
                                mm(bank(b0 + 1), hT[:, kt, t * 128:(t + 1) * 128], wgb[cb][:, kt, :], kt == 0, kt == 7,
                                   [hkeys[t], "wgb%d" % cb], [PK[b0 + 1]])
                            for h in range(8):
                                mm(bank(b0 + 2), OT[:, h, t * 128:(t + 1) * 128], wa[cb][:, h, :], h == 0, h == 7,
                                   ["OT", "wa%d" % cb], [PK[b0 + 2]])
                            for kt in range(16):
                                mm(bank(b0 + 3), YT[:, kt, t * 128:(t + 1) * 128], wb_[cb][:, kt, :], kt == 0, kt == 15,
                                   ["YT", "wb%d" % cb], [PK[b0 + 3]])
                            for kt in range(8):
                                mm(bank(b0), hT[:, kt, t * 128:(t + 1) * 128], wga[cb][:, kt, :], kt == 0, kt == 7,
                                   [hkeys[t], "wga%d" % cb], [PK[b0]])
                            act(sgb_[:], bank(b0 + 1), AF.Sigmoid, [PK[b0 + 1]], [sbk])
                            act(sga_[:], bank(b0), AF.Sigmoid, [PK[b0]], [sak])
                            mk = "mrg%d" % cb
                            stt(sgb_[:], bank(b0 + 3), rsy[:, 8 + t:9 + t], sgb_[:], ALU.mult, ALU.mult, [PK[b0 + 3], "rsyb", sbk], [sbk])
                            tt(mrg[cb][:], sga_[:], bank(b0 + 2), ALU.mult, [sak, PK[b0 + 2]], [mk])
                            tt(mrg[cb][:], mrg[cb][:], sgb_[:], ALU.add, [mk, sbk], [mk])
                            for q in range(4):
                                tr(bank(b0)[:, q * 128:(q + 1) * 128], mrg[cb][:, q * 128:(q + 1) * 128], ident,
                                   [mk, "consts"], [PK[b0]])
                        for cb in range(2):
                            act(hT[:, 4 * cb:4 * cb + 4, t * 128:(t + 1) * 128], bank(4 * cb).rearrange("p (k t) -> p k t", k=4),
                                AF.Identity, [PK[4 * cb]], [hkeys[t]])
        dbg("mT", hT[:], hkeys)
        phase_end("G1")

        barrier()
        with contextlib.ExitStack() as sH:
            x1 = sb_t(sH, "x1", [128, 8, D])
            gb1 = [sb_t(sH, "gb1_%d" % r, [128, D]) for r in range(2)]
            gb2 = [sb_t(sH, "gb2_%d" % r, [128, D]) for r in range(2)]
            dgt = [sb_t(sH, "dgt%d" % i, [128, 128]) for i in range(2)]
            cnt_g = 0
            for r in range(2):
                for gi, gbt in enumerate((gb1, gb2)):
                    for cb in range(2):
                        pb = cnt_g % 2
                        for q in range(4):
                            kt = cb * 4 + q
                            db = cnt_g % 2
                            cnt_g += 1
                            ts(dgt[db][:], ident, AB[:, 4 + gi, kt, r:r + 1], None, ALU.mult, None, ["consts", "AB"], ["dgt%d" % db])
                            mm(bank(pb)[:, q * 128:(q + 1) * 128], ones, dgt[db][:], True, True, ["consts", "dgt%d" % db], [PK[pb]])
                        cp(gbt[r][:, cb * 512:(cb + 1) * 512], bank(pb), [PK[pb]], ["gb%d_%d" % (gi + 1, r)], eng="act")
            x1keys = ["x1.%d" % t for t in range(8)]
            h2T = sb_t(sH, "h2T", [128, 8, 1024], BF16)
            h2keys = ["h2T.%d" % t for t in range(8)]
            with contextlib.ExitStack() as sG2:
                wo = sb_t(sG2, "wo", [128, 8, D], BF16)
                xr = [sb_t(sG2, "xr%d" % i, [128, D]) for i in range(2)]
                tmpg = sb_t(sG2, "tmpg", [128, 512])
                dma(wo[:], w_o.rearrange("(kt p) c -> p kt c", p=128), [], ["wo"], q="pool")

                def g2_tile(t):
                    r = 0 if t < 4 else 1
                    xb = t % 2
                    dma(xr[xb][:], x_all[t * 128:(t + 1) * 128, :], [], ["xr%d" % xb])
                    for cb in range(2):
                        pb = 2 + cb
                        for kt in range(8):
                            mm(bank(pb), hT[:, kt, t * 128:(t + 1) * 128], wo[:, kt, cb * 512:(cb + 1) * 512], kt == 0, kt == 7,
                               [hkeys[t], "wo"], [PK[pb]])
                        tt(tmpg[:], bank(pb), gb1[r][:, cb * 512:(cb + 1) * 512], ALU.mult, [PK[pb], "gb1_%d" % r], ["tmpg"])
                        tt(x1[:, t, cb * 512:(cb + 1) * 512], tmpg[:], xr[xb][:, cb * 512:(cb + 1) * 512], ALU.add,
                           ["tmpg", "xr%d" % xb], [x1keys[t]])

                g2_tile(0)
                norm_mod_to_hT(lambda t: (x1[:, t, :], x1keys[t]), 8, lambda t: 0 if t < 4 else 1, 2, h2T, h2keys, sG2, "nH",
                               hook=lambda t: g2_tile(t + 1) if t + 1 < 8 else None)
            dbg("x1", x1[:], x1keys)
            phase_end("G2")
            barrier()
            with contextlib.ExitStack() as sFF:
                actT = sb_t(sFF, "actT", [128, 22, 1024], BF16)
                wd1 = sb_t(sFF, "wd0", [128, 22, 512], BF16)
                w_d_v = w_d.rearrange("(kt p) c -> p kt c", p=128)
                wgt = [sb_t(sFF, "wgt%d" % i, [128, 8, 128], BF16) for i in range(3)]
                wut = [sb_t(sFF, "wut%d" % i, [128, 8, 128], BF16) for i in range(3)]
                sgl = [sb_t(sFF, "sgl%d" % i, [128, 512]) for i in range(4)]
                w_g_v = w_g.rearrange("(kt p) c -> p kt c", p=128)
                w_u_v = w_u.rearrange("(kt p) c -> p kt c", p=128)
                for ft in range(22):
                    wbi = ft % 3
                    dma(wgt[wbi][:], w_g_v[:, :, ft * 128:(ft + 1) * 128], [], ["wgt%d" % wbi], q="pool")
                    dma(wut[wbi][:], w_u_v[:, :, ft * 128:(ft + 1) * 128], [], ["wut%d" % wbi], q="pool")
                    for th in range(2):
                        pg, pu = 4 * (ft % 2) + 2 * th, 4 * (ft % 2) + 2 * th + 1
                        for kt in range(8):
                            mm(bank(pg), wgt[wbi][:, kt, :], h2T[:, kt, th * 512:(th + 1) * 512], kt == 0, kt == 7,
                               ["wgt%d" % wbi] + h2keys[4 * th:4 * th + 4], [PK[pg]])
                        for kt in range(8):
                            mm(bank(pu), wut[wbi][:, kt, :], h2T[:, kt, th * 512:(th + 1) * 512], kt == 0, kt == 7,
                               ["wut%d" % wbi] + h2keys[4 * th:4 * th + 4], [PK[pu]])
                        sgi = 2 * (ft % 2) + th
                        act(sgl[sgi][:], bank(pg), AF.Silu, [PK[pg]], ["sgl%d" % sgi])
                        tt(actT[:, ft, th * 512:(th + 1) * 512], sgl[sgi][:], bank(pu), ALU.mult, ["sgl%d" % sgi, PK[pu]],
                           ["actT.%d" % th])
                dbg("actT", actT[:], ["actT.0", "actT.1"])
                wd = [wd1, wd1]
                yst = [sb_t(sFF, "yst%d" % i, [128, 512]) for i in range(2)]
                cnt = 0
                for cb in range(2):
                    dma(wd[cb][:], w_d_v[:, :, cb * 512:(cb + 1) * 512], [], ["wd0"], q="pool")
                    for t in range(8):
                        r = 0 if t < 4 else 1
                        pb = 4 + cnt % 2
                        yb = cnt % 2
                        cnt += 1
                        for ft in range(22):
                            mm(bank(pb), actT[:, ft, t * 128:(t + 1) * 128], wd[cb][:, ft, :], ft == 0, ft == 21,
                               ["actT.%d" % (t // 4), "wd0"], [PK[pb]])
                        tt(yst[yb][:], bank(pb), gb2[r][:, cb * 512:(cb + 1) * 512], ALU.mult, [PK[pb], "gb2_%d" % r], ["yst%d" % yb])
                        tt(yst[yb][:], yst[yb][:], x1[:, t, cb * 512:(cb + 1) * 512], ALU.add, ["yst%d" % yb, x1keys[t]], ["yst%d" % yb])
                        dst = y_p[t * 128:(t + 1) * 128, cb * 512:(cb + 1) * 512] if t < 4 else \
                            y_s[(t - 4) * 128:(t - 3) * 128, cb * 512:(cb + 1) * 512]
                        finals.append(dma(dst, yst[yb][:], ["yst%d" % yb], []))
        P.emit(final_wait_ids=[f for f in finals if f is not None])
    return nc, P


def _consts(flip):
    c = np.zeros((128, NCON), np.float32)
    r = np.arange(128)
    c[:, CO_ID:CO_ID + 128] = np.eye(128, dtype=np.float32)
    c[:, CO_U:CO_U + 128] = (r[:, None] <= r[None, :])
    c[:, CO_LW:CO_LW + 128] = (r[:, None] >= r[None, :])
    c[:, CO_SL:CO_SL + 128] = (r[:, None] > r[None, :])
    c[:, CO_SU:CO_SU + 128] = (r[:, None] < r[None, :])
    c[:, CO_ONE:CO_ONE + 128] = 1.0
    tpos = np.arange(1024)
    if flip:
        tpos = 1023 - tpos
    row = (tpos // 64).astype(np.float32)
    col = (tpos % 64).astype(np.float32)
    inv = (10000.0 ** (-np.arange(0, 32, 2, dtype=np.float32) / 32)).astype(np.float32)
    ang_r = row[:, None] * inv[None, :]
    ang_c = col[:, None] * inv[None, :]
    cos = np.concatenate([np.cos(ang_r), np.cos(ang_c)], axis=1).astype(np.float32)
    sin = np.concatenate([np.sin(ang_r), np.sin(ang_c)], axis=1).astype(np.float32)
    c[:, CO_COS:CO_COS + 256] = cos.reshape(8, 128, 32).transpose(1, 0, 2).reshape(128, 256)
    c[:, CO_SIN:CO_SIN + 256] = sin.reshape(8, 128, 32).transpose(1, 0, 2).reshape(128, 256)
    return c


def _prep_inputs(inp):
    f = lambda a: np.ascontiguousarray(np.asarray(a, dtype=np.float32))
    x_prompt, x_sample = f(inp["x_prompt"]), f(inp["x_sample"])
    c, c_ctx = f(inp["c"]), f(inp["c_ctx"])
    cache_k, cache_v = f(inp["cache_k"]), f(inp["cache_v"])
    s_f, s_b = f(inp["state_ssm_fwd"]), f(inp["state_ssm_bwd"])
    w_in = f(inp["w_in"])[0]
    conv_w = f(inp["conv_w"])[0]
    conv_b = f(inp["conv_b"])[0]
    A_log = f(inp["A_log"])[0]
    dt_bias = f(inp["dt_bias"])[0]
    shared = dict(
        w_ada=f(inp["w_ada"])[0], b_ada2=np.ascontiguousarray(np.broadcast_to(f(inp["b_ada"])[0][None, :], (2, 6 * D))),
        w_in=w_in, w_a=f(inp["w_branch_a"])[0], w_b=f(inp["w_branch_b"])[0], w_o=f(inp["w_out"])[0],
        w_g=f(inp["w_ffn_gate"])[0], w_u=f(inp["w_ffn_up"])[0], w_d=f(inp["w_ffn_down"])[0],
        ssmg=np.ascontiguousarray(np.broadcast_to(f(inp["ssm_norm_g"])[0][None, :], (128, 2048))),
    )
    dtcols = w_in[:, C_DT:C_DT + 64]
    maps = []
    for core in range(8):
        j, flip = core // 2, (core % 2 == 1)
        xs = x_sample[j][::-1] if flip else x_sample[j]
        x_all = np.concatenate([x_prompt[2 * core], x_prompt[2 * core + 1], xs], axis=0)
        cond = np.stack([c_ctx, c[j]], axis=0)
        condT = cond.reshape(2, 8, 128).transpose(2, 1, 0).reshape(128, 16)
        ckT = cache_k[j, 0].transpose(2, 3, 1, 0).reshape(128, NH, 512)
        cvv = cache_v[j, 0]
        hf = s_f[j, 0].reshape(2048, 128).T
        hb = s_b[j, 0].reshape(2048, 128).T
        h0T = np.stack([hb, hf] if flip else [hf, hb], axis=1)
        sw = (lambda a: np.concatenate([a[..., 32:64], a[..., 0:32]], axis=-1)) if flip else (lambda a: a)
        w_dt = np.concatenate([dtcols, sw(dtcols)], axis=1)
        pv = np.zeros((128, NPV), np.float32)
        pv[:, PO_G1:PO_G1 + 8] = f(inp["norm1_g"])[0].reshape(8, 128).T
        pv[:, PO_G2:PO_G2 + 8] = f(inp["norm2_g"])[0].reshape(8, 128).T
        cwp = conv_w.reshape(5, 32, 128).transpose(2, 1, 0)
        pv[:, PO_CWP:PO_CWP + 160] = cwp.reshape(128, 160)
        pv[:, PO_CWS:PO_CWS + 160] = (cwp[:, :, ::-1] if flip else cwp).reshape(128, 160)
        pv[:, PO_CB:PO_CB + 32] = conv_b.reshape(32, 128).T
        pv[:, PO_SSG:PO_SSG + 16] = f(inp["ssm_norm_g"])[0].reshape(16, 128).T
        pv[:, PO_SUBG] = f(inp["attn_sub_g"])[0]
        bv = np.zeros((128, NBV), np.float32)
        bv[:, BO_QG:BO_QG + 64] = f(inp["q_norm_g"])[0][None]
        bv[:, BO_KG:BO_KG + 64] = f(inp["k_norm_g"])[0][None]
        bv[:, BO_SUB:BO_SUB + 128] = f(inp["attn_sub_g"])[0][None]
        bv[:, BO_DSK:BO_DSK + 32] = f(inp["D_skip"])[0][None]
        dtb = dt_bias.reshape(64)
        alg = A_log.reshape(64)
        bv[:, BO_DTBP:BO_DTBP + 64] = dtb[None]
        bv[:, BO_DTBS:BO_DTBS + 64] = sw(dtb)[None]
        bv[:, BO_ALP:BO_ALP + 64] = alg[None]
        bv[:, BO_ALS:BO_ALS + 64] = sw(alg)[None]
        for k, nm in enumerate(("lambda_q1", "lambda_k1", "lambda_q2", "lambda_k2")):
            bv[:, BO_L + 64 * k:BO_L + 64 * (k + 1)] = f(inp[nm])[0][None]
        m = dict(shared)
        m.update(x_all=np.ascontiguousarray(x_all), condT=np.ascontiguousarray(condT), ckT=np.ascontiguousarray(ckT),
                 cv=np.ascontiguousarray(cvv), h0T=np.ascontiguousarray(h0T), w_dt=np.ascontiguousarray(w_dt),
                 pvec=pv, bvec=bv, consts=_consts(flip))
        maps.append(m)
    return maps


def _assemble(results):
    y_prompt = np.zeros((16, 256, D), np.float32)
    y_sample = np.zeros((4, 1024, D), np.float32)
    nck = np.zeros((16, 1, 256, NH, 2, 64), np.float32)
    ncv = np.zeros((16, 1, 256, NH, 128), np.float32)
    nsf = np.zeros((16, 1, 32, 64, 128), np.float32)
    nsb = np.zeros((16, 1, 32, 64, 128), np.float32)
    for core in range(8):
        r = results[core]
        j, flip = core // 2, (core % 2 == 1)
        y_prompt[2 * core:2 * core + 2] = r["y_p"].reshape(2, 256, D)
        if flip:
            y_sample[j, 512:1024] = r["y_s"][::-1]
        else:
            y_sample[j, 0:512] = r["y_s"]
        nck[2 * core:2 * core + 2, 0] = r["nk"].reshape(2, 256, NH, 2, 64)
        ncv[2 * core:2 * core + 2, 0] = r["nv"].reshape(2, 256, NH, 128)
        nsf[2 * core:2 * core + 2, 0] = r["sf"].reshape(2, 32, 64, 128)
        nsb[2 * core:2 * core + 2, 0] = r["sb"].reshape(2, 32, 64, 128)
    return (y_prompt, y_sample, nck, ncv, nsf, nsb)


_CACHE = {}


def kernel(**inputs):
    if "nc" not in _CACHE:
        _CACHE["nc"] = build_program()[0]
    nc = _CACHE["nc"]
    maps = _prep_inputs(inputs)
    res = run_bass_kernel_spmd(nc, maps, core_ids=list(range(8)))
    return _assemble(res.results)
```

```python
import math
import contextlib
import numpy as np
import concourse.bass as bass
import concourse.mybir as mybir
from concourse.bass_utils import run_bass_kernel_spmd

F32 = mybir.dt.float32
BF16 = mybir.dt.bfloat16
AF = mybir.ActivationFunctionType
ALU = mybir.AluOpType
AX = mybir.AxisListType

D = 1024
NH = 8
DFF = 2816
DIN = 11328
EPS = 1e-6
LAM_INIT = 0.8 - 0.6 * math.exp(-0.3 * 0)
C_Q, C_K, C_V, C_Z, C_X, C_B, C_C, C_DT, C_GA, C_GB = 0, 1024, 2048, 3072, 5120, 7168, 8192, 9216, 9280, 10304

CO_ID, CO_U, CO_LW, CO_SL, CO_SU, CO_ONE, CO_COS, CO_SIN, CO_SEL = 0, 128, 256, 384, 512, 640, 768, 1024, 1280
NCON = 1280
BO_QG, BO_KG, BO_SUB, BO_DSK, BO_DTBP, BO_DTBS, BO_ALP, BO_ALS, BO_L = 0, 64, 128, 256, 288, 352, 416, 480, 544
NBV = 544 + 256
PO_G1, PO_G2, PO_CWP, PO_CWS, PO_CB, PO_SSG, PO_SUBG = 0, 8, 16, 176, 336, 368, 384
NPV = 385


class Prog:
    def __init__(self, nc):
        self.nc = nc
        self.ins = []
        self.last_w = {}
        self.readers = {}

    enabled = True
    tag = ""
    barrier_id = None
    barrier_from = 0

    def barrier(self, fn):
        if not self.enabled:
            return
        deps = {}
        last = {}
        for i in range(self.barrier_from, len(self.ins)):
            I = self.ins[i]
            if I["dma"]:
                deps[i] = 2
            else:
                last[I["eng"]] = i
        for i in last.values():
            deps[i] = 2
        iid = self.op("dve", fn)
        self.ins[iid]["deps"].update(deps)
        self.barrier_id = iid
        self.barrier_from = iid

    def op(self, eng, fn, reads=(), writes=(), dma=False):
        if not self.enabled:
            return None
        iid = len(self.ins)
        deps = {}
        if self.barrier_id is not None:
            deps[self.barrier_id] = 2
        for b in reads:
            w = self.last_w.get(b)
            if w is not None:
                deps[w] = 2
            if b.startswith("ps") and eng != "pe":
                for r in self.readers.get(b, ()):
                    if self.ins[r]["eng"] != eng:
                        deps[r] = max(deps.get(r, 0), 1)
        for b in writes:
            w = self.last_w.get(b)
            if w is not None:
                deps[w] = max(deps.get(w, 0), 1)
            for r in self.readers.get(b, ()):
                deps.setdefault(r, 0)
        self.ins.append(dict(eng=eng, fn=fn, deps=deps, dma=dma, tag=self.tag))
        for b in writes:
            self.last_w[b] = iid
            self.readers[b] = []
        for b in reads:
            if b not in writes:
                lst = self.readers.setdefault(b, [])
                if not dma:
                    lst[:] = [r for r in lst if self.ins[r]["dma"] or self.ins[r]["eng"] != eng]
                lst.append(iid)
        return iid

    def _need(self, I, Dd, true_dep):
        if I["dma"] or Dd["dma"]:
            return True
        if I["eng"] != Dd["eng"]:
            return True
        if I["eng"] == "pe":
            return False
        return True

    def emit(self, final_wait_ids=()):
        nc = self.nc
        ins = self.ins
        NDMA = 8
        dma_rr = {}
        prev_on_sem = {}
        dma_key = {}
        for i, I in enumerate(ins):
            if I["dma"]:
                k = dma_rr.get(I["eng"], 0)
                dma_rr[I["eng"]] = k + 1
                key = ("dma", I["eng"], k % NDMA)
                dma_key[i] = key
                if key in prev_on_sem:
                    I["deps"][prev_on_sem[key]] = 2
                prev_on_sem[key] = i
        needed = set(final_wait_ids)
        for i, I in enumerate(ins):
            for d, td in I["deps"].items():
                if self._need(I, ins[d], td):
                    needed.add(d)
        cnt = {}
        sig = {}
        for i, I in enumerate(ins):
            if I["dma"]:
                key = dma_key[i]
                cnt[key] = cnt.get(key, 0) + 16
                sig[i] = (key, cnt[key])
            elif i in needed:
                key = ("c", I["eng"])
                cnt[key] = cnt.get(key, 0) + 1
                sig[i] = (key, cnt[key])
        keys = sorted(set(k for k, _ in sig.values()), key=str)
        self.stats = dict(n_ins=len(ins), n_sig=len(sig), cnt={str(k): v for k, v in cnt.items()})
        with contextlib.ExitStack() as es:
            sems = {k: es.enter_context(nc.semaphore("s_" + "_".join(map(str, k)))) for k in keys}
            block = es.enter_context(nc.Block())
            per_eng = {}
            for i, I in enumerate(ins):
                per_eng.setdefault(I["eng"], []).append(i)
            nwaits = [0]

            def run_engine(ename, e):
                known = {}
                for i in per_eng.get(ename, []):
                    I = ins[i]
                    for d in sorted(I["deps"]):
                        if not self._need(I, ins[d], I["deps"][d]):
                            continue
                        key, val = sig[d]
                        if known.get(key, 0) >= val:
                            continue
                        e.wait_ge(sems[key], val)
                        nwaits[0] += 1
                        known[key] = val
                    r = I["fn"](e)
                    if i in sig:
                        key, val = sig[i]
                        r.then_inc(sems[key], 16 if I["dma"] else 1)
                if ename == "sp":
                    for d in final_wait_ids:
                        key, val = sig[d]
                        if known.get(key, 0) >= val:
                            continue
                        e.wait_ge(sems[key], val)
                        known[key] = val

            @block.sync
            def _(e):
                run_engine("sp", e)

            @block.gpsimd
            def _(e):
                run_engine("pool", e)

            @block.tensor
            def _(e):
                run_engine("pe", e)

            @block.vector
            def _(e):
                run_engine("dve", e)

            @block.scalar
            def _(e):
                run_engine("act", e)
            self.stats["n_waits"] = nwaits[0]


def build_program(debug=None, stop=None):
    nc = bass.Bass("TRN2", target_bir_lowering=False)
    debug = debug or {}

    def din(name, shape):
        return nc.dram_tensor(name, list(shape), F32, kind="ExternalInput").ap()

    def dout(name, shape):
        return nc.dram_tensor(name, list(shape), F32, kind="ExternalOutput").ap()

    x_all = din("x_all", [1536, D])
    condT = din("condT", [128, 16])
    ckT = din("ckT", [128, NH, 512])
    cv = din("cv", [512, NH, 128])
    h0T = din("h0T", [128, 2, 2048])
    w_ada = din("w_ada", [D, 6 * D])
    b_ada2 = din("b_ada2", [2, 6 * D])
    w_in = din("w_in", [D, DIN])
    w_dt = din("w_dt", [D, 128])
    w_a = din("w_a", [D, D])
    w_b = din("w_b", [2 * D, D])
    w_o = din("w_o", [D, D])
    w_g = din("w_g", [D, DFF])
    w_u = din("w_u", [D, DFF])
    w_d = din("w_d", [DFF, D])
    pvec_d = din("pvec", [128, NPV])
    bvec_d = din("bvec", [128, NBV])
    ssmg_d = din("ssmg", [128, 2048])
    consts_d = din("consts", [128, NCON])

    y_p = dout("y_p", [512, D])
    y_s = dout("y_s", [512, D])
    nk = dout("nk", [512, D])
    nv = dout("nv", [512, D])
    sf = dout("sf", [2, 2048, 128])
    sb = dout("sb", [2, 2048, 128])
    BF_DBG = ("hT", "KT", "QT", "Vaug", "OT", "xtok", "BTt", "CTt", "Btok", "YT", "mT", "actT")
    dbg_out = {k: nc.dram_tensor("dbg_" + k, list(shp), BF16 if k in BF_DBG else F32, kind="ExternalOutput").ap()
               for k, shp in debug.items()}

    P = Prog(nc)
    finals = []
    uid = [0]

    def U_():
        uid[0] += 1
        return uid[0]

    def sb_t(es, name, shape, dt=F32):
        return es.enter_context(nc.sbuf_tensor("sb_" + name, list(shape), dt))

    def mm(out, lhsT, rhs, start, stop, r, w):
        return P.op("pe", lambda e: e.matmul(out, lhsT=lhsT, rhs=rhs, start=start, stop=stop), reads=r, writes=w)

    def tr(out, in_, ident, r, w):
        return P.op("pe", lambda e: e.transpose(out=out, in_=in_, identity=ident), reads=r, writes=w)

    def act(out, in_, func, r, w, scale=1.0, bias=None, accum=None):
        kw = dict(scale=scale)
        if bias is not None:
            kw["bias"] = bias
        if accum is not None:
            kw["accum_out"] = accum
        return P.op("act", lambda e: e.activation(out=out, in_=in_, func=func, **kw), reads=r, writes=w)

    def tt(out, in0, in1, op, r, w, eng="dve"):
        return P.op(eng, lambda e: e.tensor_tensor(out=out, in0=in0, in1=in1, op=op), reads=r, writes=w)

    def ts(out, in0, s1, s2, op0, op1, r, w, eng="dve"):
        if op1 is None:
            return P.op(eng, lambda e: e.tensor_scalar(out=out, in0=in0, scalar1=s1, scalar2=None, op0=op0),
                        reads=r, writes=w)
        return P.op(eng, lambda e: e.tensor_scalar(out=out, in0=in0, scalar1=s1, scalar2=s2, op0=op0, op1=op1),
                    reads=r, writes=w)

    def stt(out, in0, scalar, in1, op0, op1, r, w):
        return P.op("dve", lambda e: e.scalar_tensor_tensor(out=out, in0=in0, scalar=scalar, in1=in1, op0=op0, op1=op1),
                    reads=r, writes=w)

    def cp(out, in_, r, w, eng="dve"):
        if eng == "act":
            return P.op("act", lambda e: e.copy(out=out, in_=in_), reads=r, writes=w)
        return P.op(eng, lambda e: e.tensor_copy(out=out, in_=in_), reads=r, writes=w)

    def dma(out, in_, r, w, q="sp"):
        return P.op(q, lambda e: e.dma_start(out=out, in_=in_), reads=r, writes=w, dma=True)

    def memset(ap, val, w, eng="dve"):
        return P.op(eng, lambda e: e.memset(ap, val), writes=w)

    def dbg(name, ap_sb, r):
        if name in dbg_out and P.enabled:
            finals.append(dma(dbg_out[name], ap_sb, r, []))

    def phase_end(name):
        if stop == name:
            P.enabled = False

    def rstd_from_ss(ss_ap, n, out_ap, tmp_ap, key_in, key_tmp, key_out, inv_n):
        ts(tmp_ap, ss_ap, inv_n, EPS, ALU.mult, ALU.add, [key_in], [key_tmp])
        act(tmp_ap, tmp_ap, AF.Ln, [key_tmp], [key_tmp])
        act(out_ap, tmp_ap, AF.Exp, [key_tmp], [key_out], scale=-0.5)

    with contextlib.ExitStack() as L0:
        psA = L0.enter_context(nc.psum_tensor("psA", [128, 2048], F32))
        psB = L0.enter_context(nc.psum_tensor("psB", [128, 2048], F32))

        def bank(i):
            t = psA if i < 4 else psB
            j = i % 4
            return t[:, j * 512:(j + 1) * 512]

        PK = ["ps%d" % i for i in range(8)]

        consts = sb_t(L0, "consts", [128, NCON])
        pvec = sb_t(L0, "pvec", [128, NPV])
        bvec = sb_t(L0, "bvec", [128, NBV])
        identb = sb_t(L0, "identb", [128, 128], BF16)
        AB = sb_t(L0, "AB", [128, 6, 8, 2])
        lamt = sb_t(L0, "lamt", [128, 4])
        hT = sb_t(L0, "hT", [128, 8, 1536], BF16)
        bar_t = sb_t(L0, "bar_t", [128, 2])

        def barrier():
            P.barrier(lambda e: e.memset(bar_t[:], 0.0))

        dma(consts[:], consts_d, [], ["consts"])
        dma(pvec[:], pvec_d, [], ["pvec"])
        dma(bvec[:], bvec_d, [], ["bvec"])
        ident = consts[:, CO_ID:CO_ID + 128]
        mU = consts[:, CO_U:CO_U + 128]
        mLW = consts[:, CO_LW:CO_LW + 128]
        mSL = consts[:, CO_SL:CO_SL + 128]
        mSU = consts[:, CO_SU:CO_SU + 128]
        ones = consts[:, CO_ONE:CO_ONE + 128]
        cp(identb[:], ident, ["consts"], ["identb"])

        hkeys = ["hT.%d" % t for t in range(12)]

        def norm_mod_to_hT(src_tile_fn, ntiles, rsel, Aidx, dst, dst_keys, es, tagp, hook=None):
            xn = [sb_t(es, "%s_xn%d" % (tagp, i), [128, D]) for i in range(2)]
            junk = sb_t(es, tagp + "_junk", [128, D])
            st = [sb_t(es, "%s_st%d" % (tagp, i), [128, 4]) for i in range(2)]
            for t in range(ntiles):
                if hook is not None:
                    hook(t)
                xt, xkey = src_tile_fn(t)
                b = t % 2
                sk = "%s_st%d" % (tagp, b)
                act(junk[:], xt, AF.Square, [xkey], [tagp + "_junk", sk + "a"], accum=st[b][:, 0:1])
                rstd_from_ss(st[b][:, 0:1], 1, st[b][:, 2:3], st[b][:, 1:2], sk + "a", sk + "b", sk + "c", 1.0 / D)
                xk = "%s_xn%d" % (tagp, b)
                ts(xn[b][:], xt, st[b][:, 2:3], None, ALU.mult, None, [xkey, sk + "c"], [xk])
                r = rsel(t)
                for half in range(2):
                    pb = 6 + half
                    for q in range(4):
                        kt = half * 4 + q
                        tr(bank(pb)[:, q * 128:(q + 1) * 128], xn[b][:, kt * 128:(kt + 1) * 128], ident,
                           [xk, "consts"], [PK[pb]])
                    for q in range(4):
                        kt = half * 4 + q
                        act(dst[:, kt, t * 128:(t + 1) * 128], bank(pb)[:, q * 128:(q + 1) * 128], AF.Identity,
                            [PK[pb], "AB"], [dst_keys[t]],
                            scale=AB[:, Aidx, kt, r:r + 1], bias=AB[:, Aidx + 1, kt, r:r + 1])

        with contextlib.ExitStack() as sA:
            mod = sb_t(sA, "mod", [2, 6 * D])
            bada = sb_t(sA, "bada", [2, 6 * D])
            cT = sb_t(sA, "cT", [128, 16])
            scT = sb_t(sA, "scT", [128, 16], BF16)
            wada = [sb_t(sA, "wada%d" % i, [128, 8, 512], BF16) for i in range(3)]
            ltmp = sb_t(sA, "ltmp", [128, 64])
            dma(cT[:], condT, [], ["cT"])
            dma(bada[:], b_ada2, [], ["bada"])
            act(scT[:], cT[:], AF.Silu, ["cT"], ["scT"])
            w_ada_v = w_ada.rearrange("(kt p) c -> p kt c", p=128)
            def ada_cb(cb):
                wb = cb % 3
                dma(wada[wb][:], w_ada_v[:, :, cb * 512:(cb + 1) * 512], [], ["wada%d" % wb], q="pool")
                pb = cb % 2
                for kt in range(8):
                    mm(bank(pb)[0:2, :], scT[:, kt * 2:kt * 2 + 2], wada[wb][:, kt, :], kt == 0, kt == 7,
                       ["scT", "wada%d" % wb], [PK[pb]])
                tt(mod[:, cb * 512:(cb + 1) * 512], bank(pb)[0:2, :], bada[:, cb * 512:(cb + 1) * 512], ALU.add,
                   [PK[pb], "bada"], ["mod.%d" % (cb // 2)])

            mT4 = bank(2)[:, 0:96].rearrange("p (s k r) -> p s k r", s=6, k=8)

            def ada_sections(sis):
                secs = (0, 1, 3, 4, 2, 5)
                for si in sis:
                    sec = secs[si]
                    for kt in range(8):
                        c0 = (si * 8 + kt) * 2
                        tr(bank(2)[:, c0:c0 + 2], mod[0:2, sec * D + kt * 128: sec * D + (kt + 1) * 128], ident[0:2, 0:2],
                           ["mod.%d" % sec, "consts"], [PK[2]])

            def ada_AB(j):
                po_g, s_shift, s_scale = ((PO_G1, 0, 1), (PO_G2, 2, 3))[j]
                gv = pvec[:, po_g:po_g + 8].unsqueeze(2).to_broadcast([128, 8, 2])
                ts(AB[:, 2 * j], mT4[:, s_scale], 1.0, None, ALU.add, None, [PK[2]], ["AB"])
                tt(AB[:, 2 * j], AB[:, 2 * j], gv, ALU.mult, ["AB", "pvec"], ["AB"])
                cp(AB[:, 2 * j + 1], mT4[:, s_shift], [PK[2]], ["AB"])

            for cb in range(4):
                ada_cb(cb)
            ada_sections([0, 1])
            ada_AB(0)
            phase_end("A2")
            for j in range(2):
                tt(ltmp[:], bvec[:, BO_L + 128 * j:BO_L + 128 * j + 64], bvec[:, BO_L + 128 * j + 64:BO_L + 128 * j + 128],
                   ALU.mult, ["bvec"], ["ltmp"])
                P.op("dve", lambda e, j=j: e.tensor_reduce(out=lamt[:, 2 + j:3 + j], in_=ltmp[:], axis=AX.X, op=ALU.add),
                     reads=["ltmp"], writes=["lamt"])
            act(lamt[:, 2:4], lamt[:, 2:4], AF.Exp, ["lamt"], ["lamt"])
            tt(lamt[:, 0:1], lamt[:, 2:3], lamt[:, 3:4], ALU.subtract, ["lamt"], ["lamt"])
            ts(lamt[:, 0:1], lamt[:, 0:1], LAM_INIT, None, ALU.add, None, ["lamt"], ["lamt"])
            ts(lamt[:, 1:2], lamt[:, 0:1], -1.0, None, ALU.mult, None, ["lamt"], ["lamt"])
            phase_end("A")

            xts = [sb_t(sA, "xt%d" % i, [128, D]) for i in range(3)]

            def src_tile(t):
                b = t % 3
                dma(xts[b][:], x_all[t * 128:(t + 1) * 128, :], [], ["xt%d" % b])
                return xts[b][:], "xt%d" % b

            def ada_hook(t):
                if t < 8:
                    ada_cb(4 + t)

            norm_mod_to_hT(src_tile, 12, lambda t: 0 if t < 4 else 1, 0, hT, hkeys, sA, "nB", hook=ada_hook)
            ada_sections([2, 3, 4, 5])
            ada_AB(1)
            cp(AB[:, 4], mT4[:, 4], [PK[2]], ["AB"])
            cp(AB[:, 5], mT4[:, 5], [PK[2]], ["AB"])
        dbg("hT", hT[:], hkeys)
        phase_end("B")

        barrier()
        with contextlib.ExitStack() as L1:
            OT = sb_t(L1, "OT", [128, NH, 1024], BF16)
            with contextlib.ExitStack() as sC:
                Vaug = sb_t(sC, "Vaug", [128, 16, NH, 130], BF16)
                KT = sb_t(sC, "KT", [128, NH, 2048], BF16)
                QT = sb_t(sC, "QT", [128, NH, 1024], BF16)
                vkeys = ["V.%d" % t for t in range(16)]
                kkeys = ["KT.%d" % t for t in range(16)]
                qkeys = ["QT.%d" % t for t in range(8)]
                import os
                KD = os.environ.get("KDBG", "")
                if "nomemset" not in KD:
                    memset(Vaug[:, :, :, 128:129], 1.0, vkeys)
                if "nock" not in KD:
                    dma(KT[:, :, 1536:2048], ckT, [], kkeys[12:16], q="pool")
                if "nocv" not in KD:
                    for t in range(4):
                        dma(Vaug[:, 12 + t, :, 0:128], cv[t * 128:(t + 1) * 128, :, :], [], [vkeys[12 + t]], q="pool")
                w_in_v = w_in.rearrange("(kt p) c -> p kt c", p=128)
                with contextlib.ExitStack() as sW:
                    wq = [sb_t(sW, "wqkv%d" % i, [128, 8, 512], BF16) for i in range(3)]
                    wcnt = [0]

                    def wnext(c0):
                        b = wcnt[0] % 3
                        wcnt[0] += 1
                        dma(wq[b][:], w_in_v[:, :, c0:c0 + 512], [], ["wqkv%d" % b], q="pool")
                        return wq[b], "wqkv%d" % b

                    vst = [sb_t(sW, "vst%d" % i, [128, 512]) for i in range(2)]
                    for cb in range(2):
                        wt, wk_ = wnext(C_V + cb * 512)
                        for t in range(12):
                            pb = t % 2
                            for kt in range(8):
                                mm(bank(pb), hT[:, kt, t * 128:(t + 1) * 128], wt[:, kt, :], kt == 0, kt == 7,
                                   [hkeys[t], wk_], [PK[pb]])
                            if "noact" not in KD:
                                act(Vaug[:, t, 4 * cb:4 * cb + 4, 0:128],
                                    bank(pb).rearrange("p (h e) -> p h e", h=4), AF.Identity, [PK[pb]], [vkeys[t]])
                            if t < 4 and "nonv" not in KD:
                                vb = (cb * 4 + t) % 2
                                cp(vst[vb][:], bank(pb), [PK[pb], vkeys[t]], ["vst%d" % vb])
                                finals.append(dma(nv[t * 128:(t + 1) * 128, cb * 512:(cb + 1) * 512], vst[vb][:],
                                                  ["vst%d" % vb], []))
                    phase_end("C1")
                    sq = sb_t(sW, "sq", [128, D])
                    kn = [sb_t(sW, "kn%d" % i, [128, D]) for i in range(2)]
                    kr = [sb_t(sW, "kr%d" % i, [128, D]) for i in range(2)]
                    rt = sb_t(sW, "rt", [128, 2, 512])
                    st16 = sb_t(sW, "st16", [128, 2, 16])

                    sq2 = [sq, sb_t(sW, "sq_b", [128, D])]
                    st16b = sb_t(sW, "st16_b", [128, 2, 16])
                    st2 = [st16, st16b]

                    def qk_section(col0, tiles, gofs, dstT, dkeys, colfn, is_k):
                        wts = [wnext(col0), wnext(col0 + 512)]
                        tiles = list(tiles)

                        def emit_proj(i):
                            t = tiles[i]
                            for cb in range(2):
                                pb = 2 + 2 * (i % 2) + cb
                                for kt in range(8):
                                    mm(bank(pb), hT[:, kt, t * 128:(t + 1) * 128], wts[cb][0][:, kt, :], kt == 0, kt == 7,
                                       [hkeys[t], wts[cb][1]], [PK[pb]])

                        def emit_rest(i):
                            t = tiles[i]
                            par = i % 2
                            psq = psA[:, 1024:2048] if par == 0 else psB[:, 0:1024]
                            pk2 = [PK[2 + 2 * par], PK[3 + 2 * par]]
                            b = i % 2
                            sqb, sqk = sq2[b], "sq%d" % b
                            stb, sk = st2[b], "st16_%d" % b
                            act(sqb[:], psq, AF.Square, pk2, [sqk])
                            P.op("dve", lambda e: e.tensor_reduce(out=stb[:, 0, :], in_=sqb[:].rearrange("p (g d) -> p g d", d=64),
                                                                   axis=AX.X, op=ALU.add), reads=[sqk], writes=[sk + "a"])
                            rstd_from_ss(stb[:, 0, :], 16, stb[:, 1, :], stb[:, 0, :], sk + "a", sk + "a", sk + "b", 1.0 / 64)
                            knk = "kn%d" % b
                            tt(kn[b][:].rearrange("p (g d) -> p g d", d=64), psq.rearrange("p (g d) -> p g d", d=64),
                               stb[:, 1, :].unsqueeze(2).to_broadcast([128, 16, 64]), ALU.mult, pk2 + [sk + "b"], [knk])
                            tt(kn[b][:].rearrange("p (g d) -> p g d", d=64), kn[b][:].rearrange("p (g d) -> p g d", d=64),
                               bvec[:, gofs:gofs + 64].unsqueeze(1).to_broadcast([128, 16, 64]), ALU.mult, [knk, "bvec"], [knk])
                            src, srck = kn[b], knk
                            if t >= 4:
                                rti = t - 4
                                cosv = consts[:, CO_COS + rti * 32:CO_COS + rti * 32 + 32].rearrange("p (a f) -> p a f", a=2) \
                                    .unsqueeze(1).to_broadcast([128, 16, 2, 16])
                                sinv = consts[:, CO_SIN + rti * 32:CO_SIN + rti * 32 + 32].rearrange("p (a f) -> p a f", a=2) \
                                    .unsqueeze(1).to_broadcast([128, 16, 2, 16])
                                x5 = kn[b][:].rearrange("p (g a h f) -> p g a h f", g=16, a=2, h=2)
                                o5 = kr[b][:].rearrange("p (g a h f) -> p g a h f", g=16, a=2, h=2)
                                t5 = rt[:].rearrange("p j (g a f) -> p j g a f", g=16, a=2)
                                krk = "kr%d" % b
                                tt(t5[:, 0], x5[:, :, :, 0, :], cosv, ALU.mult, [knk, "consts"], ["rt0"])
                                tt(t5[:, 1], x5[:, :, :, 1, :], sinv, ALU.mult, [knk, "consts"], ["rt1"])
                                tt(o5[:, :, :, 0, :], t5[:, 0], t5[:, 1], ALU.subtract, ["rt0", "rt1"], [krk])
                                tt(t5[:, 0], x5[:, :, :, 1, :], cosv, ALU.mult, [knk, "consts"], ["rt0"])
                                tt(t5[:, 1], x5[:, :, :, 0, :], sinv, ALU.mult, [knk, "consts"], ["rt1"])
                                tt(o5[:, :, :, 1, :], t5[:, 0], t5[:, 1], ALU.add, ["rt0", "rt1"], [krk])
                                src, srck = kr[b], krk
                            if is_k and t < 4:
                                finals.append(dma(nk[t * 128:(t + 1) * 128, :], src[:], [srck], []))
                            c0 = colfn(t)
                            for half in range(2):
                                pb = 6 + half
                                for q in range(4):
                                    h = half * 4 + q
                                    tr(bank(pb)[:, q * 128:(q + 1) * 128], src[:, h * 128:(h + 1) * 128], ident,
                                       [srck, "consts"], [PK[pb]])
                                act(dstT[:, half * 4:half * 4 + 4, c0:c0 + 128],
                                    bank(pb).rearrange("p (h t) -> p h t", h=4), AF.Identity, [PK[pb]], [dkeys(t)])

                        for step in range(len(tiles) + 1):
                            if step < len(tiles):
                                emit_proj(step)
                            if step >= 1:
                                emit_rest(step - 1)

                    qk_section(C_K, range(12), BO_KG, KT, lambda t: kkeys[t], lambda t: t * 128, True)
                    phase_end("C2")
                    qk_section(C_Q, range(8), BO_QG, QT, lambda t: qkeys[t], lambda t: t * 128, False)
                dbg("KT", KT[:], kkeys)
                dbg("QT", QT[:], qkeys)
                dbg("Vaug", Vaug[:], vkeys)
                phase_end("C")

                barrier()
                with contextlib.ExitStack() as sD:
                    NPT = 8
                    PT = sb_t(sD, "PT", [128, NPT, 512], BF16)
                    onesb = sb_t(sD, "onesb", [128, 128], BF16)
                    lnc = sb_t(sD, "lnc", [128, 1])
                    rc = [sb_t(sD, "rc%d" % i, [128, 512]) for i in range(2)]
                    tq = [sb_t(sD, "tq%d" % i, [128, 512]) for i in range(2)]
                    ocmb = [sb_t(sD, "ocmb%d" % i, [128, 512]) for i in range(2)]
                    sqo = sb_t(sD, "sqo", [128, 512])
                    rso = sb_t(sD, "rso", [128, 512])
                    memset(onesb[:], 1.0, ["onesb"])
                    memset(lnc[:], math.log(1.0 - LAM_INIT), ["lnc"])
                    subgT = pvec[:, PO_SUBG:PO_SUBG + 1]
                    segs = [([0, 1], [0, 1]), ([2, 3], [2, 3]), ([4, 5, 6, 7], list(range(4, 16)))]
                    it = 0
                    ptc = 0
                    for qts, kts in segs:
                        nq = len(qts) * 128
                        q0 = qts[0] * 128
                        qk_ = [qkeys[t] for t in qts]
                        nk_t = len(kts)
                        for h in range(NH):
                            slots = [[], []]

                            def pv(c, ki):
                                bO, bL = 2 + 2 * c, 3 + 2 * c
                                kt = kts[ki]
                                sl = slots[c][ki]
                                mm(bank(bO)[:, 0:nq], Vaug[:, kt, h, 0:128], PT[:, sl, 0:nq], ki == 0, ki == nk_t - 1,
                                   ["PT.%d" % sl, vkeys[kt]], [PK[bO]])
                                mm(bank(bL)[:, 0:nq], onesb[:], PT[:, sl, 0:nq], ki == 0, ki == nk_t - 1,
                                   ["PT.%d" % sl, "onesb"], [PK[bL]])

                            for ki, kt in enumerate(kts):
                                kc0 = kt * 128
                                for c in range(2):
                                    pr = slice(64 * c, 64 * c + 64)
                                    pb = (0 if c == 0 else 6) + ki % 2
                                    sl = ptc % NPT
                                    ptc += 1
                                    slots[c].append(sl)
                                    mm(bank(pb)[:, 0:nq], KT[pr, h, kc0:kc0 + 128], QT[pr, h, q0:q0 + nq], True, True,
                                       [kkeys[kt]] + qk_, [PK[pb]])
                                    act(PT[:, sl, 0:nq], bank(pb)[:, 0:nq], AF.Exp, [PK[pb]], ["PT.%d" % sl], scale=0.125)
                                if ki >= 1:
                                    pv(0, ki - 1)
                                    pv(1, ki - 1)
                            pv(0, nk_t - 1)
                            pv(1, nk_t - 1)
                            fb = it % 2
                            it += 1
                            for c in range(2):
                                bO, bL = 2 + 2 * c, 3 + 2 * c
                                act(rc[c][:, 0:nq], bank(bL)[:, 0:nq], AF.Ln, [PK[bL]], ["rc%d" % c])
                                act(rc[c][:, 0:nq], rc[c][:, 0:nq], AF.Exp, ["rc%d" % c], ["rc%d" % c], scale=-1.0)
                                tt(tq[c][:, 0:nq], bank(bO)[:, 0:nq], rc[c][:, 0:nq], ALU.mult, [PK[bO], "rc%d" % c], ["tq%d" % c])
                            ok_ = "ocmb%d" % fb
                            stt(ocmb[fb][:, 0:nq], tq[1][:, 0:nq], lamt[:, 1:2], tq[0][:, 0:nq], ALU.mult, ALU.add,
                                ["tq0", "tq1", "lamt"], [ok_])
                            act(sqo[:, 0:nq], ocmb[fb][:, 0:nq], AF.Square, [ok_], ["sqo"])
                            pbt = it % 2
                            mm(bank(pbt)[:, 0:nq], ones, sqo[:, 0:nq], True, True, ["consts", "sqo"], [PK[pbt]])
                            ts(rso[:, 0:nq], bank(pbt)[:, 0:nq], 1.0 / 128, EPS, ALU.mult, ALU.add, [PK[pbt]], ["rso"])
                            act(rso[:, 0:nq], rso[:, 0:nq], AF.Ln, ["rso"], ["rso"])
                            act(rso[:, 0:nq], rso[:, 0:nq], AF.Exp, ["rso", "lnc"], ["rso"], scale=-0.5, bias=lnc[:, 0:1])
                            stt(OT[:, h, q0:q0 + nq], ocmb[fb][:, 0:nq], subgT, rso[:, 0:nq], ALU.mult, ALU.mult,
                                [ok_, "pvec", "rso"], ["OT"])
            dbg("OT", OT[:], ["OT"])
            phase_end("D")

            barrier()
            with contextlib.ExitStack() as L1b:
                YT = sb_t(L1b, "YT", [128, 16, 1024], BF16)
                arenaW = sb_t(L1b, "arenaW", [128, 12288], BF16)

                def wview(off, k, c):
                    return arenaW[:, off:off + k * c].rearrange("p (k c) -> p k c", k=k)
                ssacc = sb_t(L1b, "ssacc", [128, 8])
                memset(ssacc[:], 0.0, ["ssacc"])
                w_in_v = w_in.rearrange("(kt p) c -> p kt c", p=128)
                with contextlib.ExitStack() as sF:
                    wdt = sb_t(sF, "wdt", [128, 8, 128], BF16)
                    DT = sb_t(sF, "DT", [128, 12, 64])
                    Aa = sb_t(sF, "Aa", [128, 12, 64])
                    Aneg = sb_t(sF, "Aneg", [128, 128])
                    dma(wdt[:], w_dt.rearrange("(kt p) c -> p kt c", p=128), [], ["wdt"], q="pool")
                    for t in range(12):
                        c0 = 0 if t < 4 else 64
                        pb, po = (0, t * 64) if t < 8 else (1, (t - 8) * 64)
                        for kt in range(8):
                            mm(bank(pb)[:, po:po + 64], hT[:, kt, t * 128:(t + 1) * 128], wdt[:, kt, c0:c0 + 64], kt == 0, kt == 7,
                               [hkeys[t], "wdt"], [PK[pb]])
                    tt(DT[:, 0:4, :], bank(0)[:, 0:256].rearrange("p (t c) -> p t c", t=4),
                       bvec[:, BO_DTBP:BO_DTBP + 64].unsqueeze(1).to_broadcast([128, 4, 64]), ALU.add, [PK[0], "bvec"], ["DT"])
                    tt(DT[:, 4:8, :], bank(0)[:, 256:512].rearrange("p (t c) -> p t c", t=4),
                       bvec[:, BO_DTBS:BO_DTBS + 64].unsqueeze(1).to_broadcast([128, 4, 64]), ALU.add, [PK[0], "bvec"], ["DT"])
                    tt(DT[:, 8:12, :], bank(1)[:, 0:256].rearrange("p (t c) -> p t c", t=4),
                       bvec[:, BO_DTBS:BO_DTBS + 64].unsqueeze(1).to_broadcast([128, 4, 64]), ALU.add, [PK[1], "bvec"], ["DT"])
                    act(DT[:], DT[:], AF.Exp, ["DT"], ["DT"])
                    act(DT[:], DT[:], AF.Ln, ["DT"], ["DT"], bias=ones[:, 0:1])
                    act(Aneg[:], bvec[:, BO_ALP:BO_ALP + 128], AF.Exp, ["bvec"], ["Aneg"])
                    ts(Aneg[:], Aneg[:], -1.0, None, ALU.mult, None, ["Aneg"], ["Aneg"])
                    tt(Aa[:, 0:4, :], DT[:, 0:4, :], Aneg[:, 0:64].unsqueeze(1).to_broadcast([128, 4, 64]), ALU.mult,
                       ["DT", "Aneg"], ["Aa"])
                    tt(Aa[:, 4:12, :], DT[:, 4:12, :], Aneg[:, 64:128].unsqueeze(1).to_broadcast([128, 8, 64]), ALU.mult,
                       ["DT", "Aneg"], ["Aa"])
                    dbg("DT", DT[:], ["DT"])
                    phase_end("E")

                    wx = [wview(0, 8, 256), wview(2048, 8, 256)]
                    wB = [wview(4096, 8, 128), wview(5120, 8, 128)]
                    wC = [wview(6144, 8, 128), wview(7168, 8, 128)]
                    wz = [wview(8192, 8, 256), wview(10240, 8, 256)]
                    AKEYS = ["wx0", "wx1", "wB0", "wB1", "wC0", "wC1", "wz0", "wz1"]
                    rawpad1 = sb_t(sF, "rawpad0", [128, 1548], BF16)
                    dg = sb_t(sF, "dg", [128, 4, 5, 128], BF16)
                    xs = sb_t(sF, "xs", [128, 1536])
                    xtok = sb_t(sF, "xtok", [128, 12, 256], BF16)
                    BTt = sb_t(sF, "BTt", [128, 1536], BF16)
                    CTt = sb_t(sF, "CTt", [128, 1536], BF16)
                    Btok = sb_t(sF, "Btok", [128, 12, 128], BF16)
                    zs = sb_t(sF, "zs", [128, 8, 256])
                    h0g1 = sb_t(sF, "h0g0", [128, 2, 256])
                    h0g = [h0g1, h0g1]
                    a8 = sb_t(sF, "a8", [128, 2, 12, 4])
                    dt8 = sb_t(sF, "dt8", [128, 2, 12, 4])
                    EDT = sb_t(sF, "EDT", [128, 3, 2, 12, 4])
                    DDg = sb_t(sF, "DDg", [128, 2, 12, 4])
                    diagD = sb_t(sF, "diagD", [128, 4, 128], BF16)
                    CBm = [sb_t(sF, "CBm%d" % i, [128, 4, 128], BF16) for i in range(2)]
                    aU1 = sb_t(sF, "aU0", [128, 2, 4, 128])
                    aU = [aU1, aU1]
                    Et = [sb_t(sF, "Et%d" % i, [128, 2, 4, 128], BF16) for i in range(2)]
                    Mt = [sb_t(sF, "Mt%d" % i, [128, 4, 4, 128], BF16) for i in range(2)]
                    xd = [sb_t(sF, "xd%d" % i, [128, 4, 256], BF16) for i in range(2)]
                    xdd = [sb_t(sF, "xdd0", [128, 4, 256], BF16), sb_t(sF, "xdd1", [128, 8, 256], BF16)]
                    hprev = [sb_t(sF, "hprev%d" % i, [128, 4, 256], BF16) for i in range(2)]
                    hs = [[sb_t(sF, "hs%d_%d" % (d, i), [128, 256]) for i in range(2)] for d in range(2)]
                    hzero = sb_t(sF, "hzero", [128, 256])
                    t1 = sb_t(sF, "t1", [128, 4, 256])
                    t2 = sb_t(sF, "t2", [128, 4, 256])
                    sstmp = sb_t(sF, "sstmp", [128, 4])
                    fst1 = sb_t(sF, "fst0", [128, 2, 128])
                    fst = [fst1, fst1]
                    memset(rawpad1[:], 0.0, ["rawpad0"], eng="pool")
                    memset(hzero[:], 0.0, ["hzero"])
                    seg_pad = [(0, 0, 256), (260, 256, 256), (520, 512, 1024)]
                    fcount = [0]
                    rpc = [0]
                    PARTS = ("p", "s")

                    def stageA(g, part):
                        wi = g % 2
                        pn = PARTS[part]
                        if part == 0:
                            dma(wx[wi][:], w_in_v[:, :, C_X + g * 256:C_X + (g + 1) * 256], [], ["wx%d" % wi], q="pool")
                            dma(wB[wi][:], w_in_v[:, :, C_B + g * 128:C_B + (g + 1) * 128], [], ["wB%d" % wi], q="pool")
                            dma(wC[wi][:], w_in_v[:, :, C_C + g * 128:C_C + (g + 1) * 128], [], ["wC%d" % wi], q="pool")
                            dma(wz[wi][:], w_in_v[:, :, C_Z + g * 256:C_Z + (g + 1) * 256], [], ["wz%d" % wi], q="pool")
                        tbs = [0] if part == 0 else [1, 2]
                        segs = seg_pad[0:2] if part == 0 else seg_pad[2:3]
                        chunks = list(range(0, 4)) if part == 0 else list(range(4, 12))
                        tok0 = 0 if part == 0 else 512
                        ntok = 512 if part == 0 else 1024
                        blocks = [(wx[wi], "wx%d" % wi, 0, 2 * g), (wx[wi], "wx%d" % wi, 128, 2 * g + 1),
                                  (wB[wi], "wB%d" % wi, 0, 16 + g), (wC[wi], "wC%d" % wi, 0, 24 + g)]
                        rk = "rawpad0"
                        ak = "acc"
                        xk = "xs"
                        pbank = {}

                        def proj(bi):
                            wt, wk_, wc0, cblk = blocks[bi]
                            for tb in tbs:
                                pb = tb % 2
                                for kt in range(8):
                                    mm(bank(pb), wt[:, kt, wc0:wc0 + 128], hT[:, kt, tb * 512:(tb + 1) * 512], kt == 0, kt == 7,
                                       [wk_] + hkeys[tb * 4:tb * 4 + 4], [PK[pb]])

                        def evac(bi):
                            for tb in tbs:
                                pb = tb % 2
                                if tb == 0:
                                    dst = rawpad1[:, 0:520].rearrange("p (s c) -> p s c", s=2)[:, :, 2:258]
                                    act(dst, bank(pb).rearrange("p (s c) -> p s c", s=2), AF.Identity, [PK[pb]], [rk])
                                else:
                                    o = 522 + (tb - 1) * 512
                                    act(rawpad1[:, o:o + 512], bank(pb), AF.Identity, [PK[pb]], [rk])

                        po_w = PO_CWP if part == 0 else PO_CWS
                        for bi_ in range(4):
                            cblk_ = blocks[bi_][3]
                            for j in range(5):
                                act(dg[:, bi_, j, :], ident, AF.Identity, ["consts", "pvec"], ["dg.%d.%d" % (bi_, j)],
                                    scale=pvec[:, po_w + cblk_ * 5 + j: po_w + cblk_ * 5 + j + 1])
                        subs = [(0, 0, 0, 256), (260, 0, 256, 256)] if part == 0 else [(520, 0, 0, 512), (520 + 512, 1, 0, 512)]

                        def conv(bi):
                            for (pbase, pb, co, n) in subs:
                                for j in range(5):
                                    mm(bank(pb)[:, co:co + n], dg[:, bi, j, :], rawpad1[:, pbase + j:pbase + j + n], j == 0, j == 4,
                                       ["dg.%d.%d" % (bi, j), rk], [PK[pb]])

                        def silu(bi):
                            cblk = blocks[bi][3]
                            bia = pvec[:, PO_CB + cblk:PO_CB + cblk + 1]
                            outs = []
                            if part == 0:
                                outs.append((0, 0, 512, 0))
                            else:
                                outs.append((0, 0, 512, 512))
                                outs.append((1, 0, 512, 1024))
                            for (pb, co, n, tcol) in outs:
                                if bi == 3:
                                    act(CTt[:, tcol:tcol + n], bank(pb)[:, co:co + n], AF.Silu, [PK[pb], "pvec"], ["CTt." + pn], bias=bia)
                                else:
                                    act(xs[:, tcol:tcol + n], bank(pb)[:, co:co + n], AF.Silu, [PK[pb], "pvec"], [xk], bias=bia)
                            if bi == 2:
                                for (pb, co, n, tcol) in outs:
                                    act(BTt[:, tcol:tcol + n], bank(pb)[:, co:co + n], AF.Silu, [PK[pb], "pvec"], ["BTt." + pn], bias=bia)

                        def transposes(bi):
                            if bi == 3:
                                return
                            for q4 in range(0, len(chunks), 4):
                                pb = (q4 // 4) % 2
                                cs = chunks[q4:q4 + 4]
                                for q, t in enumerate(cs):
                                    tr(bank(pb)[:, q * 128:(q + 1) * 128], xs[:, t * 128:(t + 1) * 128], ident,
                                       [xk, "consts"], [PK[pb]])
                                src = bank(pb).rearrange("p (t c) -> p t c", t=4)
                                if bi < 2:
                                    act(xtok[:, cs[0]:cs[0] + 4, bi * 128:(bi + 1) * 128], src, AF.Identity, [PK[pb]], ["xtok." + pn])
                                else:
                                    act(Btok[:, cs[0]:cs[0] + 4, :], src, AF.Identity, [PK[pb]], ["Btok." + pn])

                        for bi in range(4):
                            proj(bi)
                            evac(bi)
                            yield
                            conv(bi)
                            silu(bi)
                            yield
                            transposes(bi)
                            yield
                        for t in ([0, 1, 2, 3] if part == 0 else [4, 5, 6, 7]):
                            pb = t % 2
                            for kt in range(8):
                                mm(bank(pb)[:, 0:256], hT[:, kt, t * 128:(t + 1) * 128], wz[wi][:, kt, :], kt == 0, kt == 7,
                                   [hkeys[t], "wz%d" % wi], [PK[pb]])
                            act(zs[:, t, :], bank(pb)[:, 0:256], AF.Silu, [PK[pb]], ["zs.%d" % t])
                        yield

                    def prepG(g):
                        for d in range(2):
                            hc0 = d * 32 + 4 * g
                            cp(a8[:, d], Aa[:, :, hc0:hc0 + 4], ["Aa"], ["a8"])
                            cp(dt8[:, d], DT[:, :, hc0:hc0 + 4], ["DT"], ["dt8"])
                        for d in range(2):
                            rhs = a8[:, d].rearrange("p c h -> p (c h)")
                            for ki, msk in enumerate(((mU, mLW)[d], (mSL, mSU)[d], ones)):
                                o = (ki * 2 + d) * 48
                                mm(bank(7)[:, o:o + 48], msk, rhs, True, True, ["a8", "consts"], [PK[7]])
                        act(EDT[:].rearrange("p k d c h -> p (k d c h)"), bank(7)[:, 0:288], AF.Exp, [PK[7]], ["EDT"])
                        tt(DDg[:], EDT[:, 1], dt8[:], ALU.mult, ["EDT", "dt8"], ["DDg"])
                        dsk = bvec[:, BO_DSK + 4 * g:BO_DSK + 4 * g + 4]
                        for h in range(4):
                            ts(diagD[:, h, :], ident, dsk[:, h:h + 1], None, ALU.mult, None, ["consts", "bvec"], ["diagD"])

                    def stageB(g, part):
                        wi = g % 2
                        pn = PARTS[part]
                        XK, BTK, CTK, BKK, ZK = "xtok." + pn, "BTt." + pn, "CTt." + pn, "Btok." + pn, "zs." + pn
                        c0 = 0 if part == 0 else 4
                        if part == 0:
                            prepG(g)
                        if part == 0:
                            chains = [(0, None, [0, 1], True, 0), (1, None, [1, 0], True, 0),
                                      (0, None, [2, 3], True, 1), (1, None, [3, 2], True, 1)]
                        else:
                            chains = [(0, h0g1[:, 0, :], [0, 1, 2, 3], False, 0), (1, h0g1[:, 1, :], [7, 6, 5, 4, 3, 2, 1, 0], False, 0)]
                        work = []
                        for chn, (d, init, order, need_final, sqi) in enumerate(chains):
                            for kstep, ci in enumerate(order):
                                work.append((chn, kstep, ci))
                        state = {}
                        for chn, (d, init, order, need_final, sqi) in enumerate(chains):
                            state[chn] = (hzero[:], "hzero") if init is None else (init, "h0g0")
                        pos = [0]

                        def s_round():
                            rnd = []
                            nslot = 0
                            while pos[0] < len(work) and nslot < 8:
                                chn, kstep, ci = work[pos[0]]
                                d, init, order, need_final, sqi = chains[chn]
                                upd = need_final or kstep < len(order) - 1
                                rnd.append((chn, kstep, ci, upd, nslot if upd else None))
                                if upd:
                                    pb = 4 + nslot // 2
                                    so = (nslot % 2) * 256
                                    mm(bank(pb)[:, so:so + 256], Btok[:, c0 + ci, :], xdd[d][:, ci, :], True, True,
                                       [BKK, "xdd%d" % d], [PK[pb]])
                                    nslot += 1
                                pos[0] += 1
                            return rnd

                        def rec_round(rnd):
                            for chn, kstep, ci, upd, sl in rnd:
                                d, init, order, need_final, sqi = chains[chn]
                                cur, curk = state[chn]
                                if ci < 4:
                                    cp(hprev[d][:, ci, :], cur, [curk], ["hprev%d.%d" % (d, ci)])
                                if upd:
                                    pb = 4 + sl // 2
                                    so = (sl % 2) * 256
                                    nxt, nk_ = hs[d][kstep % 2], "hs%d_%d" % (d, kstep % 2)
                                    if init is None and kstep == 0:
                                        cp(nxt[:], bank(pb)[:, so:so + 256], [PK[pb]], [nk_])
                                    else:
                                        tt(nxt[:].rearrange("p (h q) -> p h q", h=4), cur.rearrange("p (h q) -> p h q", h=4),
                                           EDT[:, 2, d, c0 + ci, :].unsqueeze(2).to_broadcast([128, 4, 64]), ALU.mult,
                                           [curk, "EDT"], [nk_])
                                        tt(nxt[:], nxt[:], bank(pb)[:, so:so + 256], ALU.add, [nk_, PK[pb]], [nk_])
                                    cur, curk = nxt[:], nk_
                                    state[chn] = (cur, curk)
                                if need_final and kstep == len(order) - 1:
                                    fcount[0] += 1
                                    pbf = fcount[0] % 2
                                    for j in range(2):
                                        tr(bank(pbf)[:, j * 128:(j + 1) * 128], cur[:, j * 128:(j + 1) * 128], ident,
                                           [curk, "consts"], [PK[pbf]])
                                    cp(fst1[:], bank(pbf)[:, 0:256].rearrange("p (j n) -> p j n", j=2), [PK[pbf]], ["fst0"], eng="act")
                                    dst_o = (sf if d == 0 else sb)[sqi, g * 256:(g + 1) * 256, :].rearrange("(j p) n -> p j n", p=128)
                                    finals.append(dma(dst_o, fst1[:], ["fst0"], []))

                        def dexp(k, d, half):
                            ab = k % 2
                            msk_u = mU if d == 0 else mLW
                            msk_s = mSL if d == 0 else mSU
                            cc = c0 + 2 * half
                            tt(aU1[:], a8[:, d, cc:cc + 2, :].unsqueeze(3).to_broadcast([128, 2, 4, 128]),
                               msk_u.unsqueeze(1).unsqueeze(1).to_broadcast([128, 2, 4, 128]), ALU.mult,
                               ["a8", "consts"], ["aU0"])
                            for k2 in range(2):
                                mm(bank(2 + k2), msk_s, aU1[:, k2].rearrange("p h l -> p (h l)"), True, True,
                                   ["aU0", "consts"], [PK[2 + k2]])
                            act(Et[ab][:].rearrange("p c h l -> p (c h l)"), psA[:, 1024:2048],
                                AF.Exp, [PK[2], PK[3]], ["Et%d" % ab])

                        def mtmul(k, d, half):
                            ab = k % 2
                            tt(Mt[d][:, 2 * half:2 * half + 2], Et[ab][:],
                               CBm[d][:, 2 * half:2 * half + 2, :].unsqueeze(2).to_broadcast([128, 2, 4, 128]),
                               ALU.mult, ["Et%d" % ab, "CBm%d" % d], ["Mt%d" % d])

                        for d in range(2):
                            nch = 8 if (part == 1 and d == 1) else 4
                            tt(xdd[d][:, 0:nch, :].rearrange("p c (h q) -> p c h q", h=4),
                               xtok[:, c0:c0 + nch, :].rearrange("p c (h q) -> p c h q", h=4),
                               DDg[:, d, c0:c0 + nch, :].unsqueeze(3).to_broadcast([128, nch, 4, 64]), ALU.mult,
                               [XK, "DDg"], ["xdd%d" % d])
                        for ci in range(4):
                            c = c0 + ci
                            mm(bank(1)[:, ci * 128:(ci + 1) * 128], BTt[:, c * 128:(c + 1) * 128], CTt[:, c * 128:(c + 1) * 128],
                               True, True, [BTK, CTK], [PK[1]])
                        cb3 = bank(1).rearrange("p (c l) -> p c l", c=4)
                        tt(CBm[0][:], cb3, mU.unsqueeze(1).to_broadcast([128, 4, 128]), ALU.mult, [PK[1], "consts"], ["CBm0"])
                        tt(CBm[1][:], cb3, mLW.unsqueeze(1).to_broadcast([128, 4, 128]), ALU.mult, [PK[1], "consts"], ["CBm1"])
                        rnd1 = s_round()
                        yield
                        seq = [(0, 0), (0, 1), (1, 0), (1, 1)]
                        dexp(0, 0, 0)
                        yield
                        for d in range(2):
                            tt(xd[d][:].rearrange("p c (h q) -> p c h q", h=4),
                               xtok[:, c0:c0 + 4, :].rearrange("p c (h q) -> p c h q", h=4),
                               dt8[:, d, c0:c0 + 4, :].unsqueeze(3).to_broadcast([128, 4, 4, 64]), ALU.mult,
                               [XK, "dt8"], ["xd%d" % d])
                        yield
                        rec_round(rnd1)
                        yield
                        for k in range(4):
                            d, half = seq[k]
                            if k + 1 < 4:
                                dexp(k + 1, *seq[k + 1])
                            mtmul(k, d, half)
                            yield
                        while pos[0] < len(work):
                            rnd = s_round()
                            yield
                            rec_round(rnd)
                            yield
                        for ci in range(4):
                            pb = 6 + ci // 2
                            yo = (ci % 2) * 256
                            for h in range(4):
                                mm(bank(pb)[:, yo + h * 64:yo + (h + 1) * 64], diagD[:, h, :], xtok[:, c0 + ci, h * 64:(h + 1) * 64],
                                   h == 0, False, ["diagD", XK], [PK[pb]])
                            for d in range(2):
                                for h in range(4):
                                    mm(bank(pb)[:, yo + h * 64:yo + (h + 1) * 64], Mt[d][:, ci, h, :], xd[d][:, ci, h * 64:(h + 1) * 64],
                                       False, d == 1 and h == 3, ["Mt%d" % d, "xd%d" % d], [PK[pb]])
                        for d in range(2):
                            for ci in range(4):
                                pb = (2 if d == 0 else 4) + ci // 2
                                yo = (ci % 2) * 256
                                mm(bank(pb)[:, yo:yo + 256], CTt[:, (c0 + ci) * 128:(c0 + ci + 1) * 128], hprev[d][:, ci, :], True, True,
                                   [CTK, "hprev%d.%d" % (d, ci)], [PK[pb]])
                        yield
                        for hb in range(2):
                            for d in range(2):
                                pb = (2 if d == 0 else 4) + hb
                                dstt = t1 if d == 0 else t2
                                tt(dstt[:, 2 * hb:2 * hb + 2, :].rearrange("p c (h q) -> p c h q", h=4),
                                   bank(pb).rearrange("p (c h q) -> p c h q", c=2, h=4),
                                   EDT[:, 0, d, c0 + 2 * hb:c0 + 2 * hb + 2, :].unsqueeze(3).to_broadcast([128, 2, 4, 64]), ALU.mult,
                                   [PK[pb], "EDT"], ["t1" if d == 0 else "t2"])
                        tt(t1[:], t1[:], t2[:], ALU.add, ["t1", "t2"], ["t1"])
                        for hb in range(2):
                            tt(t1[:, 2 * hb:2 * hb + 2, :], t1[:, 2 * hb:2 * hb + 2, :],
                               bank(6 + hb).rearrange("p (c q) -> p c q", c=2), ALU.add, ["t1", PK[6 + hb]], ["t1"])
                        if g == 0 and part == 1:
                            dbg("ysum", t1[:], ["t1"])
                        tt(t2[:], t1[:], zs[:, c0:c0 + 4, :], ALU.mult, ["t1"] + ["zs.%d" % (c0 + i_) for i_ in range(4)], ["t2"])
                        yield
                        for ci in range(4):
                            act(t1[:, ci, :], t2[:, ci, :], AF.Square, ["t2"], ["t1", "sstmp"], accum=sstmp[:, ci:ci + 1])
                        tt(ssacc[:, c0:c0 + 4], ssacc[:, c0:c0 + 4], sstmp[:], ALU.add, ["ssacc", "sstmp"], ["ssacc"])
                        for j in range(2):
                            pb = 2 + j
                            for ci in range(4):
                                tr(bank(pb)[:, ci * 128:(ci + 1) * 128], t2[:, ci, j * 128:(j + 1) * 128], ident, ["t2", "consts"], [PK[pb]])
                            act(YT[:, 2 * g + j, c0 * 128:(c0 + 4) * 128], bank(pb), AF.Identity, [PK[pb], "pvec"], ["YT"],
                                scale=pvec[:, PO_SSG + 2 * g + j:PO_SSG + 2 * g + j + 1])
                        if part == 1 and g < 7:
                            dma(h0g1[:], h0T[:, :, (g + 1) * 256:(g + 2) * 256], [], ["h0g0"])
                        yield

                    def run_interleaved(gens):
                        gens = [[x[0], x[1], 0] for x in gens if x is not None]
                        while gens:
                            for x in list(gens):
                                try:
                                    P.tag = "%s#%d" % (x[0], x[2])
                                    x[2] += 1
                                    next(x[1])
                                except StopIteration:
                                    gens.remove(x)
                        P.tag = ""

                    dma(h0g1[:], h0T[:, :, 0:256], [], ["h0g0"])
                    def SA(g, p):
                        return ("A%d%s" % (g, PARTS[p]), stageA(g, p))

                    def SB(g, p):
                        return ("B%d%s" % (g, PARTS[p]), stageB(g, p))

                    run_interleaved([SA(0, 0)])
                    run_interleaved([SB(0, 0), SA(0, 1)])
                    wgb0_v, wa0_v, wga0_v = wview(0, 8, 512), wview(4096, 8, 512), wview(8192, 8, 512)
                    for g in range(8):
                        if g == 7:
                            dma(wgb0_v, w_in_v[:, :, C_GB:C_GB + 512], [], ["wgb0"] + AKEYS, q="pool")
                            dma(wa0_v, w_a.rearrange("(kt p) c -> p kt c", p=128)[:, :, 0:512], [], ["wa0"] + AKEYS, q="pool")
                            dma(wga0_v, w_in_v[:, :, C_GA:C_GA + 512], [], ["wga0"] + AKEYS, q="pool")
                        run_interleaved([SB(g, 1), SA(g + 1, 0) if g < 7 else None])
                        if g < 7:
                            run_interleaved([SB(g + 1, 0), SA(g + 1, 1)])
                dbg("YT", YT[:], ["YT"])
                dbg("ssacc", ssacc[:], ["ssacc"])
                phase_end("F")

                barrier()
                with contextlib.ExitStack() as sG:
                    wga = [wga0_v, sb_t(sG, "wga1", [128, 8, 512], BF16)]
                    wgb = [wgb0_v, sb_t(sG, "wgb1", [128, 8, 512], BF16)]
                    wa = [wa0_v, sb_t(sG, "wa1", [128, 8, 512], BF16)]
                    wb_ = [sb_t(sG, "wb%d" % i, [128, 16, 512], BF16) for i in range(2)]
                    rsy = sb_t(sG, "rsy", [128, 16])
                    sga = sb_t(sG, "sga", [128, 512])
                    sgb = sb_t(sG, "sgb", [128, 512])
                    mrg = [sb_t(sG, "mrg%d" % i, [128, 512]) for i in range(2)]
                    w_a_v = w_a.rearrange("(kt p) c -> p kt c", p=128)
                    w_b_v = w_b.rearrange("(kt p) c -> p kt c", p=128)
                    dma(wb_[0][:], w_b_v[:, :, 0:512], [], ["wb0"], q="pool")
                    dma(wgb[1][:], w_in_v[:, :, C_GB + 512:C_GB + 1024], [], ["wgb1"], q="pool")
                    dma(wa[1][:], w_a_v[:, :, 512:1024], [], ["wa1"], q="pool")
                    dma(wb_[1][:], w_b_v[:, :, 512:1024], [], ["wb1"], q="pool")
                    dma(wga[1][:], w_in_v[:, :, C_GA + 512:C_GA + 1024], [], ["wga1"], q="pool")
                    rstd_from_ss(ssacc[:], 8, rsy[:, 8:16], rsy[:, 0:8], "ssacc", "rsya", "rsyb", 1.0 / 2048)
                    sga2 = [sga, sb_t(sG, "sga_b", [128, 512])]
                    sgb2 = [sgb, sb_t(sG, "sgb_b", [128, 512])]
                    for t in range(8):
                        for cb in range(2):
                            b0 = 4 * cb
                            sga_, sgb_ = sga2[cb], sgb2[cb]
                            sak, sbk = "sga%d" % cb, "sgb%d" % cb
                            for kt in range(8):
                                mm(bank(b0 + 1), hT[:, kt, t * 128:(t + 1) * 128], wgb[cb][:, kt, :], kt == 0, kt == 7,
                                   [hkeys[t], "wgb%d" % cb], [PK[b0 + 1]])
                            for h in range(8):
                                mm(bank(b0 + 2), OT[:, h, t * 128:(t + 1) * 128], wa[cb][:, h, :], h == 0, h == 7,
                                   ["OT", "wa%d" % cb], [PK[b0 + 2]])
                            for kt in range(16):
                                mm(bank(b0 + 3), YT[:, kt, t * 128:(t + 1) * 128], wb_[cb][:, kt, :], kt == 0, kt == 15,
                                   ["YT", "wb%d" % cb], [PK[b0 + 3]])
                            for kt in range(8):
                                mm(bank(b0), hT[:, kt, t * 128:(t + 1) * 128], wga[cb][:, kt, :], kt == 0, kt == 7,
                                   [hkeys[t], "wga%d" % cb], [PK[b0]])
                            act(sgb_[:], bank(b0 + 1), AF.Sigmoid, [PK[b0 + 1]], [sbk])
                            act(sga_[:], bank(b0), AF.Sigmoid, [PK[b0]], [sak])
                            mk = "mrg%d" % cb
                            stt(sgb_[:], bank(b0 + 3), rsy[:, 8 + t:9 + t], sgb_[:], ALU.mult, ALU.mult, [PK[b0 + 3], "rsyb", sbk], [sbk])
                            tt(mrg[cb][:], sga_[:], bank(b0 + 2), ALU.mult, [sak, PK[b0 + 2]], [mk])
                            tt(mrg[cb][:], mrg[cb][:], sgb_[:], ALU.add, [mk, sbk], [mk])
                            for q in range(4):
                                tr(bank(b0)[:, q * 128:(q + 1) * 128], mrg[cb][:, q * 128:(q + 1) * 128], ident,
                                   [mk, "consts"], [PK[b0]])
                        for cb in range(2):
                            act(hT[:, 4 * cb:4 * cb + 4, t * 128:(t + 1) * 128], bank(4 * cb).rearrange("p (k t) -> p k t", k=4),
                                AF.Identity, [PK[4 * cb]], [hkeys[t]])
        dbg("mT", hT[:], hkeys)
        phase_end("G1")

        barrier()
        with contextlib.ExitStack() as sH:
            x1 = sb_t(sH, "x1", [128, 8, D])
            gb1 = [sb_t(sH, "gb1_%d" % r, [128, D]) for r in range(2)]
            gb2 = [sb_t(sH, "gb2_%d" % r, [128, D]) for r in range(2)]
            dgt = [sb_t(sH, "dgt%d" % i, [128, 128]) for i in range(2)]
            cnt_g = 0
            for r in range(2):
                for gi, gbt in enumerate((gb1, gb2)):
                    for cb in range(2):
                        pb = cnt_g % 2
                        for q in range(4):
                            kt = cb * 4 + q
                            db = cnt_g % 2
                            cnt_g += 1
                            ts(dgt[db][:], ident, AB[:, 4 + gi, kt, r:r + 1], None, ALU.mult, None, ["consts", "AB"], ["dgt%d" % db])
                            mm(bank(pb)[:, q * 128:(q + 1) * 128], ones, dgt[db][:], True, True, ["consts", "dgt%d" % db], [PK[pb]])
                        cp(gbt[r][:, cb * 512:(cb + 1) * 512], bank(pb), [PK[pb]], ["gb%d_%d" % (gi + 1, r)], eng="act")
            x1keys = ["x1.%d" % t for t in range(8)]
            h2T = sb_t(sH, "h2T", [128, 8, 1024], BF16)
            h2keys = ["h2T.%d" % t for t in range(8)]
            with contextlib.ExitStack() as sG2:
                wo = sb_t(sG2, "wo", [128, 8, D], BF16)
                xr = [sb_t(sG2, "xr%d" % i, [128, D]) for i in range(2)]
                tmpg = sb_t(sG2, "tmpg", [128, 512])
                dma(wo[:], w_o.rearrange("(kt p) c -> p kt c", p=128), [], ["wo"], q="pool")

                def g2_tile(t):
                    r = 0 if t < 4 else 1
                    xb = t % 2
                    dma(xr[xb][:], x_all[t * 128:(t + 1) * 128, :], [], ["xr%d" % xb])
                    for cb in range(2):
                        pb = 2 + cb
                        for kt in range(8):
                            mm(bank(pb), hT[:, kt, t * 128:(t + 1) * 128], wo[:, kt, cb * 512:(cb + 1) * 512], kt == 0, kt == 7,
                               [hkeys[t], "wo"], [PK[pb]])
                        tt(tmpg[:], bank(pb), gb1[r][:, cb * 512:(cb + 1) * 512], ALU.mult, [PK[pb], "gb1_%d" % r], ["tmpg"])
                        tt(x1[:, t, cb * 512:(cb + 1) * 512], tmpg[:], xr[xb][:, cb * 512:(cb + 1) * 512], ALU.add,
                           ["tmpg", "xr%d" % xb], [x1keys[t]])

                g2_tile(0)
                norm_mod_to_hT(lambda t: (x1[:, t, :], x1keys[t]), 8, lambda t: 0 if t < 4 else 1, 2, h2T, h2keys, sG2, "nH",
                               hook=lambda t: g2_tile(t + 1) if t + 1 < 8 else None)
            dbg("x1", x1[:], x1keys)
            phase_end("G2")
            barrier()
            with contextlib.ExitStack() as sFF:
                actT = sb_t(sFF, "actT", [128, 22, 1024], BF16)
                wd1 = sb_t(sFF, "wd0", [128, 22, 512], BF16)
                w_d_v = w_d.rearrange("(kt p) c -> p kt c", p=128)
                wgt = [sb_t(sFF, "wgt%d" % i, [128, 8, 128], BF16) for i in range(3)]
                wut = [sb_t(sFF, "wut%d" % i, [128, 8, 128], BF16) for i in range(3)]
                sgl = [sb_t(sFF, "sgl%d" % i, [128, 512]) for i in range(4)]
                w_g_v = w_g.rearrange("(kt p) c -> p kt c", p=128)
                w_u_v = w_u.rearrange("(kt p) c -> p kt c", p=128)
                for ft in range(22):
                    wbi = ft % 3
                    dma(wgt[wbi][:], w_g_v[:, :, ft * 128:(ft + 1) * 128], [], ["wgt%d" % wbi], q="pool")
                    dma(wut[wbi][:], w_u_v[:, :, ft * 128:(ft + 1) * 128], [], ["wut%d" % wbi], q="pool")
                    for th in range(2):
                        pg, pu = 4 * (ft % 2) + 2 * th, 4 * (ft % 2) + 2 * th + 1
                        for kt in range(8):
                            mm(bank(pg), wgt[wbi][:, kt, :], h2T[:, kt, th * 512:(th + 1) * 512], kt == 0, kt == 7,
                               ["wgt%d" % wbi] + h2keys[4 * th:4 * th + 4], [PK[pg]])
                        for kt in range(8):
                            mm(bank(pu), wut[wbi][:, kt, :], h2T[:, kt, th * 512:(th + 1) * 512], kt == 0, kt == 7,
                               ["wut%d" % wbi] + h2keys[4 * th:4 * th + 4], [PK[pu]])
                        sgi = 2 * (ft % 2) + th
                        act(sgl[sgi][:], bank(pg), AF.Silu, [PK[pg]], ["sgl%d" % sgi])
                        tt(actT[:, ft, th * 512:(th + 1) * 512], sgl[sgi][:], bank(pu), ALU.mult, ["sgl%d" % sgi, PK[pu]],
                           ["actT.%d" % th])
                dbg("actT", actT[:], ["actT.0", "actT.1"])
                yst = [sb_t(sFF, "yst%d" % i, [128, 512]) for i in range(2)]
                wdA = h2T[:].rearrange("p k t -> p (k t)").rearrange("p (f c) -> p f c", f=16)
                wdB = sb_t(sFF, "wdB", [128, 6, 512], BF16)
                dma(wd1[:], w_d_v[:, :, 0:512], [], ["wd0"], q="pool")
                dma(wdA, w_d_v[:, 0:16, 512:1024], [], ["wdA"] + h2keys, q="pool")
                dma(wdB[:], w_d_v[:, 16:22, 512:1024], [], ["wdB"], q="pool")
                cnt = 0
                for cb in range(2):
                    for t in range(8):
                        r = 0 if t < 4 else 1
                        pb = 4 + cnt % 2
                        yb = cnt % 2
                        cnt += 1
                        for ft in range(22):
                            if cb == 0:
                                wsl, wkk = wd1[:, ft, :], "wd0"
                            elif ft < 16:
                                wsl, wkk = wdA[:, ft, :], "wdA"
                            else:
                                wsl, wkk = wdB[:, ft - 16, :], "wdB"
                            mm(bank(pb), actT[:, ft, t * 128:(t + 1) * 128], wsl, ft == 0, ft == 21,
                               ["actT.%d" % (t // 4), wkk], [PK[pb]])
                        tt(yst[yb][:], bank(pb), gb2[r][:, cb * 512:(cb + 1) * 512], ALU.mult, [PK[pb], "gb2_%d" % r], ["yst%d" % yb])
                        tt(yst[yb][:], yst[yb][:], x1[:, t, cb * 512:(cb + 1) * 512], ALU.add, ["yst%d" % yb, x1keys[t]], ["yst%d" % yb])
                        dst = y_p[t * 128:(t + 1) * 128, cb * 512:(cb + 1) * 512] if t < 4 else \
                            y_s[(t - 4) * 128:(t - 3) * 128, cb * 512:(cb + 1) * 512]
                        finals.append(dma(dst, yst[yb][:], ["yst%d" % yb], []))
        P.emit(final_wait_ids=[f for f in finals if f is not None])
    return nc, P


def _consts(flip):
    c = np.zeros((128, NCON), np.float32)
    r = np.arange(128)
    c[:, CO_ID:CO_ID + 128] = np.eye(128, dtype=np.float32)
    c[:, CO_U:CO_U + 128] = (r[:, None] <= r[None, :])
    c[:, CO_LW:CO_LW + 128] = (r[:, None] >= r[None, :])
    c[:, CO_SL:CO_SL + 128] = (r[:, None] > r[None, :])
    c[:, CO_SU:CO_SU + 128] = (r[:, None] < r[None, :])
    c[:, CO_ONE:CO_ONE + 128] = 1.0
    tpos = np.arange(1024)
    if flip:
        tpos = 1023 - tpos
    row = (tpos // 64).astype(np.float32)
    col = (tpos % 64).astype(np.float32)
    inv = (10000.0 ** (-np.arange(0, 32, 2, dtype=np.float32) / 32)).astype(np.float32)
    ang_r = row[:, None] * inv[None, :]
    ang_c = col[:, None] * inv[None, :]
    cos = np.concatenate([np.cos(ang_r), np.cos(ang_c)], axis=1).astype(np.float32)
    sin = np.concatenate([np.sin(ang_r), np.sin(ang_c)], axis=1).astype(np.float32)
    c[:, CO_COS:CO_COS + 256] = cos.reshape(8, 128, 32).transpose(1, 0, 2).reshape(128, 256)
    c[:, CO_SIN:CO_SIN + 256] = sin.reshape(8, 128, 32).transpose(1, 0, 2).reshape(128, 256)
    return c


def _prep_inputs(inp):
    f = lambda a: np.ascontiguousarray(np.asarray(a, dtype=np.float32))
    x_prompt, x_sample = f(inp["x_prompt"]), f(inp["x_sample"])
    c, c_ctx = f(inp["c"]), f(inp["c_ctx"])
    cache_k, cache_v = f(inp["cache_k"]), f(inp["cache_v"])
    s_f, s_b = f(inp["state_ssm_fwd"]), f(inp["state_ssm_bwd"])
    w_in = f(inp["w_in"])[0]
    conv_w = f(inp["conv_w"])[0]
    conv_b = f(inp["conv_b"])[0]
    A_log = f(inp["A_log"])[0]
    dt_bias = f(inp["dt_bias"])[0]
    shared = dict(
        w_ada=f(inp["w_ada"])[0], b_ada2=np.ascontiguousarray(np.broadcast_to(f(inp["b_ada"])[0][None, :], (2, 6 * D))),
        w_in=w_in, w_a=f(inp["w_branch_a"])[0], w_b=f(inp["w_branch_b"])[0], w_o=f(inp["w_out"])[0],
        w_g=f(inp["w_ffn_gate"])[0], w_u=f(inp["w_ffn_up"])[0], w_d=f(inp["w_ffn_down"])[0],
        ssmg=np.ascontiguousarray(np.broadcast_to(f(inp["ssm_norm_g"])[0][None, :], (128, 2048))),
    )
    dtcols = w_in[:, C_DT:C_DT + 64]
    maps = []
    for core in range(8):
        j, flip = core // 2, (core % 2 == 1)
        xs = x_sample[j][::-1] if flip else x_sample[j]
        x_all = np.concatenate([x_prompt[2 * core], x_prompt[2 * core + 1], xs], axis=0)
        cond = np.stack([c_ctx, c[j]], axis=0)
        condT = cond.reshape(2, 8, 128).transpose(2, 1, 0).reshape(128, 16)
        ckT = cache_k[j, 0].transpose(2, 3, 1, 0).reshape(128, NH, 512)
        cvv = cache_v[j, 0]
        hf = s_f[j, 0].reshape(2048, 128).T
        hb = s_b[j, 0].reshape(2048, 128).T
        h0T = np.stack([hb, hf] if flip else [hf, hb], axis=1)
        sw = (lambda a: np.concatenate([a[..., 32:64], a[..., 0:32]], axis=-1)) if flip else (lambda a: a)
        w_dt = np.concatenate([dtcols, sw(dtcols)], axis=1)
        pv = np.zeros((128, NPV), np.float32)
        pv[:, PO_G1:PO_G1 + 8] = f(inp["norm1_g"])[0].reshape(8, 128).T
        pv[:, PO_G2:PO_G2 + 8] = f(inp["norm2_g"])[0].reshape(8, 128).T
        cwp = conv_w.reshape(5, 32, 128).transpose(2, 1, 0)
        pv[:, PO_CWP:PO_CWP + 160] = cwp.reshape(128, 160)
        pv[:, PO_CWS:PO_CWS + 160] = (cwp[:, :, ::-1] if flip else cwp).reshape(128, 160)
        pv[:, PO_CB:PO_CB + 32] = conv_b.reshape(32, 128).T
        pv[:, PO_SSG:PO_SSG + 16] = f(inp["ssm_norm_g"])[0].reshape(16, 128).T
        pv[:, PO_SUBG] = f(inp["attn_sub_g"])[0]
        bv = np.zeros((128, NBV), np.float32)
        bv[:, BO_QG:BO_QG + 64] = f(inp["q_norm_g"])[0][None]
        bv[:, BO_KG:BO_KG + 64] = f(inp["k_norm_g"])[0][None]
        bv[:, BO_SUB:BO_SUB + 128] = f(inp["attn_sub_g"])[0][None]
        bv[:, BO_DSK:BO_DSK + 32] = f(inp["D_skip"])[0][None]
        dtb = dt_bias.reshape(64)
        alg = A_log.reshape(64)
        bv[:, BO_DTBP:BO_DTBP + 64] = dtb[None]
        bv[:, BO_DTBS:BO_DTBS + 64] = sw(dtb)[None]
        bv[:, BO_ALP:BO_ALP + 64] = alg[None]
        bv[:, BO_ALS:BO_ALS + 64] = sw(alg)[None]
        for k, nm in enumerate(("lambda_q1", "lambda_k1", "lambda_q2", "lambda_k2")):
            bv[:, BO_L + 64 * k:BO_L + 64 * (k + 1)] = f(inp[nm])[0][None]
        m = dict(shared)
        m.update(x_all=np.ascontiguousarray(x_all), condT=np.ascontiguousarray(condT), ckT=np.ascontiguousarray(ckT),
                 cv=np.ascontiguousarray(cvv), h0T=np.ascontiguousarray(h0T), w_dt=np.ascontiguousarray(w_dt),
                 pvec=pv, bvec=bv, consts=_consts(flip))
        maps.append(m)
    return maps


def _assemble(results):
    y_prompt = np.zeros((16, 256, D), np.float32)
    y_sample = np.zeros((4, 1024, D), np.float32)
    nck = np.zeros((16, 1, 256, NH, 2, 64), np.float32)
    ncv = np.zeros((16, 1, 256, NH, 128), np.float32)
    nsf = np.zeros((16, 1, 32, 64, 128), np.float32)
    nsb = np.zeros((16, 1, 32, 64, 128), np.float32)
    for core in range(8):
        r = results[core]
        j, flip = core // 2, (core % 2 == 1)
        y_prompt[2 * core:2 * core + 2] = r["y_p"].reshape(2, 256, D)
        if flip:
            y_sample[j, 512:1024] = r["y_s"][::-1]
        else:
            y_sample[j, 0:512] = r["y_s"]
        nck[2 * core:2 * core + 2, 0] = r["nk"].reshape(2, 256, NH, 2, 64)
        ncv[2 * core:2 * core + 2, 0] = r["nv"].reshape(2, 256, NH, 128)
        nsf[2 * core:2 * core + 2, 0] = r["sf"].reshape(2, 32, 64, 128)
        nsb[2 * core:2 * core + 2, 0] = r["sb"].reshape(2, 32, 64, 128)
    return (y_prompt, y_sample, nck, ncv, nsf, nsb)


_CACHE = {}


def kernel(**inputs):
    if "nc" not in _CACHE:
        _CACHE["nc"] = build_program()[0]
    nc = _CACHE["nc"]
    maps = _prep_inputs(inputs)
    res = run_bass_kernel_spmd(nc, maps, core_ids=list(range(8)))
    return _assemble(res.results)
```

```python
import math
import contextlib
import numpy as np
import concourse.bass as bass
import concourse.mybir as mybir
from concourse.bass_utils import run_bass_kernel_spmd

F32 = mybir.dt.float32
BF16 = mybir.dt.bfloat16
AF = mybir.ActivationFunctionType
ALU = mybir.AluOpType
AX = mybir.AxisListType

D = 1024
NH = 8
DFF = 2816
DIN = 11328
EPS = 1e-6
LAM_INIT = 0.8 - 0.6 * math.exp(-0.3 * 0)
C_Q, C_K, C_V, C_Z, C_X, C_B, C_C, C_DT, C_GA, C_GB = 0, 1024, 2048, 3072, 5120, 7168, 8192, 9216, 9280, 10304

CO_ID, CO_U, CO_LW, CO_SL, CO_SU, CO_ONE, CO_COS, CO_SIN, CO_SEL = 0, 128, 256, 384, 512, 640, 768, 1024, 1280
NCON = 1280
BO_QG, BO_KG, BO_SUB, BO_DSK, BO_DTBP, BO_DTBS, BO_ALP, BO_ALS, BO_L = 0, 64, 128, 256, 288, 352, 416, 480, 544
NBV = 544 + 256
PO_G1, PO_G2, PO_CWP, PO_CWS, PO_CB, PO_SSG, PO_SUBG = 0, 8, 16, 176, 336, 368, 384
NPV = 385


class Prog:
    def __init__(self, nc):
        self.nc = nc
        self.ins = []
        self.last_w = {}
        self.readers = {}

    enabled = True
    tag = ""
    barrier_id = None
    barrier_from = 0

    def barrier(self, fn):
        if not self.enabled:
            return
        deps = {}
        last = {}
        for i in range(self.barrier_from, len(self.ins)):
            I = self.ins[i]
            if I["dma"]:
                deps[i] = 2
            else:
                last[I["eng"]] = i
        for i in last.values():
            deps[i] = 2
        iid = self.op("dve", fn)
        self.ins[iid]["deps"].update(deps)
        self.barrier_id = iid
        self.barrier_from = iid

    def op(self, eng, fn, reads=(), writes=(), dma=False):
        if not self.enabled:
            return None
        iid = len(self.ins)
        deps = {}
        if self.barrier_id is not None:
            deps[self.barrier_id] = 2
        for b in reads:
            w = self.last_w.get(b)
            if w is not None:
                deps[w] = 2
            if b.startswith("ps") and eng != "pe":
                for r in self.readers.get(b, ()):
                    if self.ins[r]["eng"] != eng:
                        deps[r] = max(deps.get(r, 0), 1)
        for b in writes:
            w = self.last_w.get(b)
            if w is not None:
                deps[w] = max(deps.get(w, 0), 1)
            for r in self.readers.get(b, ()):
                deps.setdefault(r, 0)
        self.ins.append(dict(eng=eng, fn=fn, deps=deps, dma=dma, tag=self.tag))
        for b in writes:
            self.last_w[b] = iid
            self.readers[b] = []
        for b in reads:
            if b not in writes:
                lst = self.readers.setdefault(b, [])
                if not dma:
                    lst[:] = [r for r in lst if self.ins[r]["dma"] or self.ins[r]["eng"] != eng]
                lst.append(iid)
        return iid

    def _need(self, I, Dd, true_dep):
        if I["dma"] or Dd["dma"]:
            return True
        if I["eng"] != Dd["eng"]:
            return True
        if I["eng"] == "pe":
            return False
        return True

    def emit(self, final_wait_ids=()):
        nc = self.nc
        ins = self.ins
        NDMA = 8
        dma_rr = {}
        prev_on_sem = {}
        dma_key = {}
        for i, I in enumerate(ins):
            if I["dma"]:
                k = dma_rr.get(I["eng"], 0)
                dma_rr[I["eng"]] = k + 1
                key = ("dma", I["eng"], k % NDMA)
                dma_key[i] = key
                if key in prev_on_sem:
                    I["deps"][prev_on_sem[key]] = 2
                prev_on_sem[key] = i
        needed = set(final_wait_ids)
        for i, I in enumerate(ins):
            for d, td in I["deps"].items():
                if self._need(I, ins[d], td):
                    needed.add(d)
        cnt = {}
        sig = {}
        for i, I in enumerate(ins):
            if I["dma"]:
                key = dma_key[i]
                cnt[key] = cnt.get(key, 0) + 16
                sig[i] = (key, cnt[key])
            elif i in needed:
                key = ("c", I["eng"])
                cnt[key] = cnt.get(key, 0) + 1
                sig[i] = (key, cnt[key])
        keys = sorted(set(k for k, _ in sig.values()), key=str)
        self.stats = dict(n_ins=len(ins), n_sig=len(sig), cnt={str(k): v for k, v in cnt.items()})
        with contextlib.ExitStack() as es:
            sems = {k: es.enter_context(nc.semaphore("s_" + "_".join(map(str, k)))) for k in keys}
            block = es.enter_context(nc.Block())
            per_eng = {}
            for i, I in enumerate(ins):
                per_eng.setdefault(I["eng"], []).append(i)
            nwaits = [0]

            def run_engine(ename, e):
                known = {}
                for i in per_eng.get(ename, []):
                    I = ins[i]
                    for d in sorted(I["deps"]):
                        if not self._need(I, ins[d], I["deps"][d]):
                            continue
                        key, val = sig[d]
                        if known.get(key, 0) >= val:
                            continue
                        e.wait_ge(sems[key], val)
                        nwaits[0] += 1
                        known[key] = val
                    r = I["fn"](e)
                    if i in sig:
                        key, val = sig[i]
                        r.then_inc(sems[key], 16 if I["dma"] else 1)
                if ename == "sp":
                    for d in final_wait_ids:
                        key, val = sig[d]
                        if known.get(key, 0) >= val:
                            continue
                        e.wait_ge(sems[key], val)
                        known[key] = val

            @block.sync
            def _(e):
                run_engine("sp", e)

            @block.gpsimd
            def _(e):
                run_engine("pool", e)

            @block.tensor
            def _(e):
                run_engine("pe", e)

            @block.vector
            def _(e):
                run_engine("dve", e)

            @block.scalar
            def _(e):
                run_engine("act", e)
            self.stats["n_waits"] = nwaits[0]


def build_program(debug=None, stop=None):
    nc = bass.Bass("TRN2", target_bir_lowering=False)
    debug = debug or {}

    def din(name, shape):
        return nc.dram_tensor(name, list(shape), F32, kind="ExternalInput").ap()

    def dout(name, shape):
        return nc.dram_tensor(name, list(shape), F32, kind="ExternalOutput").ap()

    x_all = din("x_all", [1536, D])
    condT = din("condT", [128, 16])
    ckT = din("ckT", [128, NH, 512])
    cv = din("cv", [512, NH, 128])
    h0T = din("h0T", [128, 2, 2048])
    w_ada = din("w_ada", [D, 6 * D])
    b_ada2 = din("b_ada2", [2, 6 * D])
    w_in = din("w_in", [D, DIN])
    w_dt = din("w_dt", [D, 128])
    w_a = din("w_a", [D, D])
    w_b = din("w_b", [2 * D, D])
    w_o = din("w_o", [D, D])
    w_g = din("w_g", [D, DFF])
    w_u = din("w_u", [D, DFF])
    w_d = din("w_d", [DFF, D])
    pvec_d = din("pvec", [128, NPV])
    bvec_d = din("bvec", [128, NBV])
    ssmg_d = din("ssmg", [128, 2048])
    consts_d = din("consts", [128, NCON])

    y_p = dout("y_p", [512, D])
    y_s = dout("y_s", [512, D])
    nk = dout("nk", [512, D])
    nv = dout("nv", [512, D])
    sf = dout("sf", [2, 2048, 128])
    sb = dout("sb", [2, 2048, 128])
    BF_DBG = ("hT", "KT", "QT", "Vaug", "OT", "xtok", "BTt", "CTt", "Btok", "YT", "mT", "actT")
    dbg_out = {k: nc.dram_tensor("dbg_" + k, list(shp), BF16 if k in BF_DBG else F32, kind="ExternalOutput").ap()
               for k, shp in debug.items()}

    P = Prog(nc)
    finals = []
    uid = [0]

    def U_():
        uid[0] += 1
        return uid[0]

    def sb_t(es, name, shape, dt=F32):
        return es.enter_context(nc.sbuf_tensor("sb_" + name, list(shape), dt))

    def mm(out, lhsT, rhs, start, stop, r, w):
        return P.op("pe", lambda e: e.matmul(out, lhsT=lhsT, rhs=rhs, start=start, stop=stop), reads=r, writes=w)

    def tr(out, in_, ident, r, w):
        return P.op("pe", lambda e: e.transpose(out=out, in_=in_, identity=ident), reads=r, writes=w)

    def act(out, in_, func, r, w, scale=1.0, bias=None, accum=None):
        kw = dict(scale=scale)
        if bias is not None:
            kw["bias"] = bias
        if accum is not None:
            kw["accum_out"] = accum
        return P.op("act", lambda e: e.activation(out=out, in_=in_, func=func, **kw), reads=r, writes=w)

    def tt(out, in0, in1, op, r, w, eng="dve"):
        return P.op(eng, lambda e: e.tensor_tensor(out=out, in0=in0, in1=in1, op=op), reads=r, writes=w)

    def ts(out, in0, s1, s2, op0, op1, r, w, eng="dve"):
        if op1 is None:
            return P.op(eng, lambda e: e.tensor_scalar(out=out, in0=in0, scalar1=s1, scalar2=None, op0=op0),
                        reads=r, writes=w)
        return P.op(eng, lambda e: e.tensor_scalar(out=out, in0=in0, scalar1=s1, scalar2=s2, op0=op0, op1=op1),
                    reads=r, writes=w)

    def stt(out, in0, scalar, in1, op0, op1, r, w):
        return P.op("dve", lambda e: e.scalar_tensor_tensor(out=out, in0=in0, scalar=scalar, in1=in1, op0=op0, op1=op1),
                    reads=r, writes=w)

    def cp(out, in_, r, w, eng="dve"):
        if eng == "act":
            return P.op("act", lambda e: e.copy(out=out, in_=in_), reads=r, writes=w)
        return P.op(eng, lambda e: e.tensor_copy(out=out, in_=in_), reads=r, writes=w)

    def dma(out, in_, r, w, q="sp"):
        return P.op(q, lambda e: e.dma_start(out=out, in_=in_), reads=r, writes=w, dma=True)

    def memset(ap, val, w, eng="dve"):
        return P.op(eng, lambda e: e.memset(ap, val), writes=w)

    def dbg(name, ap_sb, r):
        if name in dbg_out and P.enabled:
            finals.append(dma(dbg_out[name], ap_sb, r, []))

    def phase_end(name):
        if stop == name:
            P.enabled = False

    def rstd_from_ss(ss_ap, n, out_ap, tmp_ap, key_in, key_tmp, key_out, inv_n):
        ts(tmp_ap, ss_ap, inv_n, EPS, ALU.mult, ALU.add, [key_in], [key_tmp])
        act(tmp_ap, tmp_ap, AF.Ln, [key_tmp], [key_tmp])
        act(out_ap, tmp_ap, AF.Exp, [key_tmp], [key_out], scale=-0.5)

    with contextlib.ExitStack() as L0:
        psA = L0.enter_context(nc.psum_tensor("psA", [128, 2048], F32))
        psB = L0.enter_context(nc.psum_tensor("psB", [128, 2048], F32))

        def bank(i):
            t = psA if i < 4 else psB
            j = i % 4
            return t[:, j * 512:(j + 1) * 512]

        PK = ["ps%d" % i for i in range(8)]

        consts = sb_t(L0, "consts", [128, NCON])
        pvec = sb_t(L0, "pvec", [128, NPV])
        bvec = sb_t(L0, "bvec", [128, NBV])
        identb = sb_t(L0, "identb", [128, 128], BF16)
        AB = sb_t(L0, "AB", [128, 6, 8, 2])
        lamt = sb_t(L0, "lamt", [128, 4])
        hT = sb_t(L0, "hT", [128, 8, 1536], BF16)
        bar_t = sb_t(L0, "bar_t", [128, 2])

        def barrier():
            P.barrier(lambda e: e.memset(bar_t[:], 0.0))

        dma(consts[:], consts_d, [], ["consts"])
        dma(pvec[:], pvec_d, [], ["pvec"])
        dma(bvec[:], bvec_d, [], ["bvec"])
        ident = consts[:, CO_ID:CO_ID + 128]
        mU = consts[:, CO_U:CO_U + 128]
        mLW = consts[:, CO_LW:CO_LW + 128]
        mSL = consts[:, CO_SL:CO_SL + 128]
        mSU = consts[:, CO_SU:CO_SU + 128]
        ones = consts[:, CO_ONE:CO_ONE + 128]
        cp(identb[:], ident, ["consts"], ["identb"])

        hkeys = ["hT.%d" % t for t in range(12)]

        def norm_mod_to_hT(src_tile_fn, ntiles, rsel, Aidx, dst, dst_keys, es, tagp, hook=None):
            xn = [sb_t(es, "%s_xn%d" % (tagp, i), [128, D]) for i in range(2)]
            junk = sb_t(es, tagp + "_junk", [128, D])
            st = [sb_t(es, "%s_st%d" % (tagp, i), [128, 4]) for i in range(2)]
            for t in range(ntiles):
                if hook is not None:
                    hook(t)
                xt, xkey = src_tile_fn(t)
                b = t % 2
                sk = "%s_st%d" % (tagp, b)
                act(junk[:], xt, AF.Square, [xkey], [tagp + "_junk", sk + "a"], accum=st[b][:, 0:1])
                rstd_from_ss(st[b][:, 0:1], 1, st[b][:, 2:3], st[b][:, 1:2], sk + "a", sk + "b", sk + "c", 1.0 / D)
                xk = "%s_xn%d" % (tagp, b)
                ts(xn[b][:], xt, st[b][:, 2:3], None, ALU.mult, None, [xkey, sk + "c"], [xk])
                r = rsel(t)
                for half in range(2):
                    pb = 6 + half
                    for q in range(4):
                        kt = half * 4 + q
                        tr(bank(pb)[:, q * 128:(q + 1) * 128], xn[b][:, kt * 128:(kt + 1) * 128], ident,
                           [xk, "consts"], [PK[pb]])
                    for q in range(4):
                        kt = half * 4 + q
                        act(dst[:, kt, t * 128:(t + 1) * 128], bank(pb)[:, q * 128:(q + 1) * 128], AF.Identity,
                            [PK[pb], "AB"], [dst_keys[t]],
                            scale=AB[:, Aidx, kt, r:r + 1], bias=AB[:, Aidx + 1, kt, r:r + 1])

        with contextlib.ExitStack() as sA:
            mod = sb_t(sA, "mod", [2, 6 * D])
            bada = sb_t(sA, "bada", [2, 6 * D])
            cT = sb_t(sA, "cT", [128, 16])
            scT = sb_t(sA, "scT", [128, 16], BF16)
            wada = [sb_t(sA, "wada%d" % i, [128, 8, 512], BF16) for i in range(3)]
            ltmp = sb_t(sA, "ltmp", [128, 64])
            dma(cT[:], condT, [], ["cT"])
            dma(bada[:], b_ada2, [], ["bada"])
            act(scT[:], cT[:], AF.Silu, ["cT"], ["scT"])
            w_ada_v = w_ada.rearrange("(kt p) c -> p kt c", p=128)
            def ada_cb(cb):
                wb = cb % 3
                dma(wada[wb][:], w_ada_v[:, :, cb * 512:(cb + 1) * 512], [], ["wada%d" % wb], q="pool")
                pb = cb % 2
                for kt in range(8):
                    mm(bank(pb)[0:2, :], scT[:, kt * 2:kt * 2 + 2], wada[wb][:, kt, :], kt == 0, kt == 7,
                       ["scT", "wada%d" % wb], [PK[pb]])
                tt(mod[:, cb * 512:(cb + 1) * 512], bank(pb)[0:2, :], bada[:, cb * 512:(cb + 1) * 512], ALU.add,
                   [PK[pb], "bada"], ["mod.%d" % (cb // 2)])

            mT4 = bank(2)[:, 0:96].rearrange("p (s k r) -> p s k r", s=6, k=8)

            def ada_sections(sis):
                secs = (0, 1, 3, 4, 2, 5)
                for si in sis:
                    sec = secs[si]
                    for kt in range(8):
                        c0 = (si * 8 + kt) * 2
                        tr(bank(2)[:, c0:c0 + 2], mod[0:2, sec * D + kt * 128: sec * D + (kt + 1) * 128], ident[0:2, 0:2],
                           ["mod.%d" % sec, "consts"], [PK[2]])

            def ada_AB(j):
                po_g, s_shift, s_scale = ((PO_G1, 0, 1), (PO_G2, 2, 3))[j]
                gv = pvec[:, po_g:po_g + 8].unsqueeze(2).to_broadcast([128, 8, 2])
                ts(AB[:, 2 * j], mT4[:, s_scale], 1.0, None, ALU.add, None, [PK[2]], ["AB"])
                tt(AB[:, 2 * j], AB[:, 2 * j], gv, ALU.mult, ["AB", "pvec"], ["AB"])
                cp(AB[:, 2 * j + 1], mT4[:, s_shift], [PK[2]], ["AB"])

            for cb in range(4):
                ada_cb(cb)
            ada_sections([0, 1])
            ada_AB(0)
            phase_end("A2")
            for j in range(2):
                tt(ltmp[:], bvec[:, BO_L + 128 * j:BO_L + 128 * j + 64], bvec[:, BO_L + 128 * j + 64:BO_L + 128 * j + 128],
                   ALU.mult, ["bvec"], ["ltmp"])
                P.op("dve", lambda e, j=j: e.tensor_reduce(out=lamt[:, 2 + j:3 + j], in_=ltmp[:], axis=AX.X, op=ALU.add),
                     reads=["ltmp"], writes=["lamt"])
            act(lamt[:, 2:4], lamt[:, 2:4], AF.Exp, ["lamt"], ["lamt"])
            tt(lamt[:, 0:1], lamt[:, 2:3], lamt[:, 3:4], ALU.subtract, ["lamt"], ["lamt"])
            ts(lamt[:, 0:1], lamt[:, 0:1], LAM_INIT, None, ALU.add, None, ["lamt"], ["lamt"])
            ts(lamt[:, 1:2], lamt[:, 0:1], -1.0, None, ALU.mult, None, ["lamt"], ["lamt"])
            phase_end("A")

            xts = [sb_t(sA, "xt%d" % i, [128, D]) for i in range(3)]

            def src_tile(t):
                b = t % 3
                dma(xts[b][:], x_all[t * 128:(t + 1) * 128, :], [], ["xt%d" % b])
                return xts[b][:], "xt%d" % b

            def ada_hook(t):
                if t < 8:
                    ada_cb(4 + t)

            norm_mod_to_hT(src_tile, 12, lambda t: 0 if t < 4 else 1, 0, hT, hkeys, sA, "nB", hook=ada_hook)
            ada_sections([2, 3, 4, 5])
            ada_AB(1)
            cp(AB[:, 4], mT4[:, 4], [PK[2]], ["AB"])
            cp(AB[:, 5], mT4[:, 5], [PK[2]], ["AB"])
        dbg("hT", hT[:], hkeys)
        phase_end("B")

        barrier()
        with contextlib.ExitStack() as L1:
            OT = sb_t(L1, "OT", [128, NH, 1024], BF16)
            with contextlib.ExitStack() as sC:
                Vaug = sb_t(sC, "Vaug", [128, 16, NH, 130], BF16)
                KT = sb_t(sC, "KT", [128, NH, 2048], BF16)
                QT = sb_t(sC, "QT", [128, NH, 1024], BF16)
                vkeys = ["V.%d" % t for t in range(16)]
                kkeys = ["KT.%d" % t for t in range(16)]
                qkeys = ["QT.%d" % t for t in range(8)]
                import os
                KD = os.environ.get("KDBG", "")
                if "nomemset" not in KD:
                    memset(Vaug[:, :, :, 128:129], 1.0, vkeys)
                if "nock" not in KD:
                    dma(KT[:, :, 1536:2048], ckT, [], kkeys[12:16], q="pool")
                if "nocv" not in KD:
                    for t in range(4):
                        dma(Vaug[:, 12 + t, :, 0:128], cv[t * 128:(t + 1) * 128, :, :], [], [vkeys[12 + t]], q="pool")
                w_in_v = w_in.rearrange("(kt p) c -> p kt c", p=128)
                with contextlib.ExitStack() as sW:
                    wq = [sb_t(sW, "wqkv%d" % i, [128, 8, 512], BF16) for i in range(3)]
                    wcnt = [0]

                    def wnext(c0):
                        b = wcnt[0] % 3
                        wcnt[0] += 1
                        dma(wq[b][:], w_in_v[:, :, c0:c0 + 512], [], ["wqkv%d" % b], q="pool")
                        return wq[b], "wqkv%d" % b

                    vst = [sb_t(sW, "vst%d" % i, [128, 512]) for i in range(2)]
                    for cb in range(2):
                        wt, wk_ = wnext(C_V + cb * 512)
                        for t in range(12):
                            pb = t % 2
                            for kt in range(8):
                                mm(bank(pb), hT[:, kt, t * 128:(t + 1) * 128], wt[:, kt, :], kt == 0, kt == 7,
                                   [hkeys[t], wk_], [PK[pb]])
                            if "noact" not in KD:
                                act(Vaug[:, t, 4 * cb:4 * cb + 4, 0:128],
                                    bank(pb).rearrange("p (h e) -> p h e", h=4), AF.Identity, [PK[pb]], [vkeys[t]])
                            if t < 4 and "nonv" not in KD:
                                vb = (cb * 4 + t) % 2
                                cp(vst[vb][:], bank(pb), [PK[pb], vkeys[t]], ["vst%d" % vb])
                                finals.append(dma(nv[t * 128:(t + 1) * 128, cb * 512:(cb + 1) * 512], vst[vb][:],
                                                  ["vst%d" % vb], []))
                    phase_end("C1")
                    sq = sb_t(sW, "sq", [128, D])
                    kn = [sb_t(sW, "kn%d" % i, [128, D]) for i in range(2)]
                    kr = [sb_t(sW, "kr%d" % i, [128, D]) for i in range(2)]
                    rt = sb_t(sW, "rt", [128, 2, 512])
                    st16 = sb_t(sW, "st16", [128, 2, 16])

                    sq2 = [sq, sb_t(sW, "sq_b", [128, D])]
                    st16b = sb_t(sW, "st16_b", [128, 2, 16])
                    st2 = [st16, st16b]

                    def qk_section(col0, tiles, gofs, dstT, dkeys, colfn, is_k):
                        wts = [wnext(col0), wnext(col0 + 512)]
                        tiles = list(tiles)

                        def emit_proj(i):
                            t = tiles[i]
                            for cb in range(2):
                                pb = 2 + 2 * (i % 2) + cb
                                for kt in range(8):
                                    mm(bank(pb), hT[:, kt, t * 128:(t + 1) * 128], wts[cb][0][:, kt, :], kt == 0, kt == 7,
                                       [hkeys[t], wts[cb][1]], [PK[pb]])

                        def emit_rest(i):
                            t = tiles[i]
                            par = i % 2
                            psq = psA[:, 1024:2048] if par == 0 else psB[:, 0:1024]
                            pk2 = [PK[2 + 2 * par], PK[3 + 2 * par]]
                            b = i % 2
                            sqb, sqk = sq2[b], "sq%d" % b
                            stb, sk = st2[b], "st16_%d" % b
                            act(sqb[:], psq, AF.Square, pk2, [sqk])
                            P.op("dve", lambda e: e.tensor_reduce(out=stb[:, 0, :], in_=sqb[:].rearrange("p (g d) -> p g d", d=64),
                                                                   axis=AX.X, op=ALU.add), reads=[sqk], writes=[sk + "a"])
                            rstd_from_ss(stb[:, 0, :], 16, stb[:, 1, :], stb[:, 0, :], sk + "a", sk + "a", sk + "b", 1.0 / 64)
                            knk = "kn%d" % b
                            tt(kn[b][:].rearrange("p (g d) -> p g d", d=64), psq.rearrange("p (g d) -> p g d", d=64),
                               stb[:, 1, :].unsqueeze(2).to_broadcast([128, 16, 64]), ALU.mult, pk2 + [sk + "b"], [knk])
                            tt(kn[b][:].rearrange("p (g d) -> p g d", d=64), kn[b][:].rearrange("p (g d) -> p g d", d=64),
                               bvec[:, gofs:gofs + 64].unsqueeze(1).to_broadcast([128, 16, 64]), ALU.mult, [knk, "bvec"], [knk])
                            src, srck = kn[b], knk
                            if t >= 4:
                                rti = t - 4
                                cosv = consts[:, CO_COS + rti * 32:CO_COS + rti * 32 + 32].rearrange("p (a f) -> p a f", a=2) \
                                    .unsqueeze(1).to_broadcast([128, 16, 2, 16])
                                sinv = consts[:, CO_SIN + rti * 32:CO_SIN + rti * 32 + 32].rearrange("p (a f) -> p a f", a=2) \
                                    .unsqueeze(1).to_broadcast([128, 16, 2, 16])
                                x5 = kn[b][:].rearrange("p (g a h f) -> p g a h f", g=16, a=2, h=2)
                                o5 = kr[b][:].rearrange("p (g a h f) -> p g a h f", g=16, a=2, h=2)
                                t5 = rt[:].rearrange("p j (g a f) -> p j g a f", g=16, a=2)
                                krk = "kr%d" % b
                                tt(t5[:, 0], x5[:, :, :, 0, :], cosv, ALU.mult, [knk, "consts"], ["rt0"])
                                tt(t5[:, 1], x5[:, :, :, 1, :], sinv, ALU.mult, [knk, "consts"], ["rt1"])
                                tt(o5[:, :, :, 0, :], t5[:, 0], t5[:, 1], ALU.subtract, ["rt0", "rt1"], [krk])
                                tt(t5[:, 0], x5[:, :, :, 1, :], cosv, ALU.mult, [knk, "consts"], ["rt0"])
                                tt(t5[:, 1], x5[:, :, :, 0, :], sinv, ALU.mult, [knk, "consts"], ["rt1"])
                                tt(o5[:, :, :, 1, :], t5[:, 0], t5[:, 1], ALU.add, ["rt0", "rt1"], [krk])
                                src, srck = kr[b], krk
                            if is_k and t < 4:
                                finals.append(dma(nk[t * 128:(t + 1) * 128, :], src[:], [srck], []))
                            c0 = colfn(t)
                            for half in range(2):
                                pb = 6 + half
                                for q in range(4):
                                    h = half * 4 + q
                                    tr(bank(pb)[:, q * 128:(q + 1) * 128], src[:, h * 128:(h + 1) * 128], ident,
                                       [srck, "consts"], [PK[pb]])
                                act(dstT[:, half * 4:half * 4 + 4, c0:c0 + 128],
                                    bank(pb).rearrange("p (h t) -> p h t", h=4), AF.Identity, [PK[pb]], [dkeys(t)])

                        for step in range(len(tiles) + 1):
                            if step < len(tiles):
                                emit_proj(step)
                            if step >= 1:
                                emit_rest(step - 1)

                    qk_section(C_K, range(12), BO_KG, KT, lambda t: kkeys[t], lambda t: t * 128, True)
                    phase_end("C2")
                    qk_section(C_Q, range(8), BO_QG, QT, lambda t: qkeys[t], lambda t: t * 128, False)
                dbg("KT", KT[:], kkeys)
                dbg("QT", QT[:], qkeys)
                dbg("Vaug", Vaug[:], vkeys)
                phase_end("C")

                barrier()
                with contextlib.ExitStack() as sD:
                    NPT = 8
                    PT = sb_t(sD, "PT", [128, NPT, 512], BF16)
                    onesb = sb_t(sD, "onesb", [128, 128], BF16)
                    lnc = sb_t(sD, "lnc", [128, 1])
                    rc = [sb_t(sD, "rc%d" % i, [128, 512]) for i in range(2)]
                    tq = [sb_t(sD, "tq%d" % i, [128, 512]) for i in range(2)]
                    ocmb = [sb_t(sD, "ocmb%d" % i, [128, 512]) for i in range(2)]
                    sqo = sb_t(sD, "sqo", [128, 512])
                    rso = sb_t(sD, "rso", [128, 512])
                    memset(onesb[:], 1.0, ["onesb"])
                    memset(lnc[:], math.log(1.0 - LAM_INIT), ["lnc"])
                    subgT = pvec[:, PO_SUBG:PO_SUBG + 1]
                    segs = [([0, 1], [0, 1]), ([2, 3], [2, 3]), ([4, 5, 6, 7], list(range(4, 16)))]
                    it = 0
                    ptc = 0
                    for qts, kts in segs:
                        nq = len(qts) * 128
                        q0 = qts[0] * 128
                        qk_ = [qkeys[t] for t in qts]
                        nk_t = len(kts)
                        for h in range(NH):
                            slots = [[], []]

                            def pv(c, ki):
                                bO, bL = 2 + 2 * c, 3 + 2 * c
                                kt = kts[ki]
                                sl = slots[c][ki]
                                mm(bank(bO)[:, 0:nq], Vaug[:, kt, h, 0:128], PT[:, sl, 0:nq], ki == 0, ki == nk_t - 1,
                                   ["PT.%d" % sl, vkeys[kt]], [PK[bO]])
                                mm(bank(bL)[:, 0:nq], onesb[:], PT[:, sl, 0:nq], ki == 0, ki == nk_t - 1,
                                   ["PT.%d" % sl, "onesb"], [PK[bL]])

                            for ki, kt in enumerate(kts):
                                kc0 = kt * 128
                                for c in range(2):
                                    pr = slice(64 * c, 64 * c + 64)
                                    pb = (0 if c == 0 else 6) + ki % 2
                                    sl = ptc % NPT
                                    ptc += 1
                                    slots[c].append(sl)
                                    mm(bank(pb)[:, 0:nq], KT[pr, h, kc0:kc0 + 128], QT[pr, h, q0:q0 + nq], True, True,
                                       [kkeys[kt]] + qk_, [PK[pb]])
                                    act(PT[:, sl, 0:nq], bank(pb)[:, 0:nq], AF.Exp, [PK[pb]], ["PT.%d" % sl], scale=0.125)
                                if ki >= 1:
                                    pv(0, ki - 1)
                                    pv(1, ki - 1)
                            pv(0, nk_t - 1)
                            pv(1, nk_t - 1)
                            fb = it % 2
                            it += 1
                            for c in range(2):
                                bO, bL = 2 + 2 * c, 3 + 2 * c
                                act(rc[c][:, 0:nq], bank(bL)[:, 0:nq], AF.Ln, [PK[bL]], ["rc%d" % c])
                                act(rc[c][:, 0:nq], rc[c][:, 0:nq], AF.Exp, ["rc%d" % c], ["rc%d" % c], scale=-1.0)
                                tt(tq[c][:, 0:nq], bank(bO)[:, 0:nq], rc[c][:, 0:nq], ALU.mult, [PK[bO], "rc%d" % c], ["tq%d" % c])
                            ok_ = "ocmb%d" % fb
                            stt(ocmb[fb][:, 0:nq], tq[1][:, 0:nq], lamt[:, 1:2], tq[0][:, 0:nq], ALU.mult, ALU.add,
                                ["tq0", "tq1", "lamt"], [ok_])
                            act(sqo[:, 0:nq], ocmb[fb][:, 0:nq], AF.Square, [ok_], ["sqo"])
                            pbt = it % 2
                            mm(bank(pbt)[:, 0:nq], ones, sqo[:, 0:nq], True, True, ["consts", "sqo"], [PK[pbt]])
                            ts(rso[:, 0:nq], bank(pbt)[:, 0:nq], 1.0 / 128, EPS, ALU.mult, ALU.add, [PK[pbt]], ["rso"])
                            act(rso[:, 0:nq], rso[:, 0:nq], AF.Ln, ["rso"], ["rso"])
                            act(rso[:, 0:nq], rso[:, 0:nq], AF.Exp, ["rso", "lnc"], ["rso"], scale=-0.5, bias=lnc[:, 0:1])
                            stt(OT[:, h, q0:q0 + nq], ocmb[fb][:, 0:nq], subgT, rso[:, 0:nq], ALU.mult, ALU.mult,
                                [ok_, "pvec", "rso"], ["OT"])
            dbg("OT", OT[:], ["OT"])
            phase_end("D")

            barrier()
            with contextlib.ExitStack() as L1b:
                YT = sb_t(L1b, "YT", [128, 16, 1024], BF16)
                arenaW = sb_t(L1b, "arenaW", [128, 12288], BF16)

                def wview(off, k, c):
                    return arenaW[:, off:off + k * c].rearrange("p (k c) -> p k c", k=k)
                ssacc = sb_t(L1b, "ssacc", [128, 8])
                memset(ssacc[:], 0.0, ["ssacc"])
                w_in_v = w_in.rearrange("(kt p) c -> p kt c", p=128)
                with contextlib.ExitStack() as sF:
                    wdt = sb_t(sF, "wdt", [128, 8, 128], BF16)
                    DT = sb_t(sF, "DT", [128, 12, 64])
                    Aa = sb_t(sF, "Aa", [128, 12, 64])
                    Aneg = sb_t(sF, "Aneg", [128, 128])
                    dma(wdt[:], w_dt.rearrange("(kt p) c -> p kt c", p=128), [], ["wdt"], q="pool")
                    for t in range(12):
                        c0 = 0 if t < 4 else 64
                        pb, po = (0, t * 64) if t < 8 else (1, (t - 8) * 64)
                        for kt in range(8):
                            mm(bank(pb)[:, po:po + 64], hT[:, kt, t * 128:(t + 1) * 128], wdt[:, kt, c0:c0 + 64], kt == 0, kt == 7,
                               [hkeys[t], "wdt"], [PK[pb]])
                    tt(DT[:, 0:4, :], bank(0)[:, 0:256].rearrange("p (t c) -> p t c", t=4),
                       bvec[:, BO_DTBP:BO_DTBP + 64].unsqueeze(1).to_broadcast([128, 4, 64]), ALU.add, [PK[0], "bvec"], ["DT"])
                    tt(DT[:, 4:8, :], bank(0)[:, 256:512].rearrange("p (t c) -> p t c", t=4),
                       bvec[:, BO_DTBS:BO_DTBS + 64].unsqueeze(1).to_broadcast([128, 4, 64]), ALU.add, [PK[0], "bvec"], ["DT"])
                    tt(DT[:, 8:12, :], bank(1)[:, 0:256].rearrange("p (t c) -> p t c", t=4),
                       bvec[:, BO_DTBS:BO_DTBS + 64].unsqueeze(1).to_broadcast([128, 4, 64]), ALU.add, [PK[1], "bvec"], ["DT"])
                    act(DT[:], DT[:], AF.Exp, ["DT"], ["DT"])
                    act(DT[:], DT[:], AF.Ln, ["DT"], ["DT"], bias=ones[:, 0:1])
                    act(Aneg[:], bvec[:, BO_ALP:BO_ALP + 128], AF.Exp, ["bvec"], ["Aneg"])
                    ts(Aneg[:], Aneg[:], -1.0, None, ALU.mult, None, ["Aneg"], ["Aneg"])
                    tt(Aa[:, 0:4, :], DT[:, 0:4, :], Aneg[:, 0:64].unsqueeze(1).to_broadcast([128, 4, 64]), ALU.mult,
                       ["DT", "Aneg"], ["Aa"])
                    tt(Aa[:, 4:12, :], DT[:, 4:12, :], Aneg[:, 64:128].unsqueeze(1).to_broadcast([128, 8, 64]), ALU.mult,
                       ["DT", "Aneg"], ["Aa"])
                    dbg("DT", DT[:], ["DT"])
                    phase_end("E")

                    wx = [wview(0, 8, 256), wview(2048, 8, 256)]
                    wB = [wview(4096, 8, 128), wview(5120, 8, 128)]
                    wC = [wview(6144, 8, 128), wview(7168, 8, 128)]
                    wz = [wview(8192, 8, 256), wview(10240, 8, 256)]
                    AKEYS = ["wx0", "wx1", "wB0", "wB1", "wC0", "wC1", "wz0", "wz1"]
                    rawpad1 = sb_t(sF, "rawpad0", [128, 1548], BF16)
                    dg = sb_t(sF, "dg", [128, 4, 5, 128], BF16)
                    xs = sb_t(sF, "xs", [128, 1536])
                    xtok = sb_t(sF, "xtok", [128, 12, 256], BF16)
                    BTt = sb_t(sF, "BTt", [128, 1536], BF16)
                    CTt = sb_t(sF, "CTt", [128, 1536], BF16)
                    Btok = sb_t(sF, "Btok", [128, 12, 128], BF16)
                    zs = sb_t(sF, "zs", [128, 8, 256])
                    h0g1 = sb_t(sF, "h0g0", [128, 2, 256])
                    h0g = [h0g1, h0g1]
                    a8 = sb_t(sF, "a8", [128, 2, 12, 4])
                    dt8 = sb_t(sF, "dt8", [128, 2, 12, 4])
                    EDT = sb_t(sF, "EDT", [128, 3, 2, 12, 4])
                    DDg = sb_t(sF, "DDg", [128, 2, 12, 4])
                    diagD = sb_t(sF, "diagD", [128, 4, 128], BF16)
                    CBm = [sb_t(sF, "CBm%d" % i, [128, 4, 128], BF16) for i in range(2)]
                    aU1 = sb_t(sF, "aU0", [128, 2, 4, 128])
                    aU = [aU1, aU1]
                    Et = [sb_t(sF, "Et%d" % i, [128, 2, 4, 128], BF16) for i in range(2)]
                    Mt = [sb_t(sF, "Mt%d" % i, [128, 4, 4, 128], BF16) for i in range(2)]
                    xd = [sb_t(sF, "xd%d" % i, [128, 4, 256], BF16) for i in range(2)]
                    xdd = [sb_t(sF, "xdd0", [128, 4, 256], BF16), sb_t(sF, "xdd1", [128, 8, 256], BF16)]
                    hprev = [sb_t(sF, "hprev%d" % i, [128, 4, 256], BF16) for i in range(2)]
                    hs = [[sb_t(sF, "hs%d_%d" % (d, i), [128, 256]) for i in range(2)] for d in range(2)]
                    hzero = sb_t(sF, "hzero", [128, 256])
                    t1 = sb_t(sF, "t1", [128, 4, 256])
                    t2 = sb_t(sF, "t2", [128, 4, 256])
                    sstmp = sb_t(sF, "sstmp", [128, 4])
                    fst1 = sb_t(sF, "fst0", [128, 2, 128])
                    fst = [fst1, fst1]
                    memset(rawpad1[:], 0.0, ["rawpad0"], eng="pool")
                    memset(hzero[:], 0.0, ["hzero"])
                    seg_pad = [(0, 0, 256), (260, 256, 256), (520, 512, 1024)]
                    fcount = [0]
                    rpc = [0]
                    PARTS = ("p", "s")

                    def stageA(g, part):
                        wi = g % 2
                        pn = PARTS[part]
                        if part == 0:
                            dma(wx[wi][:], w_in_v[:, :, C_X + g * 256:C_X + (g + 1) * 256], [], ["wx%d" % wi], q="pool")
                            dma(wB[wi][:], w_in_v[:, :, C_B + g * 128:C_B + (g + 1) * 128], [], ["wB%d" % wi], q="pool")
                            dma(wC[wi][:], w_in_v[:, :, C_C + g * 128:C_C + (g + 1) * 128], [], ["wC%d" % wi], q="pool")
                            dma(wz[wi][:], w_in_v[:, :, C_Z + g * 256:C_Z + (g + 1) * 256], [], ["wz%d" % wi], q="pool")
                        tbs = [0] if part == 0 else [1, 2]
                        segs = seg_pad[0:2] if part == 0 else seg_pad[2:3]
                        chunks = list(range(0, 4)) if part == 0 else list(range(4, 12))
                        tok0 = 0 if part == 0 else 512
                        ntok = 512 if part == 0 else 1024
                        blocks = [(wx[wi], "wx%d" % wi, 0, 2 * g), (wx[wi], "wx%d" % wi, 128, 2 * g + 1),
                                  (wB[wi], "wB%d" % wi, 0, 16 + g), (wC[wi], "wC%d" % wi, 0, 24 + g)]
                        rk = "rawpad0"
                        ak = "acc"
                        xk = "xs"
                        pbank = {}

                        def proj(bi):
                            wt, wk_, wc0, cblk = blocks[bi]
                            for tb in tbs:
                                pb = tb % 2
                                for kt in range(8):
                                    mm(bank(pb), wt[:, kt, wc0:wc0 + 128], hT[:, kt, tb * 512:(tb + 1) * 512], kt == 0, kt == 7,
                                       [wk_] + hkeys[tb * 4:tb * 4 + 4], [PK[pb]])

                        def evac(bi):
                            for tb in tbs:
                                pb = tb % 2
                                if tb == 0:
                                    dst = rawpad1[:, 0:520].rearrange("p (s c) -> p s c", s=2)[:, :, 2:258]
                                    act(dst, bank(pb).rearrange("p (s c) -> p s c", s=2), AF.Identity, [PK[pb]], [rk])
                                else:
                                    o = 522 + (tb - 1) * 512
                                    act(rawpad1[:, o:o + 512], bank(pb), AF.Identity, [PK[pb]], [rk])

                        po_w = PO_CWP if part == 0 else PO_CWS
                        for bi_ in range(4):
                            cblk_ = blocks[bi_][3]
                            for j in range(5):
                                act(dg[:, bi_, j, :], ident, AF.Identity, ["consts", "pvec"], ["dg.%d.%d" % (bi_, j)],
                                    scale=pvec[:, po_w + cblk_ * 5 + j: po_w + cblk_ * 5 + j + 1])
                        subs = [(0, 0, 0, 256), (260, 0, 256, 256)] if part == 0 else [(520, 0, 0, 512), (520 + 512, 1, 0, 512)]

                        def conv(bi):
                            for (pbase, pb, co, n) in subs:
                                for j in range(5):
                                    mm(bank(pb)[:, co:co + n], dg[:, bi, j, :], rawpad1[:, pbase + j:pbase + j + n], j == 0, j == 4,
                                       ["dg.%d.%d" % (bi, j), rk], [PK[pb]])

                        def silu(bi):
                            cblk = blocks[bi][3]
                            bia = pvec[:, PO_CB + cblk:PO_CB + cblk + 1]
                            outs = []
                            if part == 0:
                                outs.append((0, 0, 512, 0))
                            else:
                                outs.append((0, 0, 512, 512))
                                outs.append((1, 0, 512, 1024))
                            for (pb, co, n, tcol) in outs:
                                if bi == 3:
                                    act(CTt[:, tcol:tcol + n], bank(pb)[:, co:co + n], AF.Silu, [PK[pb], "pvec"], ["CTt." + pn], bias=bia)
                                else:
                                    act(xs[:, tcol:tcol + n], bank(pb)[:, co:co + n], AF.Silu, [PK[pb], "pvec"], [xk], bias=bia)
                            if bi == 2:
                                for (pb, co, n, tcol) in outs:
                                    act(BTt[:, tcol:tcol + n], bank(pb)[:, co:co + n], AF.Silu, [PK[pb], "pvec"], ["BTt." + pn], bias=bia)

                        def transposes(bi):
                            if bi == 3:
                                return
                            for q4 in range(0, len(chunks), 4):
                                pb = (q4 // 4) % 2
                                cs = chunks[q4:q4 + 4]
                                for q, t in enumerate(cs):
                                    tr(bank(pb)[:, q * 128:(q + 1) * 128], xs[:, t * 128:(t + 1) * 128], ident,
                                       [xk, "consts"], [PK[pb]])
                                src = bank(pb).rearrange("p (t c) -> p t c", t=4)
                                if bi < 2:
                                    act(xtok[:, cs[0]:cs[0] + 4, bi * 128:(bi + 1) * 128], src, AF.Identity, [PK[pb]], ["xtok." + pn])
                                else:
                                    act(Btok[:, cs[0]:cs[0] + 4, :], src, AF.Identity, [PK[pb]], ["Btok." + pn])

                        for bi in range(4):
                            proj(bi)
                            evac(bi)
                            yield
                            conv(bi)
                            silu(bi)
                            yield
                            transposes(bi)
                            yield
                        for t in ([0, 1, 2, 3] if part == 0 else [4, 5, 6, 7]):
                            pb = t % 2
                            for kt in range(8):
                                mm(bank(pb)[:, 0:256], hT[:, kt, t * 128:(t + 1) * 128], wz[wi][:, kt, :], kt == 0, kt == 7,
                                   [hkeys[t], "wz%d" % wi], [PK[pb]])
                            act(zs[:, t, :], bank(pb)[:, 0:256], AF.Silu, [PK[pb]], ["zs.%d" % t])
                        yield

                    def prepG(g):
                        for d in range(2):
                            hc0 = d * 32 + 4 * g
                            cp(a8[:, d], Aa[:, :, hc0:hc0 + 4], ["Aa"], ["a8"])
                            cp(dt8[:, d], DT[:, :, hc0:hc0 + 4], ["DT"], ["dt8"])
                        for d in range(2):
                            rhs = a8[:, d].rearrange("p c h -> p (c h)")
                            for ki, msk in enumerate(((mU, mLW)[d], (mSL, mSU)[d], ones)):
                                o = (ki * 2 + d) * 48
                                mm(bank(7)[:, o:o + 48], msk, rhs, True, True, ["a8", "consts"], [PK[7]])
                        act(EDT[:].rearrange("p k d c h -> p (k d c h)"), bank(7)[:, 0:288], AF.Exp, [PK[7]], ["EDT"])
                        tt(DDg[:], EDT[:, 1], dt8[:], ALU.mult, ["EDT", "dt8"], ["DDg"])
                        dsk = bvec[:, BO_DSK + 4 * g:BO_DSK + 4 * g + 4]
                        for h in range(4):
                            ts(diagD[:, h, :], ident, dsk[:, h:h + 1], None, ALU.mult, None, ["consts", "bvec"], ["diagD"])

                    def stageB(g, part):
                        wi = g % 2
                        pn = PARTS[part]
                        XK, BTK, CTK, BKK, ZK = "xtok." + pn, "BTt." + pn, "CTt." + pn, "Btok." + pn, "zs." + pn
                        c0 = 0 if part == 0 else 4
                        if part == 0:
                            prepG(g)
                        if part == 0:
                            chains = [(0, None, [0, 1], True, 0), (1, None, [1, 0], True, 0),
                                      (0, None, [2, 3], True, 1), (1, None, [3, 2], True, 1)]
                        else:
                            chains = [(0, h0g1[:, 0, :], [0, 1, 2, 3], False, 0), (1, h0g1[:, 1, :], [7, 6, 5, 4, 3, 2, 1, 0], False, 0)]
                        work = []
                        for chn, (d, init, order, need_final, sqi) in enumerate(chains):
                            for kstep, ci in enumerate(order):
                                work.append((chn, kstep, ci))
                        state = {}
                        for chn, (d, init, order, need_final, sqi) in enumerate(chains):
                            state[chn] = (hzero[:], "hzero") if init is None else (init, "h0g0")
                        pos = [0]

                        def s_round():
                            rnd = []
                            nslot = 0
                            while pos[0] < len(work) and nslot < 8:
                                chn, kstep, ci = work[pos[0]]
                                d, init, order, need_final, sqi = chains[chn]
                                upd = need_final or kstep < len(order) - 1
                                rnd.append((chn, kstep, ci, upd, nslot if upd else None))
                                if upd:
                                    pb = 4 + nslot // 2
                                    so = (nslot % 2) * 256
                                    mm(bank(pb)[:, so:so + 256], Btok[:, c0 + ci, :], xdd[d][:, ci, :], True, True,
                                       [BKK, "xdd%d" % d], [PK[pb]])
                                    nslot += 1
                                pos[0] += 1
                            return rnd

                        def rec_round(rnd):
                            for chn, kstep, ci, upd, sl in rnd:
                                d, init, order, need_final, sqi = chains[chn]
                                cur, curk = state[chn]
                                if ci < 4:
                                    cp(hprev[d][:, ci, :], cur, [curk], ["hprev%d.%d" % (d, ci)])
                                if upd:
                                    pb = 4 + sl // 2
                                    so = (sl % 2) * 256
                                    nxt, nk_ = hs[d][kstep % 2], "hs%d_%d" % (d, kstep % 2)
                                    if init is None and kstep == 0:
                                        cp(nxt[:], bank(pb)[:, so:so + 256], [PK[pb]], [nk_])
                                    else:
                                        tt(nxt[:].rearrange("p (h q) -> p h q", h=4), cur.rearrange("p (h q) -> p h q", h=4),
                                           EDT[:, 2, d, c0 + ci, :].unsqueeze(2).to_broadcast([128, 4, 64]), ALU.mult,
                                           [curk, "EDT"], [nk_])
                                        tt(nxt[:], nxt[:], bank(pb)[:, so:so + 256], ALU.add, [nk_, PK[pb]], [nk_])
                                    cur, curk = nxt[:], nk_
                                    state[chn] = (cur, curk)
                                if need_final and kstep == len(order) - 1:
                                    fcount[0] += 1
                                    pbf = fcount[0] % 2
                                    for j in range(2):
                                        tr(bank(pbf)[:, j * 128:(j + 1) * 128], cur[:, j * 128:(j + 1) * 128], ident,
                                           [curk, "consts"], [PK[pbf]])
                                    cp(fst1[:], bank(pbf)[:, 0:256].rearrange("p (j n) -> p j n", j=2), [PK[pbf]], ["fst0"], eng="act")
                                    dst_o = (sf if d == 0 else sb)[sqi, g * 256:(g + 1) * 256, :].rearrange("(j p) n -> p j n", p=128)
                                    finals.append(dma(dst_o, fst1[:], ["fst0"], []))

                        def dexp(k, d, half):
                            ab = k % 2
                            msk_u = mU if d == 0 else mLW
                            msk_s = mSL if d == 0 else mSU
                            cc = c0 + 2 * half
                            tt(aU1[:], a8[:, d, cc:cc + 2, :].unsqueeze(3).to_broadcast([128, 2, 4, 128]),
                               msk_u.unsqueeze(1).unsqueeze(1).to_broadcast([128, 2, 4, 128]), ALU.mult,
                               ["a8", "consts"], ["aU0"])
                            for k2 in range(2):
                                mm(bank(2 + k2), msk_s, aU1[:, k2].rearrange("p h l -> p (h l)"), True, True,
                                   ["aU0", "consts"], [PK[2 + k2]])
                            act(Et[ab][:].rearrange("p c h l -> p (c h l)"), psA[:, 1024:2048],
                                AF.Exp, [PK[2], PK[3]], ["Et%d" % ab])

                        def mtmul(k, d, half):
                            ab = k % 2
                            tt(Mt[d][:, 2 * half:2 * half + 2], Et[ab][:],
                               CBm[d][:, 2 * half:2 * half + 2, :].unsqueeze(2).to_broadcast([128, 2, 4, 128]),
                               ALU.mult, ["Et%d" % ab, "CBm%d" % d], ["Mt%d" % d])

                        for d in range(2):
                            nch = 8 if (part == 1 and d == 1) else 4
                            tt(xdd[d][:, 0:nch, :].rearrange("p c (h q) -> p c h q", h=4),
                               xtok[:, c0:c0 + nch, :].rearrange("p c (h q) -> p c h q", h=4),
                               DDg[:, d, c0:c0 + nch, :].unsqueeze(3).to_broadcast([128, nch, 4, 64]), ALU.mult,
                               [XK, "DDg"], ["xdd%d" % d])
                        for ci in range(4):
                            c = c0 + ci
                            mm(bank(1)[:, ci * 128:(ci + 1) * 128], BTt[:, c * 128:(c + 1) * 128], CTt[:, c * 128:(c + 1) * 128],
                               True, True, [BTK, CTK], [PK[1]])
                        cb3 = bank(1).rearrange("p (c l) -> p c l", c=4)
                        tt(CBm[0][:], cb3, mU.unsqueeze(1).to_broadcast([128, 4, 128]), ALU.mult, [PK[1], "consts"], ["CBm0"])
                        tt(CBm[1][:], cb3, mLW.unsqueeze(1).to_broadcast([128, 4, 128]), ALU.mult, [PK[1], "consts"], ["CBm1"])
                        rnd1 = s_round()
                        yield
                        seq = [(0, 0), (0, 1), (1, 0), (1, 1)]
                        dexp(0, 0, 0)
                        yield
                        for d in range(2):
                            tt(xd[d][:].rearrange("p c (h q) -> p c h q", h=4),
                               xtok[:, c0:c0 + 4, :].rearrange("p c (h q) -> p c h q", h=4),
                               dt8[:, d, c0:c0 + 4, :].unsqueeze(3).to_broadcast([128, 4, 4, 64]), ALU.mult,
                               [XK, "dt8"], ["xd%d" % d])
                        yield
                        rec_round(rnd1)
                        yield
                        for k in range(4):
                            d, half = seq[k]
                            if k + 1 < 4:
                                dexp(k + 1, *seq[k + 1])
                            mtmul(k, d, half)
                            yield
                        while pos[0] < len(work):
                            rnd = s_round()
                            yield
                            rec_round(rnd)
                            yield
                        for ci in range(4):
                            pb = 6 + ci // 2
                            yo = (ci % 2) * 256
                            for h in range(4):
                                mm(bank(pb)[:, yo + h * 64:yo + (h + 1) * 64], diagD[:, h, :], xtok[:, c0 + ci, h * 64:(h + 1) * 64],
                                   h == 0, False, ["diagD", XK], [PK[pb]])
                            for d in range(2):
                                for h in range(4):
                                    mm(bank(pb)[:, yo + h * 64:yo + (h + 1) * 64], Mt[d][:, ci, h, :], xd[d][:, ci, h * 64:(h + 1) * 64],
                                       False, d == 1 and h == 3, ["Mt%d" % d, "xd%d" % d], [PK[pb]])
                        for d in range(2):
                            for ci in range(4):
                                pb = (2 if d == 0 else 4) + ci // 2
                                yo = (ci % 2) * 256
                                mm(bank(pb)[:, yo:yo + 256], CTt[:, (c0 + ci) * 128:(c0 + ci + 1) * 128], hprev[d][:, ci, :], True, True,
                                   [CTK, "hprev%d.%d" % (d, ci)], [PK[pb]])
                        yield
                        for hb in range(2):
                            for d in range(2):
                                pb = (2 if d == 0 else 4) + hb
                                dstt = t1 if d == 0 else t2
                                tt(dstt[:, 2 * hb:2 * hb + 2, :].rearrange("p c (h q) -> p c h q", h=4),
                                   bank(pb).rearrange("p (c h q) -> p c h q", c=2, h=4),
                                   EDT[:, 0, d, c0 + 2 * hb:c0 + 2 * hb + 2, :].unsqueeze(3).to_broadcast([128, 2, 4, 64]), ALU.mult,
                                   [PK[pb], "EDT"], ["t1" if d == 0 else "t2"])
                        tt(t1[:], t1[:], t2[:], ALU.add, ["t1", "t2"], ["t1"])
                        for hb in range(2):
                            tt(t1[:, 2 * hb:2 * hb + 2, :], t1[:, 2 * hb:2 * hb + 2, :],
                               bank(6 + hb).rearrange("p (c q) -> p c q", c=2), ALU.add, ["t1", PK[6 + hb]], ["t1"])
                        if g == 0 and part == 1:
                            dbg("ysum", t1[:], ["t1"])
                        tt(t2[:], t1[:], zs[:, c0:c0 + 4, :], ALU.mult, ["t1"] + ["zs.%d" % (c0 + i_) for i_ in range(4)], ["t2"])
                        yield
                        for ci in range(4):
                            act(t1[:, ci, :], t2[:, ci, :], AF.Square, ["t2"], ["t1", "sstmp"], accum=sstmp[:, ci:ci + 1])
                        tt(ssacc[:, c0:c0 + 4], ssacc[:, c0:c0 + 4], sstmp[:], ALU.add, ["ssacc", "sstmp"], ["ssacc"])
                        for j in range(2):
                            pb = 2 + j
                            for ci in range(4):
                                tr(bank(pb)[:, ci * 128:(ci + 1) * 128], t2[:, ci, j * 128:(j + 1) * 128], ident, ["t2", "consts"], [PK[pb]])
                            act(YT[:, 2 * g + j, c0 * 128:(c0 + 4) * 128], bank(pb), AF.Identity, [PK[pb], "pvec"], ["YT"],
                                scale=pvec[:, PO_SSG + 2 * g + j:PO_SSG + 2 * g + j + 1])
                        if part == 1 and g < 7:
                            dma(h0g1[:], h0T[:, :, (g + 1) * 256:(g + 2) * 256], [], ["h0g0"])
                        yield

                    def run_interleaved(gens):
                        gens = [[x[0], x[1], 0] for x in gens if x is not None]
                        while gens:
                            for x in list(gens):
                                try:
                                    P.tag = "%s#%d" % (x[0], x[2])
                                    x[2] += 1
                                    next(x[1])
                                except StopIteration:
                                    gens.remove(x)
                        P.tag = ""

                    dma(h0g1[:], h0T[:, :, 0:256], [], ["h0g0"])
                    def SA(g, p):
                        return ("A%d%s" % (g, PARTS[p]), stageA(g, p))

                    def SB(g, p):
                        return ("B%d%s" % (g, PARTS[p]), stageB(g, p))

                    run_interleaved([SA(0, 0)])
                    run_interleaved([SB(0, 0), SA(0, 1)])
                    wgb0_v, wa0_v, wga0_v = wview(0, 8, 512), wview(4096, 8, 512), wview(8192, 8, 512)
                    for g in range(8):
                        if g == 7:
                            dma(wgb0_v, w_in_v[:, :, C_GB:C_GB + 512], [], ["wgb0"] + AKEYS, q="pool")
                            dma(wa0_v, w_a.rearrange("(kt p) c -> p kt c", p=128)[:, :, 0:512], [], ["wa0"] + AKEYS, q="pool")
                            dma(wga0_v, w_in_v[:, :, C_GA:C_GA + 512], [], ["wga0"] + AKEYS, q="pool")
                        run_interleaved([SB(g, 1), SA(g + 1, 0) if g < 7 else None])
                        if g < 7:
                            run_interleaved([SB(g + 1, 0), SA(g + 1, 1)])
                dbg("YT", YT[:], ["YT"])
                dbg("ssacc", ssacc[:], ["ssacc"])
                phase_end("F")

                barrier()
                with contextlib.ExitStack() as sG:
                    wga = [wga0_v, sb_t(sG, "wga1", [128, 8, 512], BF16)]
                    wgb = [wgb0_v, sb_t(sG, "wgb1", [128, 8, 512], BF16)]
                    wa = [wa0_v, sb_t(sG, "wa1", [128, 8, 512], BF16)]
                    wb_ = [sb_t(sG, "wb%d" % i, [128, 16, 512], BF16) for i in range(2)]
                    rsy = sb_t(sG, "rsy", [128, 16])
                    sga = sb_t(sG, "sga", [128, 512])
                    sgb = sb_t(sG, "sgb", [128, 512])
                    mrg = [sb_t(sG, "mrg%d" % i, [128, 512]) for i in range(2)]
                    w_a_v = w_a.rearrange("(kt p) c -> p kt c", p=128)
                    w_b_v = w_b.rearrange("(kt p) c -> p kt c", p=128)
                    dma(wb_[0][:], w_b_v[:, :, 0:512], [], ["wb0"], q="pool")
                    dma(wgb[1][:], w_in_v[:, :, C_GB + 512:C_GB + 1024], [], ["wgb1"], q="pool")
                    dma(wa[1][:], w_a_v[:, :, 512:1024], [], ["wa1"], q="pool")
                    dma(wb_[1][:], w_b_v[:, :, 512:1024], [], ["wb1"], q="pool")
                    dma(wga[1][:], w_in_v[:, :, C_GA + 512:C_GA + 1024], [], ["wga1"], q="pool")
                    dma(hT[:, :, 1024:1536], w_o.rearrange("(kt p) c -> p kt c", p=128)[:, :, 0:512], [],
                        ["wo0"] + list(hkeys[8:12]), q="pool")
                    rstd_from_ss(ssacc[:], 8, rsy[:, 8:16], rsy[:, 0:8], "ssacc", "rsya", "rsyb", 1.0 / 2048)
                    sga2 = [sga, sb_t(sG, "sga_b", [128, 512])]
                    sgb2 = [sgb, sb_t(sG, "sgb_b", [128, 512])]
                    for t in range(8):
                        for cb in range(2):
                            b0 = 4 * cb
                            sga_, sgb_ = sga2[cb], sgb2[cb]
                            sak, sbk = "sga%d" % cb, "sgb%d" % cb
                            for kt in range(8):
                                mm(bank(b0 + 1), hT[:, kt, t * 128:(t + 1) * 128], wgb[cb][:, kt, :], kt == 0, kt == 7,
                                   [hkeys[t], "wgb%d" % cb], [PK[b0 + 1]])
                            for h in range(8):
                                mm(bank(b0 + 2), OT[:, h, t * 128:(t + 1) * 128], wa[cb][:, h, :], h == 0, h == 7,
                                   ["OT", "wa%d" % cb], [PK[b0 + 2]])
                            for kt in range(16):
                                mm(bank(b0 + 3), YT[:, kt, t * 128:(t + 1) * 128], wb_[cb][:, kt, :], kt == 0, kt == 15,
                                   ["YT", "wb%d" % cb], [PK[b0 + 3]])
                            for kt in range(8):
                                mm(bank(b0), hT[:, kt, t * 128:(t + 1) * 128], wga[cb][:, kt, :], kt == 0, kt == 7,
                                   [hkeys[t], "wga%d" % cb], [PK[b0]])
                            act(sgb_[:], bank(b0 + 1), AF.Sigmoid, [PK[b0 + 1]], [sbk])
                            act(sga_[:], bank(b0), AF.Sigmoid, [PK[b0]], [sak])
                            mk = "mrg%d" % cb
                            stt(sgb_[:], bank(b0 + 3), rsy[:, 8 + t:9 + t], sgb_[:], ALU.mult, ALU.mult, [PK[b0 + 3], "rsyb", sbk], [sbk])
                            tt(mrg[cb][:], sga_[:], bank(b0 + 2), ALU.mult, [sak, PK[b0 + 2]], [mk])
                            tt(mrg[cb][:], mrg[cb][:], sgb_[:], ALU.add, [mk, sbk], [mk])
                            for q in range(4):
                                tr(bank(b0)[:, q * 128:(q + 1) * 128], mrg[cb][:, q * 128:(q + 1) * 128], ident,
                                   [mk, "consts"], [PK[b0]])
                        for cb in range(2):
                            act(hT[:, 4 * cb:4 * cb + 4, t * 128:(t + 1) * 128], bank(4 * cb).rearrange("p (k t) -> p k t", k=4),
                                AF.Identity, [PK[4 * cb]], [hkeys[t]])
        dbg("mT", hT[:], hkeys)
        phase_end("G1")

        barrier()
        with contextlib.ExitStack() as sH:
            x1 = sb_t(sH, "x1", [128, 8, D])
            gb1 = [sb_t(sH, "gb1_%d" % r, [128, D]) for r in range(2)]
            gb2 = [sb_t(sH, "gb2_%d" % r, [128, D]) for r in range(2)]
            dgt = [sb_t(sH, "dgt%d" % i, [128, 128]) for i in range(2)]
            cnt_g = 0
            for r in range(2):
                for gi, gbt in enumerate((gb1, gb2)):
                    for cb in range(2):
                        pb = cnt_g % 2
                        for q in range(4):
                            kt = cb * 4 + q
                            db = cnt_g % 2
                            cnt_g += 1
                            ts(dgt[db][:], ident, AB[:, 4 + gi, kt, r:r + 1], None, ALU.mult, None, ["consts", "AB"], ["dgt%d" % db])
                            mm(bank(pb)[:, q * 128:(q + 1) * 128], ones, dgt[db][:], True, True, ["consts", "dgt%d" % db], [PK[pb]])
                        cp(gbt[r][:, cb * 512:(cb + 1) * 512], bank(pb), [PK[pb]], ["gb%d_%d" % (gi + 1, r)], eng="act")
            x1keys = ["x1.%d" % t for t in range(8)]
            h2T = sb_t(sH, "h2T", [128, 8, 1024], BF16)
            h2keys = ["h2T.%d" % t for t in range(8)]
            with contextlib.ExitStack() as sG2:
                wo1 = sb_t(sG2, "wo1", [128, 8, 512], BF16)
                xr = [sb_t(sG2, "xr%d" % i, [128, D]) for i in range(2)]
                tmpg = sb_t(sG2, "tmpg", [128, 512])
                dma(wo1[:], w_o.rearrange("(kt p) c -> p kt c", p=128)[:, :, 512:1024], [], ["wo1"], q="pool")

                def g2_tile(t):
                    r = 0 if t < 4 else 1
                    xb = t % 2
                    dma(xr[xb][:], x_all[t * 128:(t + 1) * 128, :], [], ["xr%d" % xb])
                    for cb in range(2):
                        pb = 2 + cb
                        for kt in range(8):
                            wsl_ = hT[:, kt, 1024:1536] if cb == 0 else wo1[:, kt, :]
                            mm(bank(pb), hT[:, kt, t * 128:(t + 1) * 128], wsl_, kt == 0, kt == 7,
                               [hkeys[t], "wo0" if cb == 0 else "wo1"], [PK[pb]])
                        tt(tmpg[:], bank(pb), gb1[r][:, cb * 512:(cb + 1) * 512], ALU.mult, [PK[pb], "gb1_%d" % r], ["tmpg"])
                        tt(x1[:, t, cb * 512:(cb + 1) * 512], tmpg[:], xr[xb][:, cb * 512:(cb + 1) * 512], ALU.add,
                           ["tmpg", "xr%d" % xb], [x1keys[t]])

                g2_tile(0)
                norm_mod_to_hT(lambda t: (x1[:, t, :], x1keys[t]), 8, lambda t: 0 if t < 4 else 1, 2, h2T, h2keys, sG2, "nH",
                               hook=lambda t: g2_tile(t + 1) if t + 1 < 8 else None)
            dbg("x1", x1[:], x1keys)
            phase_end("G2")
            barrier()
            with contextlib.ExitStack() as sFF:
                actT = sb_t(sFF, "actT", [128, 22, 1024], BF16)
                wd1 = sb_t(sFF, "wd0", [128, 22, 512], BF16)
                w_d_v = w_d.rearrange("(kt p) c -> p kt c", p=128)
                wgt = [sb_t(sFF, "wgt%d" % i, [128, 8, 128], BF16) for i in range(3)]
                wut = [sb_t(sFF, "wut%d" % i, [128, 8, 128], BF16) for i in range(3)]
                sgl = [sb_t(sFF, "sgl%d" % i, [128, 512]) for i in range(4)]
                w_g_v = w_g.rearrange("(kt p) c -> p kt c", p=128)
                w_u_v = w_u.rearrange("(kt p) c -> p kt c", p=128)
                for ft in range(22):
                    wbi = ft % 3
                    dma(wgt[wbi][:], w_g_v[:, :, ft * 128:(ft + 1) * 128], [], ["wgt%d" % wbi], q="pool")
                    dma(wut[wbi][:], w_u_v[:, :, ft * 128:(ft + 1) * 128], [], ["wut%d" % wbi], q="pool")
                    for th in range(2):
                        pg, pu = 4 * (ft % 2) + 2 * th, 4 * (ft % 2) + 2 * th + 1
                        for kt in range(8):
                            mm(bank(pg), wgt[wbi][:, kt, :], h2T[:, kt, th * 512:(th + 1) * 512], kt == 0, kt == 7,
                               ["wgt%d" % wbi] + h2keys[4 * th:4 * th + 4], [PK[pg]])
                        for kt in range(8):
                            mm(bank(pu), wut[wbi][:, kt, :], h2T[:, kt, th * 512:(th + 1) * 512], kt == 0, kt == 7,
                               ["wut%d" % wbi] + h2keys[4 * th:4 * th + 4], [PK[pu]])
                        sgi = 2 * (ft % 2) + th
                        act(sgl[sgi][:], bank(pg), AF.Silu, [PK[pg]], ["sgl%d" % sgi])
                        tt(actT[:, ft, th * 512:(th + 1) * 512], sgl[sgi][:], bank(pu), ALU.mult, ["sgl%d" % sgi, PK[pu]],
                           ["actT.%d" % th])
                dbg("actT", actT[:], ["actT.0", "actT.1"])
                yst = [sb_t(sFF, "yst%d" % i, [128, 512]) for i in range(2)]
                wdA = h2T[:].rearrange("p k t -> p (k t)").rearrange("p (f c) -> p f c", f=16)
                wdB = sb_t(sFF, "wdB", [128, 6, 512], BF16)
                dma(wd1[:], w_d_v[:, :, 0:512], [], ["wd0"], q="pool")
                dma(wdA, w_d_v[:, 0:16, 512:1024], [], ["wdA"] + h2keys, q="pool")
                dma(wdB[:], w_d_v[:, 16:22, 512:1024], [], ["wdB"], q="pool")
                cnt = 0
                for cb in range(2):
                    for t in range(8):
                        r = 0 if t < 4 else 1
                        pb = 4 + cnt % 2
                        yb = cnt % 2
                        cnt += 1
                        for ft in range(22):
                            if cb == 0:
                                wsl, wkk = wd1[:, ft, :], "wd0"
                            elif ft < 16:
                                wsl, wkk = wdA[:, ft, :], "wdA"
                            else:
                                wsl, wkk = wdB[:, ft - 16, :], "wdB"
                            mm(bank(pb), actT[:, ft, t * 128:(t + 1) * 128], wsl, ft == 0, ft == 21,
                               ["actT.%d" % (t // 4), wkk], [PK[pb]])
                        tt(yst[yb][:], bank(pb), gb2[r][:, cb * 512:(cb + 1) * 512], ALU.mult, [PK[pb], "gb2_%d" % r], ["yst%d" % yb])
                        tt(yst[yb][:], yst[yb][:], x1[:, t, cb * 512:(cb + 1) * 512], ALU.add, ["yst%d" % yb, x1keys[t]], ["yst%d" % yb])
                        dst = y_p[t * 128:(t + 1) * 128, cb * 512:(cb + 1) * 512] if t < 4 else \
                            y_s[(t - 4) * 128:(t - 3) * 128, cb * 512:(cb + 1) * 512]
                        finals.append(dma(dst, yst[yb][:], ["yst%d" % yb], []))
        P.emit(final_wait_ids=[f for f in finals if f is not None])
    return nc, P


def _consts(flip):
    c = np.zeros((128, NCON), np.float32)
    r = np.arange(128)
    c[:, CO_ID:CO_ID + 128] = np.eye(128, dtype=np.float32)
    c[:, CO_U:CO_U + 128] = (r[:, None] <= r[None, :])
    c[:, CO_LW:CO_LW + 128] = (r[:, None] >= r[None, :])
    c[:, CO_SL:CO_SL + 128] = (r[:, None] > r[None, :])
    c[:, CO_SU:CO_SU + 128] = (r[:, None] < r[None, :])
    c[:, CO_ONE:CO_ONE + 128] = 1.0
    tpos = np.arange(1024)
    if flip:
        tpos = 1023 - tpos
    row = (tpos // 64).astype(np.float32)
    col = (tpos % 64).astype(np.float32)
    inv = (10000.0 ** (-np.arange(0, 32, 2, dtype=np.float32) / 32)).astype(np.float32)
    ang_r = row[:, None] * inv[None, :]
    ang_c = col[:, None] * inv[None, :]
    cos = np.concatenate([np.cos(ang_r), np.cos(ang_c)], axis=1).astype(np.float32)
    sin = np.concatenate([np.sin(ang_r), np.sin(ang_c)], axis=1).astype(np.float32)
    c[:, CO_COS:CO_COS + 256] = cos.reshape(8, 128, 32).transpose(1, 0, 2).reshape(128, 256)
    c[:, CO_SIN:CO_SIN + 256] = sin.reshape(8, 128, 32).transpose(1, 0, 2).reshape(128, 256)
    return c


def _prep_inputs(inp):
    f = lambda a: np.ascontiguousarray(np.asarray(a, dtype=np.float32))
    x_prompt, x_sample = f(inp["x_prompt"]), f(inp["x_sample"])
    c, c_ctx = f(inp["c"]), f(inp["c_ctx"])
    cache_k, cache_v = f(inp["cache_k"]), f(inp["cache_v"])
    s_f, s_b = f(inp["state_ssm_fwd"]), f(inp["state_ssm_bwd"])
    w_in = f(inp["w_in"])[0]
    conv_w = f(inp["conv_w"])[0]
    conv_b = f(inp["conv_b"])[0]
    A_log = f(inp["A_log"])[0]
    dt_bias = f(inp["dt_bias"])[0]
    shared = dict(
        w_ada=f(inp["w_ada"])[0], b_ada2=np.ascontiguousarray(np.broadcast_to(f(inp["b_ada"])[0][None, :], (2, 6 * D))),
        w_in=w_in, w_a=f(inp["w_branch_a"])[0], w_b=f(inp["w_branch_b"])[0], w_o=f(inp["w_out"])[0],
        w_g=f(inp["w_ffn_gate"])[0], w_u=f(inp["w_ffn_up"])[0], w_d=f(inp["w_ffn_down"])[0],
        ssmg=np.ascontiguousarray(np.broadcast_to(f(inp["ssm_norm_g"])[0][None, :], (128, 2048))),
    )
    dtcols = w_in[:, C_DT:C_DT + 64]
    maps = []
    for core in range(8):
        j, flip = core // 2, (core % 2 == 1)
        xs = x_sample[j][::-1] if flip else x_sample[j]
        x_all = np.concatenate([x_prompt[2 * core], x_prompt[2 * core + 1], xs], axis=0)
        cond = np.stack([c_ctx, c[j]], axis=0)
        condT = cond.reshape(2, 8, 128).transpose(2, 1, 0).reshape(128, 16)
        ckT = cache_k[j, 0].transpose(2, 3, 1, 0).reshape(128, NH, 512)
        cvv = cache_v[j, 0]
        hf = s_f[j, 0].reshape(2048, 128).T
        hb = s_b[j, 0].reshape(2048, 128).T
        h0T = np.stack([hb, hf] if flip else [hf, hb], axis=1)
        sw = (lambda a: np.concatenate([a[..., 32:64], a[..., 0:32]], axis=-1)) if flip else (lambda a: a)
        w_dt = np.concatenate([dtcols, sw(dtcols)], axis=1)
        pv = np.zeros((128, NPV), np.float32)
        pv[:, PO_G1:PO_G1 + 8] = f(inp["norm1_g"])[0].reshape(8, 128).T
        pv[:, PO_G2:PO_G2 + 8] = f(inp["norm2_g"])[0].reshape(8, 128).T
        cwp = conv_w.reshape(5, 32, 128).transpose(2, 1, 0)
        pv[:, PO_CWP:PO_CWP + 160] = cwp.reshape(128, 160)
        pv[:, PO_CWS:PO_CWS + 160] = (cwp[:, :, ::-1] if flip else cwp).reshape(128, 160)
        pv[:, PO_CB:PO_CB + 32] = conv_b.reshape(32, 128).T
        pv[:, PO_SSG:PO_SSG + 16] = f(inp["ssm_norm_g"])[0].reshape(16, 128).T
        pv[:, PO_SUBG] = f(inp["attn_sub_g"])[0]
        bv = np.zeros((128, NBV), np.float32)
        bv[:, BO_QG:BO_QG + 64] = f(inp["q_norm_g"])[0][None]
        bv[:, BO_KG:BO_KG + 64] = f(inp["k_norm_g"])[0][None]
        bv[:, BO_SUB:BO_SUB + 128] = f(inp["attn_sub_g"])[0][None]
        bv[:, BO_DSK:BO_DSK + 32] = f(inp["D_skip"])[0][None]
        dtb = dt_bias.reshape(64)
        alg = A_log.reshape(64)
        bv[:, BO_DTBP:BO_DTBP + 64] = dtb[None]
        bv[:, BO_DTBS:BO_DTBS + 64] = sw(dtb)[None]
        bv[:, BO_ALP:BO_ALP + 64] = alg[None]
        bv[:, BO_ALS:BO_ALS + 64] = sw(alg)[None]
        for k, nm in enumerate(("lambda_q1", "lambda_k1", "lambda_q2", "lambda_k2")):
            bv[:, BO_L + 64 * k:BO_L + 64 * (k + 1)] = f(inp[nm])[0][None]
        m = dict(shared)
        m.update(x_all=np.ascontiguousarray(x_all), condT=np.ascontiguousarray(condT), ckT=np.ascontiguousarray(ckT),
                 cv=np.ascontiguousarray(cvv), h0T=np.ascontiguousarray(h0T), w_dt=np.ascontiguousarray(w_dt),
                 pvec=pv, bvec=bv, consts=_consts(flip))
        maps.append(m)
    return maps


def _assemble(results):
    y_prompt = np.zeros((16, 256, D), np.float32)
    y_sample = np.zeros((4, 1024, D), np.float32)
    nck = np.zeros((16, 1, 256, NH, 2, 64), np.float32)
    ncv = np.zeros((16, 1, 256, NH, 128), np.float32)
    nsf = np.zeros((16, 1, 32, 64, 128), np.float32)
    nsb = np.zeros((16, 1, 32, 64, 128), np.float32)
    for core in range(8):
        r = results[core]
        j, flip = core // 2, (core % 2 == 1)
        y_prompt[2 * core:2 * core + 2] = r["y_p"].reshape(2, 256, D)
        if flip:
            y_sample[j, 512:1024] = r["y_s"][::-1]
        else:
            y_sample[j, 0:512] = r["y_s"]
        nck[2 * core:2 * core + 2, 0] = r["nk"].reshape(2, 256, NH, 2, 64)
        ncv[2 * core:2 * core + 2, 0] = r["nv"].reshape(2, 256, NH, 128)
        nsf[2 * core:2 * core + 2, 0] = r["sf"].reshape(2, 32, 64, 128)
        nsb[2 * core:2 * core + 2, 0] = r["sb"].reshape(2, 32, 64, 128)
    return (y_prompt, y_sample, nck, ncv, nsf, nsb)


_CACHE = {}


def kernel(**inputs):
    if "nc" not in _CACHE:
        _CACHE["nc"] = build_program()[0]
    nc = _CACHE["nc"]
    maps = _prep_inputs(inputs)
    res = run_bass_kernel_spmd(nc, maps, core_ids=list(range(8)))
    return _assemble(res.results)
```
